# Optimizing a Trainium2 kernel written in Bass

```python
import math
import jax, jax.numpy as jnp
from jax import lax
import numpy as np


D_MODEL = 1024
BATCH = 8
SEQ = 2048
DEPTH = 4

D_FF = 2816
DEEPNORM_ALPHA = (2.0 * DEPTH) ** 0.25
DEEPNORM_BETA = (8.0 * DEPTH) ** -0.25
LN_EPS = 1e-5
RMS_EPS = 1e-6
CONV_K = 4

GDN_QK_HEADS = 4
GDN_V_HEADS = 8
GDN_DK = 128
GDN_DV = 128
GDN_CHUNK = 64
GDN_QK_W = GDN_QK_HEADS * GDN_DK
GDN_V_W = GDN_V_HEADS * GDN_DV
GDN_CONV_DIM = 2 * GDN_QK_W + GDN_V_W

SSD_D_INNER = D_MODEL
SSD_HEADDIM = 64
SSD_HEADS = SSD_D_INNER // SSD_HEADDIM
SSD_GROUPS = 2
SSD_STATE = 128
SSD_CHUNK = 128
SSD_CONV_DIM = SSD_D_INNER + 2 * SSD_GROUPS * SSD_STATE

HYB_IN = GDN_CONV_DIM + GDN_V_W + 2 * GDN_V_HEADS + SSD_D_INNER + SSD_CONV_DIM + SSD_HEADS
HYB_OUT = GDN_V_W + SSD_D_INNER

NSA_HEADS = 16
NSA_GROUPS = 2
NSA_DK = 64
NSA_DV = 64
ROPE_DIM = NSA_DK // 4
ROPE_THETA = 500000.0
CMP_LEN = 32
CMP_STRIDE = 16
CMP_HIDDEN = 256
SEL_LEN = 64
SEL_TOPK = 8
SEL_LOCAL = 2
FORCE_SCORE = 1e4
WINDOW = 512
Q_BLOCK = 128
NSA_IN = NSA_HEADS * NSA_DK + 3 * NSA_GROUPS * (NSA_DK + NSA_DV) + 3 * NSA_HEADS

kernel_name = 'hybrid_gdn_ssd_nsa_macaron_deepnorm'


def layer_norm(x, g, b):
    xf = x.astype(jnp.float32)
    mu = jnp.mean(xf, -1, keepdims=True)
    var = jnp.mean(jnp.square(xf - mu), -1, keepdims=True)
    return ((xf - mu) * lax.rsqrt(var + LN_EPS) * g + b).astype(x.dtype)


def rms_normalize(x):
    xf = x.astype(jnp.float32)
    return xf * lax.rsqrt(jnp.mean(jnp.square(xf), -1, keepdims=True) + RMS_EPS)


def l2_normalize(x):
    xf = x.astype(jnp.float32)
    return xf * lax.rsqrt(jnp.sum(jnp.square(xf), -1, keepdims=True) + 1e-6)


def swiglu(x, w_in, w_out):
    gate, up = jnp.split(x @ w_in, 2, axis=-1)
    return (jax.nn.silu(gate) * up) @ w_out


def causal_conv(x, w):
    k, c = w.shape
    return lax.conv_general_dilated(x, w[:, None, :].astype(x.dtype), window_strides=(1,),
                                    padding=[(k - 1, 0)], dimension_numbers=('NWC', 'WIO', 'NWC'),
                                    feature_group_count=c)


def decay_matrix(cs):
    n = cs.shape[-1]
    mask = np.tril(np.ones((n, n), dtype=bool))
    return jnp.exp(jnp.where(mask, cs[..., :, None] - cs[..., None, :], -jnp.inf))


def masked_softmax(s, mask):
    s = jnp.where(mask, s.astype(jnp.float32), -jnp.inf)
    m = jnp.max(s, -1, keepdims=True)
    m = jnp.where(jnp.isfinite(m), m, 0.0)
    e = jnp.exp(s - m)
    return e / jnp.maximum(jnp.sum(e, -1, keepdims=True), 1e-30)


def partial_rope(x, positions):
    half = ROPE_DIM // 2
    inv_freq = jnp.asarray(ROPE_THETA ** (-np.arange(half) / half), jnp.float32)
    ang = positions.astype(jnp.float32)[..., None] * inv_freq
    cos = jnp.cos(ang)[:, :, None, :]
    sin = jnp.sin(ang)[:, :, None, :]
    xf = x.astype(jnp.float32)
    x1, x2, rest = xf[..., :half], xf[..., half:ROPE_DIM], xf[..., ROPE_DIM:]
    out = jnp.concatenate([x1 * cos - x2 * sin, x2 * cos + x1 * sin, rest], axis=-1)
    return out.astype(x.dtype)


def gated_delta_rule(q, k, v, g, beta):
    bsz, t_len, h, dk = q.shape
    dv = v.shape[-1]
    c = GDN_CHUNK
    n = t_len // c

    def to_chunks(a):
        return a.astype(jnp.float32).reshape(bsz, n, c, h, -1).transpose(0, 3, 1, 2, 4)

    q, k, v = to_chunks(q), to_chunks(k), to_chunks(v)
    g = g.astype(jnp.float32).reshape(bsz, n, c, h).transpose(0, 3, 1, 2)
    beta = beta.astype(jnp.float32).reshape(bsz, n, c, h).transpose(0, 3, 1, 2)
    g_cum = jnp.cumsum(g, axis=-1)
    decay = decay_matrix(g_cum)
    strict = np.tril(np.ones((c, c), dtype=bool), -1)
    k_beta = k * beta[..., None]
    a_low = jnp.where(strict, jnp.einsum('bhncd,bhnsd->bhncs', k_beta, k) * decay, 0.0)
    rhs = jnp.concatenate([v * beta[..., None], k_beta * jnp.exp(g_cum)[..., None]], axis=-1)
    sol = lax.linalg.triangular_solve(a_low, rhs, left_side=True, lower=True, unit_diagonal=True)
    u, w = sol[..., :dv], sol[..., dv:]
    attn = jnp.einsum('bhncd,bhnsd->bhncs', q, k) * decay
    q_dec = q * jnp.exp(g_cum)[..., None]
    k_dec = k * jnp.exp(g_cum[..., -1:] - g_cum)[..., None]
    chunk_dec = jnp.exp(g_cum[..., -1])

    def step(state, inp):
        u_c, w_c, q_c, k_c, a_c, d_c = inp
        v_new = u_c - jnp.einsum('bhck,bhkv->bhcv', w_c, state)
        o_c = jnp.einsum('bhck,bhkv->bhcv', q_c, state) + jnp.einsum('bhcs,bhsv->bhcv', a_c, v_new)
        state = state * d_c[..., None, None] + jnp.einsum('bhck,bhcv->bhkv', k_c, v_new)
        return state, o_c

    xs = tuple(jnp.moveaxis(a, 2, 0) for a in (u, w, q_dec, k_dec, attn, chunk_dec))
    s0 = jnp.zeros((bsz, h, dk, dv), jnp.float32)
    _, o = lax.scan(step, s0, xs)
    return o.transpose(1, 0, 3, 2, 4).reshape(bsz, t_len, h, dv)


def gdn_branch(qkv, z, b_logit, a_logit, conv_w, a_log, dt_bias, norm_w):
    bsz, t_len, _ = qkv.shape
    qkv = jax.nn.silu(causal_conv(qkv, conv_w))
    q, k, v = jnp.split(qkv, [GDN_QK_W, 2 * GDN_QK_W], axis=-1)
    rep = GDN_V_HEADS // GDN_QK_HEADS
    q = jnp.repeat(l2_normalize(q.reshape(bsz, t_len, GDN_QK_HEADS, GDN_DK)) * (GDN_DK ** -0.5), rep, axis=2)
    k = jnp.repeat(l2_normalize(k.reshape(bsz, t_len, GDN_QK_HEADS, GDN_DK)), rep, axis=2)
    v = v.reshape(bsz, t_len, GDN_V_HEADS, GDN_DV)
    beta = jax.nn.sigmoid(b_logit.astype(jnp.float32))
    g = -jnp.exp(a_log.astype(jnp.float32)) * jax.nn.softplus(a_logit.astype(jnp.float32) + dt_bias)
    o = gated_delta_rule(q, k, v, g, beta)
    zf = z.reshape(bsz, t_len, GDN_V_HEADS, GDN_DV).astype(jnp.float32)
    o = rms_normalize(o) * norm_w * jax.nn.silu(zf)
    return o.reshape(bsz, t_len, GDN_V_W).astype(qkv.dtype)


def ssd_scan(x, dt, a, bm, cm):
    bsz, t_len, h, p = x.shape
    g, n = bm.shape[2], bm.shape[3]
    e = h // g
    l = SSD_CHUNK
    nc = t_len // l
    x = x.astype(jnp.float32)
    dt = dt.astype(jnp.float32)
    xdt = (x * dt[..., None]).reshape(bsz, nc, l, g, e, p)
    adt = (dt * a).reshape(bsz, nc, l, g, e).transpose(0, 3, 4, 1, 2)
    bm = bm.astype(jnp.float32).reshape(bsz, nc, l, g, n)
    cm = cm.astype(jnp.float32).reshape(bsz, nc, l, g, n)
    a_cs = jnp.cumsum(adt, axis=-1)
    seg = decay_matrix(a_cs)
    cb = jnp.einsum('bclgn,bcsgn->bgcls', cm, bm)
    y_diag = jnp.einsum('bgcls,bgecls,bcsgep->bclgep', cb, seg, xdt)
    decay_states = jnp.exp(a_cs[..., -1:] - a_cs)
    states = jnp.einsum('bclgn,bgecl,bclgep->bcgepn', bm, decay_states, xdt)
    chunk_cs = jnp.cumsum(jnp.pad(a_cs[..., -1], ((0, 0), (0, 0), (0, 0), (1, 0))), axis=-1)
    decay_chunk = decay_matrix(chunk_cs)
    states = jnp.concatenate([jnp.zeros_like(states[:, :1]), states], axis=1)
    prev = jnp.einsum('bgezc,bcgepn->bzgepn', decay_chunk, states)[:, :-1]
    y_off = jnp.einsum('bclgn,bcgepn,bgecl->bclgep', cm, prev, jnp.exp(a_cs))
    return (y_diag + y_off).reshape(bsz, t_len, h, p)


def ssd_branch(z, xbc, dt_logit, conv_w, conv_b, a_log, dt_bias, d_skip, norm_w):
    bsz, t_len, _ = xbc.shape
    xbc = jax.nn.silu(causal_conv(xbc, conv_w) + conv_b)
    xs, bm, cm = jnp.split(xbc, [SSD_D_INNER, SSD_D_INNER + SSD_GROUPS * SSD_STATE], axis=-1)
    x = xs.reshape(bsz, t_len, SSD_HEADS, SSD_HEADDIM)
    bm = bm.reshape(bsz, t_len, SSD_GROUPS, SSD_STATE)
    cm = cm.reshape(bsz, t_len, SSD_GROUPS, SSD_STATE)
    dt = jax.nn.softplus(dt_logit.astype(jnp.float32) + dt_bias)
    a = -jnp.exp(a_log.astype(jnp.float32))
    y = ssd_scan(x, dt, a, bm, cm) + x.astype(jnp.float32) * d_skip[:, None]
    y = y.reshape(bsz, t_len, SSD_D_INNER) * jax.nn.silu(z.astype(jnp.float32))
    y = rms_normalize(y.reshape(bsz, t_len, SSD_GROUPS, -1)) * norm_w.reshape(SSD_GROUPS, -1)
    return y.reshape(bsz, t_len, SSD_D_INNER).astype(xbc.dtype)


def hybrid_mixer(x, w_in, gdn_conv_w, gdn_a_log, gdn_dt_bias, gdn_norm_w,
                 ssd_conv_w, ssd_conv_b, ssd_a_log, ssd_dt_bias, ssd_d, ssd_norm_w, w_out):
    proj = x @ w_in
    sizes = [GDN_CONV_DIM, GDN_V_W, GDN_V_HEADS, GDN_V_HEADS, SSD_D_INNER, SSD_CONV_DIM]
    gdn_qkv, gdn_z, gdn_b, gdn_a, ssd_z, ssd_xbc, ssd_dt = jnp.split(
        proj, [int(s) for s in np.cumsum(sizes)], axis=-1)
    o_a = gdn_branch(gdn_qkv, gdn_z, gdn_b, gdn_a, gdn_conv_w, gdn_a_log, gdn_dt_bias, gdn_norm_w)
    o_b = ssd_branch(ssd_z, ssd_xbc, ssd_dt, ssd_conv_w, ssd_conv_b, ssd_a_log, ssd_dt_bias, ssd_d, ssd_norm_w)
    return jnp.concatenate([o_a, o_b], axis=-1) @ w_out


def compress_blocks(kv, idx, pos, w1, w2):
    bsz, _, g, d = kv.shape
    blocks = kv[:, idx] + pos[:, None, :]
    flat = blocks.transpose(0, 1, 3, 2, 4).reshape(bsz, idx.shape[0], g, CMP_LEN * d)
    return jax.nn.silu(flat @ w1) @ w2


def nsa_mixer(x, positions, w_in, cmp_pos, cmp_w1, cmp_w2, w_out):
    bsz, t_len, _ = x.shape
    h, g, dk, dv = NSA_HEADS, NSA_GROUPS, NSA_DK, NSA_DV
    e = h // g
    proj = x @ w_in
    sizes = [h * dk, g * dk, g * dv, g * dk, g * dv, g * dk, g * dv]
    q, k_c, v_c, k_s, v_s, k_w, v_w, gates = jnp.split(proj, [int(s) for s in np.cumsum(sizes)], axis=-1)
    q = q.reshape(bsz, t_len, h, dk)
    k_c, k_s, k_w = (a.reshape(bsz, t_len, g, dk) for a in (k_c, k_s, k_w))
    v_c, v_s, v_w = (a.reshape(bsz, t_len, g, dv) for a in (v_c, v_s, v_w))
    q_rot = partial_rope(q, positions)
    k_s = partial_rope(k_s, positions)
    k_w = partial_rope(k_w, positions)
    gates = jax.nn.sigmoid(gates.astype(jnp.float32)).reshape(bsz, t_len, h, 3)

    n_cmp = (t_len - CMP_LEN) // CMP_STRIDE + 1
    cmp_idx = np.arange(n_cmp)[:, None] * CMP_STRIDE + np.arange(CMP_LEN)[None, :]
    k_cmp = compress_blocks(k_c, cmp_idx, cmp_pos[0], cmp_w1[0], cmp_w2[0])
    v_cmp = compress_blocks(v_c, cmp_idx, cmp_pos[1], cmp_w1[1], cmp_w2[1])
    cmp_end = jnp.asarray(cmp_idx[:, -1], jnp.int32)

    n_sel = t_len // SEL_LEN
    n_top = min(SEL_TOPK, n_sel)
    c0 = np.arange(n_cmp)[:, None] * CMP_STRIDE
    s0 = np.arange(n_sel)[None, :] * SEL_LEN
    agg = np.clip(np.minimum(c0 + CMP_LEN, s0 + SEL_LEN) - np.maximum(c0, s0), 0, None) / CMP_LEN
    agg = jnp.asarray(agg, jnp.float32)
    k_blocks = k_s.reshape(bsz, n_sel, SEL_LEN, g, dk).transpose(0, 3, 1, 2, 4)
    v_blocks = v_s.reshape(bsz, n_sel, SEL_LEN, g, dv).transpose(0, 3, 1, 2, 4)
    k_win = jnp.pad(k_w, ((0, 0), (WINDOW, 0), (0, 0), (0, 0)))
    v_win = jnp.pad(v_w, ((0, 0), (WINDOW, 0), (0, 0), (0, 0)))

    n_q = t_len // Q_BLOCK

    def to_blocks(a):
        return a.reshape(bsz, n_q, Q_BLOCK, *a.shape[2:]).swapaxes(0, 1)

    xs = (to_blocks(q), to_blocks(q_rot), to_blocks(gates), jnp.arange(n_q, dtype=jnp.int32) * Q_BLOCK)
    scale = dk ** -0.5
    b_ix = jnp.arange(bsz)[:, None, None]
    g_ix = jnp.arange(g)[None, :, None]
    sel_ids = jnp.arange(n_sel)

    def query_block(inp):
        qb, qrb, gb, q0 = inp
        t = q0 + jnp.arange(Q_BLOCK)
        qg = qb.reshape(bsz, Q_BLOCK, g, e, dk)
        qrg = qrb.reshape(bsz, Q_BLOCK, g, e, dk)
        s_cmp = jnp.einsum('bqged,bcgd->bgeqc', qg, k_cmp) * scale
        p_cmp = masked_softmax(s_cmp, cmp_end[None, :] <= t[:, None])
        o_cmp = jnp.einsum('bgeqc,bcgd->bqged', p_cmp, v_cmp)
        importance = jnp.einsum('bgeqc,cj->bgqj', p_cmp, agg)
        cur = t // SEL_LEN
        causal_blk = sel_ids[None, :] <= cur[:, None]
        forced = (sel_ids[None, :] == 0) | (causal_blk & (sel_ids[None, :] > cur[:, None] - SEL_LOCAL))
        score = jnp.where(forced, FORCE_SCORE, jnp.where(causal_blk, importance, -1.0))
        _, top = lax.top_k(score, n_top)
        flat = top.reshape(bsz, g, Q_BLOCK * n_top)
        k_sel = k_blocks[b_ix, g_ix, flat].reshape(bsz, g, Q_BLOCK, n_top * SEL_LEN, dk)
        v_sel = v_blocks[b_ix, g_ix, flat].reshape(bsz, g, Q_BLOCK, n_top * SEL_LEN, dv)
        key_pos = (top[..., None] * SEL_LEN + jnp.arange(SEL_LEN)).reshape(bsz, g, Q_BLOCK, n_top * SEL_LEN)
        s_sel = jnp.einsum('bqged,bgqmd->bgeqm', qrg, k_sel) * scale
        p_sel = masked_softmax(s_sel, (key_pos <= t[:, None])[:, :, None])
        o_sel = jnp.einsum('bgeqm,bgqmd->bqged', p_sel, v_sel)
        kw = lax.dynamic_slice_in_dim(k_win, q0, WINDOW + Q_BLOCK, axis=1)
        vw = lax.dynamic_slice_in_dim(v_win, q0, WINDOW + Q_BLOCK, axis=1)
        win_pos = q0 - WINDOW + jnp.arange(WINDOW + Q_BLOCK)
        win_mask = ((win_pos[None, :] <= t[:, None]) & (win_pos[None, :] > t[:, None] - WINDOW)
                    & (win_pos[None, :] >= 0))
        s_win = jnp.einsum('bqged,bkgd->bgeqk', qrg, kw) * scale
        p_win = masked_softmax(s_win, win_mask)
        o_win = jnp.einsum('bgeqk,bkgd->bqged', p_win, vw)
        gb = gb.reshape(bsz, Q_BLOCK, g, e, 3)
        o = gb[..., 0:1] * o_cmp + gb[..., 1:2] * o_sel + gb[..., 2:3] * o_win
        return o.reshape(bsz, Q_BLOCK, h * dv)

    o = lax.map(query_block, xs)
    o = o.swapaxes(0, 1).reshape(bsz, t_len, h * dv).astype(x.dtype)
    return o @ w_out


def setup_inputs(seed: int = 0) -> dict:
    key = jax.random.key(seed)
    keys = list(jax.random.split(key, 32))
    n_hyb = (DEPTH + 1) // 2
    n_nsa = DEPTH // 2

    def normal(i, shape, s):
        return jax.random.normal(keys[i], shape, jnp.float32) * s

    def dt_bias(i, shape):
        dt = jnp.exp(jax.random.uniform(keys[i], shape, jnp.float32, math.log(1e-3), math.log(1e-1)))
        return dt + jnp.log(-jnp.expm1(-dt))

    def log_rate(i, shape):
        return jnp.log(jax.random.uniform(keys[i], shape, jnp.float32, 1.0, 16.0))

    x = normal(0, (BATCH, SEQ, D_MODEL), 1.0)
    positions = (jnp.arange(SEQ, dtype=jnp.int32)[None, :]
                 + jax.random.randint(keys[1], (BATCH, 1), 0, 4096, dtype=jnp.int32))
    return {
        'x': x,
        'positions': positions,
        'ln_g': 1.0 + normal(2, (DEPTH, 3, D_MODEL), 0.05),
        'ln_b': normal(3, (DEPTH, 3, D_MODEL), 0.02),
        'ffn_w_in': normal(4, (DEPTH, 2, D_MODEL, 2 * D_FF), D_MODEL ** -0.5),
        'ffn_w_out': normal(5, (DEPTH, 2, D_FF, D_MODEL), DEEPNORM_BETA * D_FF ** -0.5),
        'hyb_w_in': normal(6, (n_hyb, D_MODEL, HYB_IN), D_MODEL ** -0.5),
        'gdn_conv_w': normal(7, (n_hyb, CONV_K, GDN_CONV_DIM), CONV_K ** -0.5),
        'gdn_a_log': log_rate(8, (n_hyb, GDN_V_HEADS)),
        'gdn_dt_bias': dt_bias(9, (n_hyb, GDN_V_HEADS)),
        'gdn_norm_w': 1.0 + normal(10, (n_hyb, GDN_DV), 0.05),
        'ssd_conv_w': normal(11, (n_hyb, CONV_K, SSD_CONV_DIM), CONV_K ** -0.5),
        'ssd_conv_b': normal(12, (n_hyb, SSD_CONV_DIM), 0.02),
        'ssd_a_log': log_rate(13, (n_hyb, SSD_HEADS)),
        'ssd_dt_bias': dt_bias(14, (n_hyb, SSD_HEADS)),
        'ssd_d': 1.0 + normal(15, (n_hyb, SSD_HEADS), 0.05),
        'ssd_norm_w': 1.0 + normal(16, (n_hyb, SSD_D_INNER), 0.05),
        'hyb_w_out': normal(17, (n_hyb, HYB_OUT, D_MODEL), DEEPNORM_BETA * HYB_OUT ** -0.5),
        'nsa_w_in': normal(18, (n_nsa, D_MODEL, NSA_IN), D_MODEL ** -0.5),
        'nsa_cmp_pos': normal(19, (n_nsa, 2, CMP_LEN, NSA_DK), 0.1),
        'nsa_cmp_w1': normal(20, (n_nsa, 2, CMP_LEN * NSA_DK, CMP_HIDDEN), (CMP_LEN * NSA_DK) ** -0.5),
        'nsa_cmp_w2': normal(21, (n_nsa, 2, CMP_HIDDEN, NSA_DK), CMP_HIDDEN ** -0.5),
        'nsa_w_out': normal(22, (n_nsa, NSA_HEADS * NSA_DV, D_MODEL), DEEPNORM_BETA * (NSA_HEADS * NSA_DV) ** -0.5),
    }


def reference(x, positions, ln_g, ln_b, ffn_w_in, ffn_w_out, hyb_w_in, gdn_conv_w, gdn_a_log,
              gdn_dt_bias, gdn_norm_w, ssd_conv_w, ssd_conv_b, ssd_a_log, ssd_dt_bias, ssd_d,
              ssd_norm_w, hyb_w_out, nsa_w_in, nsa_cmp_pos, nsa_cmp_w1, nsa_cmp_w2, nsa_w_out):
    h = x
    for layer in range(DEPTH):
        i = layer // 2
        h = layer_norm(DEEPNORM_ALPHA * h + 0.5 * swiglu(h, ffn_w_in[layer, 0], ffn_w_out[layer, 0]),
                       ln_g[layer, 0], ln_b[layer, 0])
        if layer % 2 == 0:
            mix = hybrid_mixer(h, hyb_w_in[i], gdn_conv_w[i], gdn_a_log[i], gdn_dt_bias[i], gdn_norm_w[i],
                               ssd_conv_w[i], ssd_conv_b[i], ssd_a_log[i], ssd_dt_bias[i], ssd_d[i],
                               ssd_norm_w[i], hyb_w_out[i])
        else:
            mix = nsa_mixer(h, positions, nsa_w_in[i], nsa_cmp_pos[i], nsa_cmp_w1[i], nsa_cmp_w2[i], nsa_w_out[i])
        h = layer_norm(DEEPNORM_ALPHA * h + mix, ln_g[layer, 1], ln_b[layer, 1])
        h = layer_norm(DEEPNORM_ALPHA * h + 0.5 * swiglu(h, ffn_w_in[layer, 1], ffn_w_out[layer, 1]),
                       ln_g[layer, 2], ln_b[layer, 2])
    return h
```

```python
import numpy as np
from contextlib import ExitStack
import concourse.bass as bass
import concourse.mybir as mybir
from concourse.bass_utils import run_bass_kernel_spmd

F32 = mybir.dt.float32
BF16 = mybir.dt.bfloat16
I32 = mybir.dt.int32
AF = mybir.ActivationFunctionType
ALU = mybir.AluOpType
AX = mybir.AxisListType

D = 1024
T = 2048
DEPTH = 4
DFF = 2816
ALPHA = (2.0 * DEPTH) ** 0.25
LN_EPS = 1e-5
NKC = D // 128
NFC = DFF // 128
TT = 512

SAME_ENGINE_SYNC = True


class Ctx:
    def __init__(self, nc, es, n_dma_sems=32):
        self.nc = nc
        self.eng = {'pe': nc.tensor, 'act': nc.scalar, 'dve': nc.vector, 'pool': nc.gpsimd, 'sp': nc.sync}
        self.sem = {}
        self.cnt = {}
        for e in ['pe', 'act', 'dve', 'pool']:
            self.sem[e] = es.enter_context(nc.semaphore("s_" + e))
            self.cnt[e] = 0
        self.dma_sems = [es.enter_context(nc.semaphore("s_dma%d" % i)) for i in range(n_dma_sems)]
        self.dma_val = [0] * n_dma_sems
        self.dma_rr = 0
        self.waited = {}
        self.last_w = {}
        self.readers = {}
        self.ninstr = 0
        self.alias = {}

    def _wait(self, e, tok):
        sem, key, val = tok
        k = (e, key)
        if self.waited.get(k, 0) >= val:
            return
        self.eng[e].wait_ge(sem, val)
        self.waited[k] = val

    def _deps(self, e, reads, writes):
        toks = {}

        def add(tok):
            if tok is None:
                return
            sem, key, val = tok
            if key == e and (e == 'pe' or not SAME_ENGINE_SYNC):
                return
            if key not in toks or toks[key][2] < val:
                toks[key] = tok
        for r in reads:
            add(self.last_w.get(r))
        for w in writes:
            add(self.last_w.get(w))
            for tok in self.readers.get(w, {}).values():
                add(tok)
        for tok in toks.values():
            self._wait(e, tok)

    def _record(self, tok, reads, writes):
        for r in reads:
            self.readers.setdefault(r, {})[tok[1]] = tok
        for w in writes:
            self.last_w[w] = tok
            self.readers[w] = {}

    def op(self, e, fn, reads=(), writes=(), inc=True):
        reads = [self.alias.get(k, k) for k in reads]
        writes = [self.alias.get(k, k) for k in writes]
        for k in reads:
            kk = k[0] if isinstance(k, tuple) else k
            if isinstance(kk, str) and kk.startswith('ps') and k not in writes:
                writes.append(k)
        reads = [k for k in reads if k not in writes]
        self._deps(e, reads, writes)
        ins = fn()
        self.ninstr += 1
        tok = (self.sem[e], e, self.cnt[e] + 1)
        self._record(tok, reads, writes)
        if inc:
            ins.then_inc(self.sem[e], 1)
            self.cnt[e] += 1
        return ins

    def dma(self, q, out, in_, reads=(), writes=(), **kw):
        reads = list(reads)
        writes = list(writes)
        i = self.dma_rr
        self.dma_rr = (self.dma_rr + 1) % len(self.dma_sems)
        sem = self.dma_sems[i]
        key = "dma%d" % i
        if self.dma_val[i] > 0:
            self._wait(q, (sem, key, self.dma_val[i]))
        self._deps(q, reads, writes)
        ins = self.eng[q].dma_start(out=out, in_=in_, **kw)
        self.dma_val[i] += 16
        ins.then_inc(sem, 16)
        tok = (sem, key, self.dma_val[i])
        self._record(tok, reads, writes)
        self.ninstr += 1
        return tok

    def barrier(self):
        for e in ['pe', 'act', 'dve', 'pool', 'sp']:
            for e2 in ['pe', 'act', 'dve', 'pool']:
                if e2 != e and self.cnt[e2] > 0:
                    self._wait(e, (self.sem[e2], e2, self.cnt[e2]))
            for i, sem in enumerate(self.dma_sems):
                if self.dma_val[i] > 0:
                    self._wait(e, (sem, "dma%d" % i, self.dma_val[i]))

    def wait_all(self, e, keys):
        for k in keys:
            tok = self.last_w.get(k)
            if tok is not None:
                self._wait(e, tok)


class Prog:
    def __init__(self, stages, dbg=()):
        self.stages = stages
        self.dbg = dbg
        nc = self.nc = bass.Bass("TRN2", target_bir_lowering=False)
        self.es = ExitStack()
        es = self.es
        self.c = Ctx(nc, es)
        dt = lambda name, shape, dtype=F32, kind="ExternalInput": nc.dram_tensor(name, shape, dtype, kind=kind).ap()
        self.xT = dt("xT", [D, T])
        self.ffn_w_in = dt("ffn_w_in", [DEPTH, 2, D, 2 * DFF])
        self.ffn_w_out = dt("ffn_w_out", [DEPTH, 2, DFF, D])
        self.ln_gT = dt("ln_gT", [128, DEPTH * 3 * NKC])
        self.ln_bT = dt("ln_bT", [128, DEPTH * 3 * NKC])
        self.outT = dt("outT", [D, T], kind="ExternalOutput")
        self.hyb_w_in = dt("hyb_w_in", [2, D, 5664])
        self.hyb_w_out = dt("hyb_w_out", [2, 2048, D])
        self.consts_d = dt("consts", [128, 8, 128])
        self.gdn_con_d = dt("gdn_con", [128, 2, 16])
        self.gdn_cw_d = dt("gdn_cw", [128, 2, 16, 4])
        self.gdn_nw_d = dt("gdn_nw", [128, 2, 128])
        self.ssd_con_d = dt("ssd_con", [128, 2, 48])
        self.ssd_cw_d = dt("ssd_cw", [128, 2, 12, 5])
        self.ssd_nw_d = dt("ssd_nw", [128, 2, 8])
        self.nsa_w_in = dt("nsa_w_in", [2, D, 1840])
        self.nsa_w_sw = dt("nsa_w_sw", [2, D, 1280])
        self.nsa_w_out = dt("nsa_w_out", [2, D, D])
        self.nsa_cmp_w1 = dt("nsa_cmp_w1", [2, 2, 2048, 256])
        self.nsa_cmp_w2 = dt("nsa_cmp_w2", [2, 2, 256, 64])
        self.nsa_pos2 = dt("nsa_pos2", [128, 2, 2, 16])
        self.pos_rep = dt("pos_rep", [128, T], I32)
        self.rope_c_d = dt("rope_c", [128, 2])
        self.nsa_Mc = dt("nsa_Mc", [128, T])
        self.nsa_Cm = dt("nsa_Cm", [128, 4, TT])
        self.nsa_Wm = dt("nsa_Wm", [128, 4, TT])
        self.nsa_E = dt("nsa_E", [32, 16, 128])
        self.nsa_agg = dt("nsa_agg", [128, 33])
        self.nsa_M12 = dt("nsa_M12", [128, 2, 512])
        sb = lambda name, shape, dtype=F32: es.enter_context(nc.sbuf_tensor(name, shape, dtype))
        self.sb = sb
        self.h32 = sb("h32", [128, NKC, T])
        self.h16 = sb("h16", [128, NKC, T], BF16)
        self.lng = sb("lng", [128, DEPTH * 3 * NKC])
        self.lnb = sb("lnb", [128, DEPTH * 3 * NKC])
        self.ones32 = sb("ones32", [128, 128])
        self.ones16 = sb("ones16", [128, 128], BF16)
        self.epsc = sb("epsc", [128, 1])
        self.onec = sb("onec", [128, 1])
        self.c1e6 = sb("c1e6", [128, 1])
        self.consts = sb("consts_sb", [128, 8, 128])
        self.ident = self.consts[:, 0, :]
        self.maskL = self.consts[:, 1, :]
        self.maskU = self.consts[:, 2, :]
        self.offdiag = self.consts[:, 3, :]
        self.tri64 = self.consts[:, 4, :]
        self.blk64 = self.consts[:, 5, :]
        self.tri128 = self.consts[:, 6, :]
        self.maskU128 = self.consts[:, 7, :]
        self.gdn_con = sb("gdn_con_sb", [128, 2, 16])
        self.gdn_cw = sb("gdn_cw_sb", [128, 2, 16, 4])
        self.gdn_nw = sb("gdn_nw_sb", [128, 2, 128])
        self.ssd_con = sb("ssd_con_sb", [128, 2, 48])
        self.ssd_cw = sb("ssd_cw_sb", [128, 2, 12, 5])
        self.ssd_nw = sb("ssd_nw_sb", [128, 2, 8])
        self.rope_c = sb("rope_c_sb", [128, 2])

    def kh32(self, c, tt):
        return ("h32", c, tt)

    def kh16(self, c, tt):
        return ("h16", c, tt)

    def prologue(self):
        nc, c = self.nc, self.c
        c.op('pool', lambda: nc.gpsimd.memset(self.ones32[:], 1.0), writes=['ones32'])
        c.op('pool', lambda: nc.gpsimd.memset(self.ones16[:], 1.0), writes=['ones16'])
        c.op('pool', lambda: nc.gpsimd.memset(self.epsc[:], LN_EPS / (ALPHA * ALPHA)), writes=['epsc'])
        c.op('pool', lambda: nc.gpsimd.memset(self.onec[:], 1.0), writes=['onec'])
        c.op('pool', lambda: nc.gpsimd.memset(self.c1e6[:], 1e-6), writes=['c1e6'])
        c.dma('sp', self.consts[:], self.consts_d[:, :, :], writes=['ident', 'maskL', 'maskU', 'offdiag', 'tri64', 'blk64', 'tri128', 'maskU128'])
        c.dma('sp', self.gdn_con[:], self.gdn_con_d[:, :, :], writes=['gdn_con'])
        c.dma('sp', self.gdn_cw[:], self.gdn_cw_d[:, :, :, :], writes=['gdn_cw'])
        c.dma('sp', self.gdn_nw[:], self.gdn_nw_d[:, :, :], writes=['gdn_nw'])
        c.dma('sp', self.ssd_con[:], self.ssd_con_d[:, :, :], writes=['ssd_con'])
        c.dma('sp', self.ssd_cw[:], self.ssd_cw_d[:, :, :, :], writes=['ssd_cw'])
        c.dma('sp', self.ssd_nw[:], self.ssd_nw_d[:, :, :], writes=['ssd_nw'])
        c.dma('sp', self.rope_c[:], self.rope_c_d[:, :], writes=['rope_c'])
        c.dma('sp', self.lng[:], self.ln_gT[:, :], writes=['lng'])
        c.dma('sp', self.lnb[:], self.ln_bT[:, :], writes=['lnb'])
        xv = self.xT.rearrange("(c p) t -> p c t", p=128)
        for kc in range(NKC):
            c.dma('sp' if kc % 2 == 0 else 'act', self.h32[:, kc, :], xv[:, kc, :],
                  writes=[self.kh32(kc, tt) for tt in range(T // TT)])
        for kc in range(NKC):
            for tt in range(T // TT):
                e = 'dve' if (kc + tt) % 2 == 0 else 'pool'
                eng = nc.vector if e == 'dve' else nc.gpsimd
                c.op(e, lambda: eng.tensor_copy(self.h16[:, kc, tt * TT:(tt + 1) * TT], self.h32[:, kc, tt * TT:(tt + 1) * TT]),
                     reads=[self.kh32(kc, tt)], writes=[self.kh16(kc, tt)])

    def epilogue(self):
        nc, c = self.nc, self.c
        ov = self.outT.rearrange("(c p) t -> p c t", p=128)
        keys = []
        for kc in range(NKC):
            k = ("out", kc)
            c.dma('sp', ov[:, kc, :], self.h32[:, kc, :], reads=[self.kh32(kc, tt) for tt in range(T // TT)], writes=[k])
            keys.append(k)
        c.wait_all('sp', keys)

    def layer_norm_tile(self, li, tt, ps_sum, ps_sq, scr):
        nc, c = self.nc, self.c
        ts = slice(tt * TT, (tt + 1) * TT)
        zsq, mean, rstd, tmp = scr['zsq'], scr['mean'], scr['rstd'], scr['tmp']
        eps = LN_EPS / (ALPHA * ALPHA)
        for kc in range(NKC):
            b = kc % 2
            c.op('act', lambda: nc.scalar.activation(zsq[b][:], self.h32[:, kc, ts], AF.Square),
                 reads=[self.kh32(kc, tt)], writes=[('zsq', b)])
            c.op('pe', lambda: nc.tensor.matmul(ps_sum[:], self.ones32[:], self.h32[:, kc, ts], start=(kc == 0), stop=(kc == NKC - 1)),
                 reads=['ones32', self.kh32(kc, tt)], writes=['ps_sum'], inc=False)
            c.op('pe', lambda: nc.tensor.matmul(ps_sq[:], self.ones16[:], zsq[b][:], start=(kc == 0), stop=(kc == NKC - 1)),
                 reads=['ones16', ('zsq', b)], writes=['ps_sq'], inc=True)
        c.op('act', lambda: nc.scalar.activation(mean[:], ps_sum[:], AF.Copy, scale=1.0 / D), reads=['ps_sum'], writes=['mean'])
        c.op('pool', lambda: nc.gpsimd.tensor_tensor(tmp[:], mean[:], mean[:], ALU.mult), reads=['mean'], writes=[('zc', 0)])
        c.op('dve', lambda: nc.vector.scalar_tensor_tensor(rstd[:], ps_sq[:], 1.0 / D, tmp[:], ALU.mult, ALU.subtract),
             reads=['ps_sq', ('zc', 0)], writes=['rstd'])
        c.op('act', lambda: nc.scalar.activation(rstd[:], rstd[:], AF.Sqrt, bias=self.epsc[:, 0:1], scale=1.0), reads=['rstd', 'epsc'], writes=['rstd'])
        c.op('dve', lambda: nc.vector.reciprocal(rstd[:], rstd[:]), reads=['rstd'], writes=['rstd'])
        for kc in range(NKC):
            gi = li * NKC + kc
            b = kc % 2
            zc = scr['zc'][b]
            c.op('dve', lambda: nc.vector.tensor_tensor(zc[:], self.h32[:, kc, ts], mean[:], ALU.subtract),
                 reads=[self.kh32(kc, tt), 'mean'], writes=[('zc', b)])
            c.op('dve', lambda: nc.vector.scalar_tensor_tensor(zc[:], zc[:], self.lng[:, gi:gi + 1], rstd[:], ALU.mult, ALU.mult),
                 reads=[('zc', b), 'rstd', 'lng'], writes=[('zc', b)])
            c.op('act', lambda: nc.scalar.activation(self.h32[:, kc, ts], zc[:], AF.Identity, bias=self.lnb[:, gi:gi + 1], scale=1.0),
                 reads=[('zc', b), 'lnb'], writes=[self.kh32(kc, tt)])
            c.op('pool', lambda: nc.gpsimd.tensor_scalar(self.h16[:, kc, ts], zc[:], 1.0, self.lnb[:, gi:gi + 1], ALU.mult, ALU.add),
                 reads=[('zc', b), 'lnb'], writes=[self.kh16(kc, tt)])

    def ffn(self, layer, which, li):
        nc, c = self.nc, self.c
        cres = 0.5 / ALPHA
        with ExitStack() as es:
            self._uid = getattr(self, '_uid', 0) + 1
            _u = "_%d" % self._uid
            sb = lambda name, shape, dtype=F32: es.enter_context(nc.sbuf_tensor(name + _u, shape, dtype))
            ps = lambda name, shape, dtype=F32: es.enter_context(nc.psum_tensor(name + _u, shape, dtype))
            NH = 2
            HT = T // NH
            act = sb("ffn_act", [128, NFC, HT], BF16)
            NWB = 3
            wi = [sb("ffn_wi%d" % i, [128, NKC, 512], BF16) for i in range(NWB)]
            NOB = 2
            wo = [sb("ffn_wo%d" % i, [128, NFC, 256], BF16) for i in range(NOB)]
            sg = [sb("ffn_sg%d" % i, [128, TT], BF16) for i in range(2)]
            scr = {
                'zsq': [sb("ln_zsq%d" % i, [128, TT], BF16) for i in range(2)],
                'zc': [sb("ln_zc%d" % i, [128, TT]) for i in range(2)],
                'mean': sb("ln_mean", [128, TT]), 'rstd': sb("ln_rstd", [128, TT]),
            }
            scr['tmp'] = scr['zc'][0]
            pg = [ps("ps_g%d" % i, [128, TT]) for i in range(2)]
            pu = [ps("ps_u%d" % i, [128, TT]) for i in range(2)]
            po = [ps("ps_o%d" % i, [128, TT]) for i in range(2)]
            ps_sum = ps("ps_sum", [128, TT])
            ps_sq = ps("ps_sq", [128, TT])
            w_in = self.ffn_w_in[layer, which].rearrange("(kc p) n -> p kc n", p=128)
            w_out = self.ffn_w_out[layer, which].rearrange("(fc p) n -> p fc n", p=128)
            NJB = NFC // 2
            cnt = 0
            for half in range(NH):
                for jb in range(NJB):
                    s = (half * NJB + jb) % NWB
                    kw = ('ffn_wi', s)
                    c.dma('pool', wi[s][:, :, 0:256], w_in[:, :, jb * 256:(jb + 1) * 256], writes=[kw])
                    c.dma('pool', wi[s][:, :, 256:512], w_in[:, :, DFF + jb * 256:DFF + (jb + 1) * 256], writes=[kw])
                    for jj in range(2):
                        j = jb * 2 + jj
                        for t2 in range(HT // TT):
                            tt = half * (HT // TT) + t2
                            ts = slice(tt * TT, (tt + 1) * TT)
                            b = cnt % 2
                            cnt += 1
                            for kc in range(NKC):
                                c.op('pe', lambda: nc.tensor.matmul(pg[b][:], wi[s][:, kc, jj * 128:(jj + 1) * 128], self.h16[:, kc, ts],
                                                                    start=(kc == 0), stop=(kc == NKC - 1)),
                                     reads=[kw, self.kh16(kc, tt)], writes=[('pg', b)], inc=False)
                            for kc in range(NKC):
                                c.op('pe', lambda: nc.tensor.matmul(pu[b][:], wi[s][:, kc, 256 + jj * 128:256 + (jj + 1) * 128], self.h16[:, kc, ts],
                                                                    start=(kc == 0), stop=(kc == NKC - 1)),
                                     reads=[kw, self.kh16(kc, tt)], writes=[('pu', b)], inc=(kc == NKC - 1))
                            c.op('act', lambda: nc.scalar.activation(sg[b][:], pg[b][:], AF.Silu), reads=[('pg', b)], writes=[('sg', b)])
                            c.op('dve', lambda: nc.vector.tensor_tensor(act[:, j, t2 * TT:(t2 + 1) * TT], sg[b][:], pu[b][:], ALU.mult),
                                 reads=[('sg', b), ('pu', b)], writes=[('act', j, t2)])
                for db in range(4):
                    s = (half * 4 + db) % NOB
                    kw = ('ffn_wo', s)
                    c.dma('pool', wo[s][:], w_out[:, :, db * 256:(db + 1) * 256], writes=[kw])
                    for dd in range(2):
                        dc = db * 2 + dd
                        for t2 in range(HT // TT):
                            tt = half * (HT // TT) + t2
                            ts = slice(tt * TT, (tt + 1) * TT)
                            b = cnt % 2
                            cnt += 1
                            for fc in range(NFC):
                                c.op('pe', lambda: nc.tensor.matmul(po[b][:], wo[s][:, fc, dd * 128:(dd + 1) * 128], act[:, fc, t2 * TT:(t2 + 1) * TT],
                                                                    start=(fc == 0), stop=(fc == NFC - 1)),
                                     reads=[kw, ('act', fc, t2)], writes=[('po', b)], inc=(fc == NFC - 1))
                            c.op('dve', lambda: nc.vector.scalar_tensor_tensor(self.h32[:, dc, ts], po[b][:], cres, self.h32[:, dc, ts], ALU.mult, ALU.add),
                                 reads=[('po', b), self.kh32(dc, tt)], writes=[self.kh32(dc, tt)])
                for t2 in range(HT // TT):
                    tt = half * (HT // TT) + t2
                    self.layer_norm_tile(li, tt, ps_sum, ps_sq, scr)
            c.barrier()

    def mm(self, out, lhsT, rhs, r, w, start=True, stop=True, inc=True):
        nc = self.nc
        return self.c.op('pe', lambda: nc.tensor.matmul(out, lhsT, rhs, start=start, stop=stop), reads=r, writes=w, inc=inc)

    def tr(self, out, in_, r, w, inc=True):
        nc = self.nc
        return self.c.op('pe', lambda: nc.tensor.transpose(out, in_, self.ident[:]), reads=list(r) + ['ident'], writes=w, inc=inc)

    def V(self, fn, r, w):
        return self.c.op('dve', fn, reads=r, writes=w)

    def A(self, fn, r, w):
        return self.c.op('act', fn, reads=r, writes=w)

    def P(self, fn, r, w):
        return self.c.op('pool', fn, reads=r, writes=w)

    def load_w(self, dst, src, key, q='pool'):
        self.c.dma(q, dst, src, writes=[key])

    def out_proj_acc(self, wout_rows, oT, okeys, nchunks, ps_proj, scale, wbuf, wkey):
        nc, c = self.nc, self.c
        wv = wout_rows.rearrange("(cc p) n -> p cc n", p=128)
        if 'nodma' in self.dbg:
            self.P(lambda: nc.gpsimd.memset(wbuf[:, 0:nchunks, :], 0.0), [], [wkey])
        else:
            c.dma('pool', wbuf[:, 0:nchunks, :], wv, writes=[wkey])
        i = 0
        for dc in range(NKC):
            for tt in range(T // TT):
                ts = slice(tt * TT, (tt + 1) * TT)
                pp = ps_proj[i % 2]
                pk = ('ps_proj', i % 2)
                i += 1
                for cc in range(nchunks):
                    self.mm(pp[:], wbuf[:, cc, dc * 128:(dc + 1) * 128], oT[:, cc, ts], [wkey] + okeys, [pk],
                            start=(cc == 0), stop=(cc == nchunks - 1), inc=(cc == nchunks - 1))
                self.V(lambda: nc.vector.scalar_tensor_tensor(self.h32[:, dc, ts], pp[:], scale, self.h32[:, dc, ts], ALU.mult, ALU.add),
                       [pk, self.kh32(dc, tt)], [self.kh32(dc, tt)])

    def proj_fm(self, dst, wbuf, wkey, ps_proj, dkey, col0=0):
        nc = self.nc
        for tt in range(T // TT):
            ts = slice(tt * TT, (tt + 1) * TT)
            pp = ps_proj[tt % 2]
            pk = ('ps_proj', tt % 2)
            for kc in range(NKC):
                self.mm(pp[:], wbuf[:, kc, :], self.h16[:, kc, ts], [wkey, self.kh16(kc, tt)], [pk],
                        start=(kc == 0), stop=(kc == NKC - 1), inc=(kc == NKC - 1))
            self.A(lambda: nc.scalar.copy(dst[:, col0 + tt * TT:col0 + (tt + 1) * TT], pp[:]), [pk], [dkey])

    def proj_conv(self, dst, dkey, wbuf, wkey, cw4, bias, cwkey):
        nc = self.nc
        cv = self._cv
        xp16, dgw, ps_proj = cv['xp16'], cv['dgw'], cv['ps_proj']
        self.V(lambda: nc.vector.tensor_tensor(dgw[:], self.ident.unsqueeze(1).to_broadcast([128, 4, 128]),
                                               cw4.unsqueeze(2).to_broadcast([128, 4, 128]), ALU.mult), ['ident', cwkey], ['dgw'])
        for tt in range(T // TT):
            ts = slice(tt * TT, (tt + 1) * TT)
            pp = ps_proj[tt % 2]; pk = ('ps_proj', tt % 2)
            for kc in range(NKC):
                self.mm(pp[:], wbuf[:, kc, :], self.h16[:, kc, ts], [wkey, self.kh16(kc, tt)], [pk],
                        start=(kc == 0), stop=(kc == NKC - 1), inc=(kc == NKC - 1))
            self.A(lambda: nc.scalar.copy(xp16[:, 3 + tt * TT:3 + (tt + 1) * TT], pp[:]), [pk], [('xp', tt)])
            pc, pck = cv['psc'][tt % 2]
            rd = [('xp', tt)] + ([('xp', tt - 1)] if tt > 0 else [])
            for j in range(4):
                self.mm(pc[:], dgw[:, j, :], xp16[:, tt * TT + j:tt * TT + j + TT], ['dgw'] + rd, [pck], start=(j == 0), stop=(j == 3), inc=(j == 3))
            if bias is None:
                self.A(lambda: nc.scalar.activation(dst[:, ts], pc[:], AF.Silu), [pck], [dkey])
            else:
                self.A(lambda: nc.scalar.activation(dst[:, ts], pc[:], AF.Silu, bias=bias, scale=1.0), [pck, cwkey], [dkey])

    def conv_silu(self, dst, xp, cw, acc, r, w, bias=None):
        nc = self.nc
        self.V(lambda: nc.vector.tensor_scalar(acc[:], xp[:, 3:3 + T], cw[:, 3:4], None, ALU.mult), r, ['convacc'])
        for j in (2, 1, 0):
            self.V(lambda: nc.vector.scalar_tensor_tensor(acc[:], xp[:, j:j + T], cw[:, j:j + 1], acc[:], ALU.mult, ALU.add), r + ['convacc'], ['convacc'])
        if bias is None:
            self.A(lambda: nc.scalar.activation(dst, acc[:], AF.Silu), ['convacc'], w)
        else:
            self.A(lambda: nc.scalar.activation(dst, acc[:], AF.Silu, bias=bias, scale=1.0), ['convacc'] + r, w)

    def hybrid(self, layer):
        nc, c = self.nc, self.c
        i = layer // 2
        W = self.hyb_w_in[i]
        Wv = W.rearrange("(kc p) n -> p kc n", p=128)
        cres = 1.0 / ALPHA
        NS = T // 128
        with ExitStack() as es:
            self._uid = getattr(self, '_uid', 0) + 1
            _u = "_%d" % self._uid
            sb = lambda name, shape, dtype=F32: es.enter_context(nc.sbuf_tensor(name + _u, shape, dtype))
            ps = lambda name, shape, dtype=F32: es.enter_context(nc.psum_tensor(name + _u, shape, dtype))
            banks = [ps("psb%d" % k, [128, 4, 128]) for k in range(8)]
            flat = lambda t_: t_[:].rearrange("p a b -> p (a b)")
            ps_proj = [flat(banks[0]), flat(banks[1])]
            ps_misc = flat(banks[2])
            psq = banks[3:8]
            c.alias = {('ps_proj', 0): ('psb', 0), ('ps_proj', 1): ('psb', 1), 'ps_misc': ('psb', 2)}
            for k in range(5):
                c.alias[('psq', k)] = ('psb', 3 + k)
            wb = [sb("hw%d" % k, [128, NKC, 128], BF16) for k in range(4)]
            xp = sb("xp", [128, 3 + T])
            acc = sb("convacc", [128, 1024])
            self.P(lambda: nc.gpsimd.memset(xp[:, 0:3], 0.0), [], [('xp', 0)])
            dgw = sb("dgw", [128, 4, 128], BF16)
            self._cv = {'xp16': xp[:].bitcast(BF16), 'dgw': dgw, 'ps_proj': ps_proj,
                        'psc': [(flat(banks[3]), ('psq', 0)), (flat(banks[4]), ('psq', 1))]}
            wba = sb("wba", [128, NKC, 16], BF16)
            self.load_w(wba[:], Wv[:, :, 3072:3088], 'wba')
            ba = sb("ba_tok", [128, NS, 16])
            pm = ps_misc[:, 0:NS * 16].rearrange("p (s n) -> p s n", n=16)
            for s in range(NS):
                for kc in range(NKC):
                    self.mm(pm[:, s, :], self.h16[:, kc, s * 128:(s + 1) * 128], wba[:, kc, :], ['wba', self.kh16(kc, s // 4)], ['ps_misc'],
                            start=(kc == 0), stop=(kc == NKC - 1), inc=(kc == NKC - 1))
            self.A(lambda: nc.scalar.copy(ba[:], pm), ['ps_misc'], ['ba'])
            beta = sb("beta", [128, NS, 8])
            gg = sb("g_tok", [128, NS, 8])
            gc = sb("gc_tok", [128, NS, 8])
            ebg = sb("ebg", [128, NS, 8])
            ekd = sb("ekd", [128, NS, 8])
            ngc = sb("ngc", [128, NS, 8])
            tmpa = sb("tmpa", [128, NS, 8])
            gcon = self.gdn_con[:, i, :]
            alog = gcon[:, 0:8].unsqueeze(1).to_broadcast([128, NS, 8])
            dtb = gcon[:, 8:16].unsqueeze(1).to_broadcast([128, NS, 8])
            self.A(lambda: nc.scalar.activation(beta[:], ba[:, :, 0:8], AF.Exp, scale=-1.0), ['ba'], ['beta'])
            self.V(lambda: nc.vector.tensor_scalar(beta[:], beta[:], 1.0, None, ALU.add), ['beta'], ['beta'])
            self.V(lambda: nc.vector.reciprocal(beta[:], beta[:]), ['beta'], ['beta'])
            self.V(lambda: nc.vector.tensor_tensor(gg[:], ba[:, :, 8:16], dtb, ALU.add), ['ba', 'gdn_con'], ['gg'])
            self.A(lambda: nc.scalar.activation(gg[:], gg[:], AF.Exp), ['gg'], ['gg'])
            self.A(lambda: nc.scalar.activation(gg[:], gg[:], AF.Ln, bias=self.onec[:, 0:1], scale=1.0), ['gg', 'onec'], ['gg'])
            self.A(lambda: nc.scalar.activation(tmpa[:], alog, AF.Exp), ['gdn_con'], ['tmpa'])
            self.V(lambda: nc.vector.scalar_tensor_tensor(gg[:], gg[:], -1.0, tmpa[:], ALU.mult, ALU.mult), ['gg', 'tmpa'], ['gg'])
            g2 = gg[:].rearrange("p s n -> p (s n)")
            pm2 = ps_misc[:, 0:NS * 8]
            self.mm(pm2, self.tri64[:], g2, ['tri64', 'gg'], ['ps_misc'])
            self.A(lambda: nc.scalar.copy(gc[:].rearrange("p s n -> p (s n)"), pm2), ['ps_misc'], ['gc'])
            self.mm(pm2, self.blk64[:], g2, ['blk64', 'gg'], ['ps_misc'])
            self.V(lambda: nc.vector.tensor_tensor(ekd[:].rearrange("p s n -> p (s n)"), pm2, gc[:].rearrange("p s n -> p (s n)"), ALU.subtract),
                   ['ps_misc', 'gc'], ['ekd'])
            self.A(lambda: nc.scalar.activation(ekd[:], ekd[:], AF.Exp), ['ekd'], ['ekd'])
            self.A(lambda: nc.scalar.activation(ebg[:], gc[:], AF.Exp), ['gc'], ['ebg'])
            self.V(lambda: nc.vector.tensor_tensor(ebg[:], ebg[:], beta[:], ALU.mult), ['ebg', 'beta'], ['ebg'])
            self.V(lambda: nc.vector.tensor_scalar(ngc[:], gc[:], -1.0, None, ALU.mult), ['gc'], ['ngc'])
            with ExitStack() as es2:
                sb2 = lambda name, shape, dtype=F32: es2.enter_context(nc.sbuf_tensor(name + _u, shape, dtype))
                qT = sb2("qT", [128, T]); kT = sb2("kT", [128, T])
                vTs = [sb2("vT%d" % x, [128, T]) for x in range(2)]
                oTg = sb2("goT", [128, 2, T], BF16); woutg = sb2("gwout", [128, 2, D], BF16)
                sq = acc[:, 0:TT]; rinv = acc[:, TT:2 * TT]
                names = ["dg", "t1", "Dl", "Du", "Amat", "ATm", "TTa", "TTb", "Pa", "PTa", "Pb", "PTb", "u", "eRB", "og", "sz"]
                hnames = ["attnT", "kbe", "vb", "kdec", "wTe", "wTo", "qdTe", "qdTo", "vnew", "TT16"]
                Bs = [{n: sb2("g%d_" % x + n, [128, 128]) for n in names} for x in range(2)]
                for x in range(2):
                    for n in hnames:
                        Bs[x][n] = sb2("g%d_" % x + n, [128, 128], BF16)
                Ss = [[sb2("g%d_S%d" % (x, k), [128, 128]) for k in range(3)] for x in range(2)]
                Shs = [[sb2("g%d_Sh%d" % (x, k), [128, 128], BF16) for k in range(3)] for x in range(2)]
                KKs = sb2("g_KKs", [128, 128]); KQs = sb2("g_KQs", [128, 128])
                cols = [{n: sb2("g%d_c" % x + n, [128, 1]) for n in ["ss", "ri"]} for x in range(2)]
                for x in range(2):
                    for n in ["wTe", "wTo", "qdTe", "qdTo"]:
                        self.P(lambda: nc.gpsimd.memset(Bs[x][n][:], 0.0), [], [(n, x)])
                cw = self.gdn_cw[:, i]

                def chain(x, hv):
                    B_ = Bs[x]; Sst = Ss[x]; Sh = Shs[x]; col = cols[x]; vT = vTs[x]; wz = wb[x]; wzk = 'wb%d' % x
                    bb = 4 * x
                    K = lambda n: (n, x)
                    def slot(n):
                        if n >= 16:
                            return banks[bb][:, n - 16, :], ('psb', bb)
                        return banks[bb + n // 4][:, n % 4, :], ('psb', bb + n // 4)
                    self.P(lambda: nc.gpsimd.memset(Sst[0][:], 0.0), [], [('S', x, 0)])
                    self.P(lambda: nc.gpsimd.memset(Sh[0][:], 0.0), [], [('Sh', x, 0)])
                    yield
                    sidx = 0
                    for s in range(NS):
                        sp_ = slice(s * 128, (s + 1) * 128)
                        gcc = gc[:, s, hv:hv + 1]; ngcc = ngc[:, s, hv:hv + 1]; betc = beta[:, s, hv:hv + 1]
                        ebgc = ebg[:, s, hv:hv + 1]; ekdc = ekd[:, s, hv:hv + 1]
                        sc_keys = ['gc', 'ngc', 'beta', 'ebg', 'ekd']
                        pKK, kKK = slot(0); pKQ, kKQ = slot(1); pRB, kRB = slot(2); pT1, kT1 = slot(3)
                        if x == 0:
                            self.mm(pKK, kT[:, sp_], kT[:, sp_], ['kT'], [kKK])
                            self.mm(pKQ, kT[:, sp_], qT[:, sp_], ['kT', 'qT'], [kKQ])
                            self.A(lambda: nc.scalar.copy(KKs[:], pKK), [kKK], ['KKs'])
                            self.A(lambda: nc.scalar.copy(KQs[:], pKQ), [kKQ], ['KQs'])
                        self.V(lambda: nc.vector.tensor_scalar(B_['dg'][:], self.ident[:], gcc, None, ALU.mult), ['ident'] + sc_keys, [K('dg')])
                        yield
                        self.mm(pRB, self.ones32[:], B_['dg'][:], ['ones32', K('dg')], [kRB])
                        yield
                        self.V(lambda: nc.vector.scalar_tensor_tensor(B_['t1'][:], pRB, -1.0, self.maskL[:], ALU.mult, ALU.add), [kRB, 'maskL'], [K('t1')])
                        yield
                        self.A(lambda: nc.scalar.activation(B_['Dl'][:], B_['t1'][:], AF.Exp, bias=gcc, scale=1.0), [K('t1')] + sc_keys, [K('Dl')])
                        yield
                        self.V(lambda: nc.vector.scalar_tensor_tensor(B_['t1'][:], pRB, 1.0, self.maskU[:], ALU.mult, ALU.add), [kRB, 'maskU'], [K('t1')])
                        yield
                        self.A(lambda: nc.scalar.activation(B_['Du'][:], B_['t1'][:], AF.Exp, bias=ngcc, scale=1.0), [K('t1')] + sc_keys, [K('Du')])
                        self.A(lambda: nc.scalar.activation(B_['eRB'][:], pRB, AF.Exp), [kRB], [K('eRB')])
                        yield
                        self.V(lambda: nc.vector.scalar_tensor_tensor(B_['Amat'][:], KKs[:], betc, B_['Dl'][:], ALU.mult, ALU.mult), ['KKs', K('Dl')] + sc_keys, [K('Amat')])
                        yield
                        self.V(lambda: nc.vector.tensor_tensor(B_['attnT'][:], KQs[:], B_['Du'][:], ALU.mult), ['KQs', K('Du')], [K('attnT')])
                        self.tr(pT1, B_['Amat'][:], [K('Amat')], [kT1])
                        yield
                        self.A(lambda: nc.scalar.copy(B_['ATm'][:], pT1), [kT1], [K('ATm')])
                        self.V(lambda: nc.vector.tensor_tensor(B_['TTa'][:], self.ident[:], pT1, ALU.subtract), ['ident', kT1], [K('TTa')])
                        yield
                        Pc, PTc, TTc = 'Amat', 'ATm', 'TTa'
                        for lvl in range(5):
                            Pn = 'Pa' if lvl % 2 == 0 else 'Pb'
                            PTn = 'PTa' if lvl % 2 == 0 else 'PTb'
                            TTn = 'TTb' if TTc == 'TTa' else 'TTa'
                            p1, k1 = slot(4 + (lvl % 2) * 3); p2, k2 = slot(5 + (lvl % 2) * 3); p3, k3 = slot(6 + (lvl % 2) * 3)
                            self.mm(p1, B_[PTc][:], B_[Pc][:], [K(PTc), K(Pc)], [k1])
                            if lvl < 4:
                                self.mm(p2, B_[Pc][:], B_[PTc][:], [K(PTc), K(Pc)], [k2])
                            yield
                            self.A(lambda: nc.scalar.copy(B_[Pn][:], p1), [k1], [K(Pn)])
                            if lvl < 4:
                                self.V(lambda: nc.vector.tensor_copy(B_[PTn][:], p2), [k2], [K(PTn)])
                            yield
                            self.mm(p3, B_[Pn][:], B_[TTc][:], [K(Pn), K(TTc)], [k3])
                            yield
                            self.V(lambda: nc.vector.tensor_tensor(B_[TTn][:], p3, B_[TTc][:], ALU.add), [k3, K(TTc)], [K(TTn)])
                            yield
                            Pc, PTc, TTc = Pn, PTn, TTn
                        pk_, kk_ = slot(10); pv_, kv_ = slot(11)
                        self.tr(pk_, kT[:, sp_], ['kT'], [kk_])
                        self.tr(pv_, vT[:, sp_], [('vT', x)], [kv_])
                        yield
                        self.A(lambda: nc.scalar.activation(B_['kbe'][:], pk_, AF.Identity, scale=ebgc), [kk_] + sc_keys, [K('kbe')])
                        self.A(lambda: nc.scalar.activation(B_['kdec'][:], pk_, AF.Identity, scale=ekdc), [kk_] + sc_keys, [K('kdec')])
                        self.V(lambda: nc.vector.tensor_scalar(B_['vb'][:], pv_, betc, None, ALU.mult), [kv_] + sc_keys, [K('vb')])
                        yield
                        pu_, ku_ = slot(12); pw_, kw_ = slot(13)
                        self.A(lambda: nc.scalar.copy(B_['TT16'][:], B_[TTc][:]), [K(TTc)], [K('TT16')])
                        yield
                        self.mm(pu_, B_['TT16'][:], B_['vb'][:], [K('TT16'), K('vb')], [ku_])
                        self.mm(pw_, B_['kbe'][:], B_['TT16'][:], [K('TT16'), K('kbe')], [kw_])
                        yield
                        self.A(lambda: nc.scalar.copy(B_['u'][:], pu_), [ku_], [K('u')])
                        self.V(lambda: nc.vector.tensor_copy(B_['wTe'][:, 0:64], pw_[:, 0:64]), [kw_], [K('wTe')])
                        self.V(lambda: nc.vector.tensor_copy(B_['wTo'][:, 64:128], pw_[:, 64:128]), [kw_], [K('wTo')])
                        yield
                        self.V(lambda: nc.vector.tensor_tensor(B_['qdTe'][:, 0:64], qT[:, s * 128:s * 128 + 64], B_['eRB'][:, 0:64], ALU.mult), ['qT', K('eRB')], [K('qdTe')])
                        self.V(lambda: nc.vector.tensor_tensor(B_['qdTo'][:, 64:128], qT[:, s * 128 + 64:(s + 1) * 128], B_['eRB'][:, 64:128], ALU.mult), ['qT', K('eRB')], [K('qdTo')])
                        yield
                        S0 = Sst[sidx % 3]; S1 = Sst[(sidx + 1) % 3]; S2 = Sst[(sidx + 2) % 3]
                        kS0 = ('S', x, sidx % 3); kS1 = ('S', x, (sidx + 1) % 3); kS2 = ('S', x, (sidx + 2) % 3)
                        H0 = Sh[sidx % 3]; H1 = Sh[(sidx + 1) % 3]; H2 = Sh[(sidx + 2) % 3]
                        kH0 = ('Sh', x, sidx % 3); kH1 = ('Sh', x, (sidx + 1) % 3); kH2 = ('Sh', x, (sidx + 2) % 3)
                        pa, ka = slot(14); pb, kb = slot(15); pc_, kc_ = slot(16); pd, kd = slot(17); po_, ko_ = slot(18)
                        self.mm(pa, B_['wTe'][:], H0[:], [K('wTe'), kH0], [ka])
                        yield
                        self.V(lambda: nc.vector.tensor_tensor(B_['vnew'][0:64, :], B_['u'][0:64, :], pa[0:64, :], ALU.subtract), [K('u'), ka], [K('vnew')])
                        yield
                        self.mm(pb, B_['kdec'][0:64, :], B_['vnew'][0:64, :], [K('kdec'), K('vnew')], [kb])
                        yield
                        self.V(lambda: nc.vector.scalar_tensor_tensor(S1[:], S0[:], B_['eRB'][:, 63:64], pb, ALU.mult, ALU.add), [kS0, K('eRB'), kb], [kS1])
                        self.A(lambda: nc.scalar.copy(H1[:], S1[:]), [kS1], [kH1])
                        yield
                        self.mm(pc_, B_['wTo'][:], H1[:], [K('wTo'), kH1], [kc_])
                        yield
                        self.V(lambda: nc.vector.tensor_tensor(B_['vnew'][64:128, :], B_['u'][64:128, :], pc_[64:128, :], ALU.subtract), [K('u'), kc_], [K('vnew')])
                        yield
                        self.mm(pd, B_['kdec'][64:128, :], B_['vnew'][64:128, :], [K('kdec'), K('vnew')], [kd])
                        yield
                        self.V(lambda: nc.vector.scalar_tensor_tensor(S2[:], S1[:], B_['eRB'][:, 127:128], pd, ALU.mult, ALU.add), [kS1, K('eRB'), kd], [kS2])
                        self.A(lambda: nc.scalar.copy(H2[:], S2[:]), [kS2], [kH2])
                        self.mm(po_, B_['qdTe'][:], H0[:], [K('qdTe'), kH0], [ko_], start=True, stop=False, inc=False)
                        self.mm(po_, B_['qdTo'][:], H1[:], [K('qdTo'), kH1], [ko_], start=False, stop=False, inc=False)
                        self.mm(po_, B_['attnT'][:], B_['vnew'][:], [K('attnT'), K('vnew')], [ko_], start=False, stop=True, inc=True)
                        sidx += 2
                        yield
                        pz, kz = slot(19)
                        for kc in range(NKC):
                            self.mm(pz, self.h16[:, kc, sp_], wz[:, kc, :], [wzk, self.kh16(kc, s // 4)], [kz], start=(kc == 0), stop=(kc == NKC - 1), inc=(kc == NKC - 1))
                        yield
                        self.A(lambda: nc.scalar.activation(B_['sz'][:], pz, AF.Silu), [kz], [K('sz')])
                        self.A(lambda: nc.scalar.activation(B_['og'][:], po_, AF.Square, accum_out=col['ss'][:]), [ko_], [K('og'), K('ss')])
                        yield
                        self.A(lambda: nc.scalar.activation(col['ri'][:], col['ss'][:], AF.Sqrt, bias=self.c1e6[:, 0:1], scale=1.0 / 128), [K('ss'), 'c1e6'], [K('ri')])
                        yield
                        self.V(lambda: nc.vector.reciprocal(col['ri'][:], col['ri'][:]), [K('ri')], [K('ri')])
                        self.V(lambda: nc.vector.scalar_tensor_tensor(B_['og'][:], po_, col['ri'][:, 0:1], self.gdn_nw[:, i, :], ALU.mult, ALU.mult), [ko_, K('ri'), 'gdn_nw'], [K('og')])
                        self.V(lambda: nc.vector.tensor_tensor(B_['og'][:], B_['og'][:], B_['sz'][:], ALU.mult), [K('og'), K('sz')], [K('og')])
                        yield
                        self.tr(pz, B_['og'][:], [K('og')], [kz])
                        yield
                        self.A(lambda: nc.scalar.copy(oTg[:, x, sp_], pz), [kz], [('goT', x, s // 4)])
                        yield

                for hq in range(4):
                    hvs = (2 * hq, 2 * hq + 1)
                    self.load_w(wb[0][:], Wv[:, :, hq * 128:(hq + 1) * 128], 'wb0')
                    self.load_w(wb[1][:], Wv[:, :, 512 + hq * 128:512 + (hq + 1) * 128], 'wb1')
                    self.load_w(wb[2][:], Wv[:, :, 1024 + hvs[0] * 128:1024 + (hvs[0] + 1) * 128], 'wb2')
                    self.load_w(wb[3][:], Wv[:, :, 1024 + hvs[1] * 128:1024 + (hvs[1] + 1) * 128], 'wb3')
                    for (dst, dk_, wi_, cchunk, l2, sc) in ((qT, 'qT', 0, hq, True, 128.0 ** -0.5), (kT, 'kT', 1, 4 + hq, True, 1.0),
                                                            (vTs[0], ('vT', 0), 2, 8 + hvs[0], False, 1.0), (vTs[1], ('vT', 1), 3, 8 + hvs[1], False, 1.0)):
                        self.proj_conv(dst, dk_, wb[wi_], 'wb%d' % wi_, cw[:, cchunk, 0:4], None, 'gdn_cw')
                        if l2:
                            for tt in range(T // TT):
                                ts = slice(tt * TT, (tt + 1) * TT)
                                self.A(lambda: nc.scalar.activation(sq, dst[:, ts], AF.Square), [dk_], ['convacc'])
                                self.mm(ps_misc[:], self.ones32[:], sq, ['ones32', 'convacc'], ['ps_misc'])
                                self.A(lambda: nc.scalar.activation(rinv, ps_misc[:], AF.Sqrt, bias=self.c1e6[:, 0:1], scale=1.0), ['ps_misc', 'c1e6'], ['convacc'])
                                self.V(lambda: nc.vector.reciprocal(rinv, rinv), ['convacc'], ['convacc'])
                                self.V(lambda: nc.vector.scalar_tensor_tensor(dst[:, ts], dst[:, ts], sc, rinv, ALU.mult, ALU.mult), [dk_, 'convacc'], [dk_])
                    self.load_w(wb[0][:], Wv[:, :, 2048 + hvs[0] * 128:2048 + (hvs[0] + 1) * 128], 'wb0')
                    self.load_w(wb[1][:], Wv[:, :, 2048 + hvs[1] * 128:2048 + (hvs[1] + 1) * 128], 'wb1')
                    gens = [chain(0, hvs[0]), chain(1, hvs[1])]
                    while gens:
                        for g_ in list(gens):
                            try:
                                next(g_)
                            except StopIteration:
                                gens.remove(g_)
                    self.out_proj_acc(self.hyb_w_out[i, hvs[0] * 128:(hvs[0] + 2) * 128, :], oTg,
                                      [('goT', x, k) for x in range(2) for k in range(4)], 2, ps_proj, cres, woutg, 'gwout')
            c.barrier()
            wout = sb("hwout", [128, 4, D], BF16)
            oT = sb("hoT", [128, 4, T], BF16)
            self._banks1 = banks[1]
            self.ssd(layer, es, ps_proj, ps_misc, psq, wb, wout, oT, xp, acc, Wv)
            c.barrier()
            c.alias = {}

    def ssd(self, layer, es, ps_proj, ps_misc, psq, wb, wout, oT, xp, acc, Wv):
        nc, c = self.nc, self.c
        i = layer // 2
        NS = T // 128
        cres = 1.0 / ALPHA
        def slot(n):
            return psq[n // 4][:, n % 4, :], ('psq', n // 4)
        banks1 = self._banks1
        _u = "_s%d" % layer
        sb = lambda name, shape, dtype=F32: es.enter_context(nc.sbuf_tensor(name + _u, shape, dtype))
        wdt = sb("wdt", [128, NKC, 16], BF16)
        self.load_w(wdt[:], Wv[:, :, 5648:5664], 'wdt')
        dt_ = sb("dt_tok", [128, NS, 16]); acs = sb("acs", [128, NS, 16]); nacs = sb("nacs", [128, NS, 16])
        edl = sb("edl", [128, NS, 16]); adt = sb("adt", [128, NS, 16]); ea = sb("ea", [128, NS, 16])
        pm = ps_misc[:, 0:NS * 16].rearrange("p (s n) -> p s n", n=16)
        for s in range(NS):
            for kc in range(NKC):
                self.mm(pm[:, s, :], self.h16[:, kc, s * 128:(s + 1) * 128], wdt[:, kc, :], ['wdt', self.kh16(kc, s // 4)], ['ps_misc'],
                        start=(kc == 0), stop=(kc == NKC - 1), inc=(kc == NKC - 1))
        scon = self.ssd_con[:, i, :]
        alog = scon[:, 0:16].unsqueeze(1).to_broadcast([128, NS, 16])
        dtb = scon[:, 16:32].unsqueeze(1).to_broadcast([128, NS, 16])
        self.V(lambda: nc.vector.tensor_tensor(dt_[:], pm, dtb, ALU.add), ['ps_misc', 'ssd_con'], ['dt'])
        self.A(lambda: nc.scalar.activation(dt_[:], dt_[:], AF.Exp), ['dt'], ['dt'])
        self.A(lambda: nc.scalar.activation(dt_[:], dt_[:], AF.Ln, bias=self.onec[:, 0:1], scale=1.0), ['dt', 'onec'], ['dt'])
        self.A(lambda: nc.scalar.activation(ea[:], alog, AF.Exp), ['ssd_con'], ['ea'])
        self.V(lambda: nc.vector.scalar_tensor_tensor(adt[:], dt_[:], -1.0, ea[:], ALU.mult, ALU.mult), ['dt', 'ea'], ['adt'])
        f2 = lambda t_: t_[:].rearrange("p s n -> p (s n)")
        pm2 = ps_misc[:, 0:NS * 16]
        self.mm(pm2, self.tri128[:], f2(adt), ['tri128', 'adt'], ['ps_misc'])
        self.A(lambda: nc.scalar.copy(f2(acs), pm2), ['ps_misc'], ['acs'])
        self.mm(pm2, self.ones32[:], f2(adt), ['ones32', 'adt'], ['ps_misc'])
        self.V(lambda: nc.vector.tensor_tensor(f2(edl), pm2, f2(acs), ALU.subtract), ['ps_misc', 'acs'], ['edl'])
        self.A(lambda: nc.scalar.activation(edl[:], edl[:], AF.Exp), ['edl'], ['edl'])
        self.V(lambda: nc.vector.tensor_scalar(nacs[:], acs[:], -1.0, None, ALU.mult), ['acs'], ['nacs'])
        sck = ['dt', 'acs', 'nacs', 'edl']
        xT = sb("xT16", [128, 4, T], BF16)
        BT = sb("BT", [128, T]); CT = sb("CT", [128, T])
        names = ["dg", "t1", "Du", "MT", "eRB", "CdT"]
        Bs = [{n: sb("s%d_" % x + n, [128, 128]) for n in names} for x in range(2)]
        for x in (2, 3):
            o = 128 + (x - 2) * 896
            Bs.append({n: xp[:, o + k_ * 128:o + (k_ + 1) * 128] for k_, n in enumerate(names)})
        Btok = sb("s_Btok", [128, 128]); xc32 = sb("s_xc32", [128, 128]); CBs = sb("s_CBs", [128, 128])
        xtok = acc[:, 512:1024]; szb = acc[:, 0:512]
        ygrp = sb("s_ygrp", [128, 512])
        xdts = [sb("s_xdt%d" % x, [128, 64]) for x in range(2)]; xdtds = [sb("s_xdtd%d" % x, [128, 64]) for x in range(2)]
        for x in (2, 3):
            o = 128 + (x - 2) * 896 + 768
            xdts.append(xp[:, o:o + 64]); xdtds.append(xp[:, o + 64:o + 128])
        chain_banks = [(psq[2], ('psq', 2)), (psq[3], ('psq', 3)), (psq[4], ('psq', 4)), (banks1, ('ps_proj', 1))]
        ST = sb("s_ST", [128, 8, 64])
        col = {n: sb("s_c" + n, [128, 1]) for n in ["ss", "ri"]}
        wz = wout[:].rearrange("p a b -> p (a b)").rearrange("p (k n) -> p k n", n=512)
        cw = self.ssd_cw[:, i]
        for g in range(2):
            for (dst, dk_, col0, cchunk) in ((BT, 'BT', 5136 + g * 128, 8 + g), (CT, 'CT', 5392 + g * 128, 10 + g)):
                self.load_w(wb[0][:], Wv[:, :, col0:col0 + 128], 'wb0')
                self.proj_conv(dst, dk_, wb[0], 'wb0', cw[:, cchunk, 0:4], cw[:, cchunk, 4:5], 'ssd_cw')
            for cc in range(4):
                k = cc % 2
                col0 = 4112 + (g * 4 + cc) * 128
                self.load_w(wb[1 + k][:], Wv[:, :, col0:col0 + 128], 'wb%d' % (1 + k))
                self.proj_conv(xT[:, cc, :], ('xT', cc), wb[1 + k], 'wb%d' % (1 + k), cw[:, g * 4 + cc, 0:4], cw[:, g * 4 + cc, 4:5], 'ssd_cw')
            c.dma('pool', wz, Wv[:, :, 3088 + g * 512:3088 + (g + 1) * 512], writes=['hwout'])
            self.P(lambda: nc.gpsimd.memset(ST[:], 0.0), [], [('ST', e_) for e_ in range(8)])
            c.barrier()

            def hchain(x, s, heads):
                B_ = Bs[x]; xdt = xdts[x]; xdtd = xdtds[x]
                K = lambda n: (n, x)
                sp_ = slice(s * 128, (s + 1) * 128)
                bank, kb_ = chain_banks[x]
                pRB = bank[:, 0, :]; py = bank[:, 1, :]; pS = bank[:, 2, :]
                for e in heads:
                    h = g * 8 + e
                    hs = slice(e * 64, (e + 1) * 64)
                    acc_ = acs[:, s, h:h + 1]; nacc = nacs[:, s, h:h + 1]; dtc = dt_[:, s, h:h + 1]; edc = edl[:, s, h:h + 1]
                    self.V(lambda: nc.vector.tensor_scalar(B_['dg'][:], self.ident[:], acc_, None, ALU.mult), ['ident'] + sck, [K('dg')])
                    self.V(lambda: nc.vector.tensor_scalar(xdt[:], xtok[:, hs], dtc, None, ALU.mult), ['xtok'] + sck, [K('xdt')])
                    yield
                    self.mm(pRB, self.ones32[:], B_['dg'][:], ['ones32', K('dg')], [kb_])
                    self.V(lambda: nc.vector.tensor_scalar(xdtd[:], xdt[:], edc, None, ALU.mult), [K('xdt')] + sck, [K('xdtd')])
                    yield
                    self.V(lambda: nc.vector.scalar_tensor_tensor(B_['t1'][:], pRB, 1.0, self.maskU128[:], ALU.mult, ALU.add), [kb_, 'maskU128'], [K('t1')])
                    self.A(lambda: nc.scalar.activation(B_['eRB'][:], pRB, AF.Exp), [kb_], [K('eRB')])
                    yield
                    self.A(lambda: nc.scalar.activation(B_['Du'][:], B_['t1'][:], AF.Exp, bias=nacc, scale=1.0), [K('t1')] + sck, [K('Du')])
                    self.V(lambda: nc.vector.tensor_tensor(B_['CdT'][:], CT[:, sp_], B_['eRB'][:], ALU.mult), ['CT', K('eRB')], [K('CdT')])
                    yield
                    self.V(lambda: nc.vector.tensor_tensor(B_['MT'][:], CBs[:], B_['Du'][:], ALU.mult), ['CBs', K('Du')], [K('MT')])
                    yield
                    self.mm(py[:, 0:64], B_['MT'][:], xdt[:], [K('MT'), K('xdt')], [kb_], start=True, stop=False, inc=False)
                    self.mm(py[:, 0:64], B_['CdT'][:], ST[:, e, :], [K('CdT'), ('ST', e)], [kb_], start=False, stop=True, inc=True)
                    self.mm(pS[:, 0:64], Btok[:], xdtd[:], ['Btok', K('xdtd')], [kb_])
                    yield
                    self.V(lambda: nc.vector.scalar_tensor_tensor(ygrp[:, hs], xtok[:, hs], self.ssd_con[:, i, 32 + h:33 + h], py[:, 0:64], ALU.mult, ALU.add),
                           ['xtok', 'ssd_con', kb_], [('ygrp', e)])
                    self.V(lambda: nc.vector.scalar_tensor_tensor(ST[:, e, :], ST[:, e, :], B_['eRB'][:, 127:128], pS[:, 0:64], ALU.mult, ALU.add),
                           [('ST', e), K('eRB'), kb_], [('ST', e)])
                    yield

            for s in range(NS):
                sp_ = slice(s * 128, (s + 1) * 128)
                pCB, kCB = slot(0); pBt, kBt = slot(1)
                self.mm(pCB, BT[:, sp_], CT[:, sp_], ['BT', 'CT'], [kCB])
                self.tr(pBt, BT[:, sp_], ['BT'], [kBt])
                self.A(lambda: nc.scalar.copy(CBs[:], pCB), [kCB], ['CBs'])
                self.A(lambda: nc.scalar.copy(Btok[:], pBt), [kBt], ['Btok'])
                for cc in range(4):
                    px, kx = slot(4 + cc)
                    self.V(lambda: nc.vector.tensor_copy(xc32[:], xT[:, cc, sp_]), [('xT', cc)], ['xc32'])
                    self.tr(px, xc32[:], ['xc32'], [kx])
                    self.A(lambda: nc.scalar.copy(xtok[:, cc * 128:(cc + 1) * 128], px), [kx], ['xtok'])
                self.run_chains([hchain(x_, s, range(x_, 8, 4)) for x_ in range(4)])
                ygk = [('ygrp', e_) for e_ in range(8)]
                pz = ps_proj[0]; kz = ('ps_proj', 0)
                for kc in range(NKC):
                    self.mm(pz[:], self.h16[:, kc, sp_], wz[:, kc, :], ['hwout', self.kh16(kc, s // 4)], [kz], start=(kc == 0), stop=(kc == NKC - 1), inc=(kc == NKC - 1))
                self.A(lambda: nc.scalar.activation(szb, pz[:], AF.Silu), [kz], ['szb'])
                self.V(lambda: nc.vector.tensor_tensor(ygrp[:], ygrp[:], szb, ALU.mult), ygk + ['szb'], ygk)
                self.A(lambda: nc.scalar.activation(szb, ygrp[:], AF.Square, accum_out=col['ss'][:]), ygk, ['szb', 'ss'])
                self.A(lambda: nc.scalar.activation(col['ri'][:], col['ss'][:], AF.Sqrt, bias=self.c1e6[:, 0:1], scale=1.0 / 512), ['ss', 'c1e6'], ['ri'])
                self.V(lambda: nc.vector.reciprocal(col['ri'][:], col['ri'][:]), ['ri'], ['ri'])
                self.V(lambda: nc.vector.tensor_scalar(ygrp[:], ygrp[:], col['ri'][:, 0:1], None, ALU.mult), ygk + ['ri'], ygk)
                for cc in range(4):
                    pt, kt = slot(4 + cc)
                    self.tr(pt, ygrp[:, cc * 128:(cc + 1) * 128], ygk, [kt])
                    self.A(lambda: nc.scalar.activation(oT[:, cc, sp_], pt, AF.Identity, scale=self.ssd_nw[:, i, g * 4 + cc:g * 4 + cc + 1]),
                           [kt, 'ssd_nw'], [('oT', s // 4)])
            self.out_proj_acc(self.hyb_w_out[i, 1024 + g * 512:1024 + (g + 1) * 512, :], oT, [('oT', k) for k in range(4)], 4, ps_proj, cres, wout, 'hwout')
            c.barrier()

    def nsa(self, layer):
        nc, c = self.nc, self.c
        i = layer // 2
        NS = T // 128
        NT = T // TT
        cres = 1.0 / ALPHA
        W = self.nsa_w_in[i].rearrange("(kc p) n -> p kc n", p=128)
        Wsw = self.nsa_w_sw[i].rearrange("(kc p) n -> p kc n", p=128)
        with ExitStack() as es:
            self._uid = getattr(self, '_uid', 0) + 1
            _u = "_%d" % self._uid
            sb = lambda name, shape, dtype=F32: es.enter_context(nc.sbuf_tensor(name + _u, shape, dtype))
            ps = lambda name, shape, dtype=F32: es.enter_context(nc.psum_tensor(name + _u, shape, dtype))
            psS = [ps("psS%d" % k, [128, TT]) for k in range(4)]
            ps_proj = [psS[0], psS[1]]
            c.alias = {('ps_proj', 0): ('psS', 0), ('ps_proj', 1): ('psS', 1)}
            psO = {br: ps("psO%d" % br, [128, 4, 128]) for br in range(3)}
            ps_misc = ps("ps_misc", [128, TT])
            cosT = sb("cosT", [128, T]); sinS = sb("sinS", [128, T])
            Mc16 = sb("Mc16", [128, T], BF16); Cm16 = sb("Cm16", [128, 4, TT], BF16); Wm16 = sb("Wm16", [128, 4, TT], BF16)
            E16 = sb("E16", [128, 16, 128], BF16); agg16 = sb("agg16", [128, 33], BF16)
            M12 = sb("M12", [128, 2, NS * 32])
            id16 = sb("id16", [128, 128], BF16); z16 = sb("z16", [128, 520], BF16)
            self.P(lambda: nc.gpsimd.memset(z16[:], 0.0), [], ['z16'])
            self.P(lambda: nc.gpsimd.tensor_copy(id16[:], self.ident[:]), ['ident'], ['id16'])
            c.dma('pool', Mc16[:], self.nsa_Mc[:, :], writes=['Mc16'])
            c.dma('pool', Cm16[:], self.nsa_Cm[:, :, :], writes=['Cm16'])
            c.dma('pool', Wm16[:], self.nsa_Wm[:, :, :], writes=['Wm16'])
            c.dma('pool', E16[0:32, :, :], self.nsa_E[:, :, :], writes=['E16'])
            c.dma('pool', agg16[:], self.nsa_agg[:, :], writes=['agg16'])
            c.dma('sp', M12[:], self.nsa_M12[:, :, :], writes=['M12'])
            es_t = ExitStack()
            sbt = lambda name, shape, dtype=F32: es_t.enter_context(nc.sbuf_tensor(name + _u, shape, dtype))
            posi = sbt("posi", [128, T], I32)
            ang = sbt("ang", [128, T]); kf = sbt("kf", [128, T])
            c.dma('sp', posi[:], self.pos_rep[:, :], writes=['posi'])
            self.V(lambda: nc.vector.tensor_copy(ang[:], posi[:]), ['posi'], ['ang'])
            self.V(lambda: nc.vector.tensor_scalar(ang[:], ang[:], self.rope_c[:, 0:1], None, ALU.mult), ['ang', 'rope_c'], ['ang'])
            MAGIC = 12582912.0
            TWO_PI = 2.0 * np.pi
            C1 = float(np.float32(TWO_PI)); C2 = float(TWO_PI - np.float64(np.float32(TWO_PI)))
            for (dst, shift, dk_) in ((sinS, 0.0, 'sinS'), (cosT, 0.5 * np.pi, 'cosT')):
                self.V(lambda: nc.vector.tensor_scalar(kf[:], ang[:], shift, 1.0 / TWO_PI, ALU.add, ALU.mult), ['ang'], ['kf'])
                self.V(lambda: nc.vector.tensor_scalar(kf[:], kf[:], MAGIC, None, ALU.add), ['kf'], ['kf'])
                self.V(lambda: nc.vector.tensor_scalar(kf[:], kf[:], -MAGIC, None, ALU.add), ['kf'], ['kf'])
                self.V(lambda: nc.vector.tensor_scalar(dst[:], ang[:], shift, None, ALU.add), ['ang'], [dk_])
                self.V(lambda: nc.vector.scalar_tensor_tensor(dst[:], kf[:], -C1, dst[:], ALU.mult, ALU.add), ['kf', dk_], [dk_])
                self.V(lambda: nc.vector.scalar_tensor_tensor(dst[:], kf[:], -C2, dst[:], ALU.mult, ALU.add), ['kf', dk_], [dk_])
                self.V(lambda: nc.vector.tensor_scalar(dst[:], dst[:], float(np.pi), -float(np.pi), ALU.min, ALU.max), [dk_], [dk_])
                self.A(lambda: nc.scalar.activation(dst[:], dst[:], AF.Sin), [dk_], [dk_])
            self.V(lambda: nc.vector.tensor_scalar(sinS[:], sinS[:], self.rope_c[:, 1:2], None, ALU.mult), ['sinS', 'rope_c'], ['sinS'])
            c.barrier()
            es_t.close()
            gates = sb("gates", [128, NS, 48])
            wg = sb("wg", [128, NKC, 48], BF16)
            self.load_w(wg[:], W[:, :, 1792:1840], 'wg')
            pm = ps_misc[:, 0:384].rearrange("p (s n) -> p s n", n=48)
            for half in range(2):
                for s8 in range(8):
                    s_ = half * 8 + s8
                    for kc in range(NKC):
                        self.mm(pm[:, s8, :], self.h16[:, kc, s_ * 128:(s_ + 1) * 128], wg[:, kc, :], ['wg', self.kh16(kc, s_ // 4)], ['ps_misc'],
                                start=(kc == 0), stop=(kc == NKC - 1), inc=(kc == NKC - 1))
                self.A(lambda: nc.scalar.activation(gates[:, half * 8:(half + 1) * 8, :], pm, AF.Exp, scale=-1.0), ['ps_misc'], ['gates'])
            self.V(lambda: nc.vector.tensor_scalar(gates[:], gates[:], 1.0, None, ALU.add), ['gates'], ['gates'])
            self.V(lambda: nc.vector.reciprocal(gates[:], gates[:]), ['gates'], ['gates'])
            wa = [sb("nwa%d" % k, [128, NKC, 128], BF16) for k in range(2)]
            w1t = sb("w1t", [128, 16, 256], BF16); w2t = sb("w2t", [128, 2, 128], BF16)
            pos16 = sb("pos16", [128, 16], BF16); bcol = sb("bcol", [128, 1])
            hT16 = sb("hT16", [128, 2, 128], BF16)
            kcmp = sb("kcmp", [128, 128], BF16); vcx = sb("vcx", [128, 65], BF16)
            ksT2 = sb("ksT2", [128, T], BF16); kwT2 = sb("kwT2", [128, T], BF16)
            vsx = sb("vsx", [128, NS, 65], BF16); vwx = sb("vwx", [128, NS, 65], BF16)
            selT = sb("selT", [128, T], BF16)
            imp = sb("imp", [128, NS, 32]); score = sb("score", [128, NS, 32]); top8 = sb("top8", [128, 8]); selb = sb("selb", [128, 32])
            qT16 = sb("qT16", [128, T], BF16); qrT16 = sb("qrT16", [128, T], BF16)
            kc2 = qT16
            c.alias['kc2'] = 'qT16'
            t1 = sb("nt1", [128, TT]); t2 = sb("nt2", [128, TT])
            pT = [sb("pT%d" % k, [128, TT], BF16) for k in range(4)]
            rl4 = [sb("rl4_%d" % k, [128, 4, 1]) for k in range(3)]
            ctmp = [sb("ctmp%d" % k, [128, 4, 64]) for k in range(3)]
            oacc = sb("oacc", [128, 4, 128]); oT = sb("noT", [128, 1, T], BF16); wout = sb("nwout", [128, 1, D], BF16)
            rl = sb("rl", [128, 1])
            self.P(lambda: nc.gpsimd.memset(vcx[:], 1.0), [], ['vcx'])
            self.P(lambda: nc.gpsimd.memset(vsx[:], 1.0), [], ['vsx'])
            self.P(lambda: nc.gpsimd.memset(vwx[:], 1.0), [], ['vwx'])

            def load_dup(dst, key, src, col0):
                c.dma('pool', dst[:, :, 0:64], src[:, :, col0:col0 + 64], writes=[key])
                c.dma('pool', dst[:, :, 64:128], src[:, :, col0:col0 + 64], writes=[key])

            def proj_tile(wt, wkey, tt, k):
                ts = slice(tt * TT, (tt + 1) * TT)
                pp = ps_proj[k]; pk = ('ps_proj', k)
                for kc in range(NKC):
                    self.mm(pp[:], wt[:, kc, :], self.h16[:, kc, ts], [wkey, self.kh16(kc, tt)], [pk], start=(kc == 0), stop=(kc == NKC - 1), inc=(kc == NKC - 1))
                return pp, pk

            def rope_proj(dst, dkey, col0, colsw, dup):
                if dup:
                    load_dup(wa[0], 'wa0', W, col0); load_dup(wa[1], 'wa1', Wsw, colsw)
                else:
                    self.load_w(wa[0][:], W[:, :, col0:col0 + 128], 'wa0'); self.load_w(wa[1][:], Wsw[:, :, colsw:colsw + 128], 'wa1')
                for tt in range(NT):
                    ts = slice(tt * TT, (tt + 1) * TT)
                    p0, k0 = proj_tile(wa[0], 'wa0', tt, 0)
                    p1, k1 = proj_tile(wa[1], 'wa1', tt, 1)
                    self.V(lambda: nc.vector.tensor_tensor(t1[:], p0[:], cosT[:, ts], ALU.mult), [k0, 'cosT'], ['nt1'])
                    self.V(lambda: nc.vector.tensor_tensor(t2[:], p1[:], sinS[:, ts], ALU.mult), [k1, 'sinS'], ['nt2'])
                    self.P(lambda: nc.gpsimd.tensor_tensor(dst[:, ts], t1[:], t2[:], ALU.add), ['nt1', 'nt2'], [dkey])
                    if dst is qrT16:
                        self.A(lambda: nc.scalar.copy(qT16[:, ts], p0[:]), [k0], ['qT16'])

            def cmp_scores(hh, tt, k):
                ts = slice(tt * TT, (tt + 1) * TT)
                hp = slice(hh * 64, (hh + 1) * 64)
                pS = psS[k]; kS = ('psS', k)
                self.mm(pS[0:127, :], kcmp[hp, 0:127], qT16[hp, ts], ['kcmp', 'qT16'], [kS], start=True, stop=False, inc=False)
                self.mm(pS[0:127, :], id16[0:127, 0:127], Mc16[0:127, ts], ['id16', 'Mc16'], [kS], start=False, stop=True, inc=True)
                self.A(lambda: nc.scalar.activation(pT[k][0:127, :], pS[0:127, :], AF.Exp, scale=0.125), [kS], [('pT', k)])

            for g in range(2):
                rope_proj(ksT2, 'ksT2', 1280 + g * 64, 1024 + g * 64, True)
                rope_proj(kwT2, 'kwT2', 1536 + g * 64, 1152 + g * 64, True)
                c.dma('pool', wa[0][:, :, 0:64], W[:, :, 1408 + g * 64:1408 + (g + 1) * 64], writes=['wa0'])
                c.dma('pool', wa[0][:, :, 64:128], W[:, :, 1664 + g * 64:1664 + (g + 1) * 64], writes=['wa0'])
                pv4 = ps_misc[:, 0:512].rearrange("p (s n) -> p s n", n=128)
                for s4 in range(4):
                    for sj in range(4):
                        s_ = s4 * 4 + sj
                        for kc in range(NKC):
                            self.mm(pv4[:, sj, :], self.h16[:, kc, s_ * 128:(s_ + 1) * 128], wa[0][:, kc, :], ['wa0', self.kh16(kc, s_ // 4)], ['ps_misc'],
                                    start=(kc == 0), stop=(kc == NKC - 1), inc=(kc == NKC - 1))
                    self.A(lambda: nc.scalar.copy(vsx[:, s4 * 4:(s4 + 1) * 4, 0:64], pv4[:, :, 0:64]), ['ps_misc'], ['vsx'])
                    self.A(lambda: nc.scalar.copy(vwx[:, s4 * 4:(s4 + 1) * 4, 0:64], pv4[:, :, 64:128]), ['ps_misc'], ['vwx'])
                for kv in range(2):
                    load_dup(wa[0], 'wa0', W, (1024 if kv == 0 else 1152) + g * 64)
                    for tt in range(NT):
                        p0, k0 = proj_tile(wa[0], 'wa0', tt, tt % 2)
                        self.A(lambda: nc.scalar.copy(kc2[0:64, tt * TT:(tt + 1) * TT], p0[0:64, :]), [k0], ['kc2'])
                        if tt == 0:
                            self.V(lambda: nc.vector.tensor_copy(kc2[64:128, 0:TT - 1], p0[64:128, 1:TT]), [k0], ['kc2'])
                        else:
                            self.V(lambda: nc.vector.tensor_copy(kc2[64:128, tt * TT - 1:(tt + 1) * TT - 1], p0[64:128, :]), [k0], ['kc2'])
                    c.dma('pool', w1t[:], self.nsa_cmp_w1[i, kv].rearrange("(m p) j -> p m j", p=128), writes=['w1t'])
                    c.dma('pool', pos16[:], self.nsa_pos2[:, i, kv, :], writes=['pos16'])
                    w2v = self.nsa_cmp_w2[i, kv].rearrange("(jc p) d -> p jc d", p=128)
                    c.dma('pool', w2t[:, :, 0:64], w2v, writes=['w2t'])
                    c.dma('pool', w2t[:, :, 64:128], w2v, writes=['w2t'])
                    for jc in range(2):
                        js = slice(jc * 128, (jc + 1) * 128)
                        for m in range(16):
                            self.mm(ps_misc[:, 0:1], w1t[:, m, js], pos16[:, m:m + 1], ['w1t', 'pos16'], ['ps_misc'], start=(m == 0), stop=(m == 15), inc=(m == 15))
                        self.A(lambda: nc.scalar.copy(bcol[:], ps_misc[:, 0:1]), ['ps_misc'], ['bcol'])
                        for m in range(16):
                            self.mm(ps_misc[:, 0:127], w1t[:, m, js], kc2[:, 2 * m:2 * m + 16 * 126 + 1:16], ['w1t', 'kc2'], ['ps_misc'],
                                    start=(m == 0), stop=(m == 15), inc=(m == 15))
                        self.A(lambda: nc.scalar.activation(hT16[:, jc, 0:127], ps_misc[:, 0:127], AF.Silu, bias=bcol[:, 0:1], scale=1.0), ['ps_misc', 'bcol'], ['hT16'])
                    if kv == 0:
                        for jc in range(2):
                            self.mm(ps_misc[:, 0:127], w2t[:, jc, :], hT16[:, jc, 0:127], ['w2t', 'hT16'], ['ps_misc'], start=(jc == 0), stop=(jc == 1), inc=(jc == 1))
                        self.A(lambda: nc.scalar.copy(kcmp[:, 0:127], ps_misc[:, 0:127]), ['ps_misc'], ['kcmp'])
                    else:
                        for jc in range(2):
                            self.mm(ps_misc[0:127, 0:64], hT16[:, jc, 0:127], w2t[:, jc, 0:64], ['w2t', 'hT16'], ['ps_misc'], start=(jc == 0), stop=(jc == 1), inc=(jc == 1))
                        self.A(lambda: nc.scalar.copy(vcx[0:127, 0:64], ps_misc[0:127, 0:64]), ['ps_misc'], ['vcx'])
                self.P(lambda: nc.gpsimd.memset(imp[:], 0.0), [], ['imp'])

                def p1chain(x):
                    for tt in range(NT):
                        k = 2 * x + (tt % 2)
                        cmp_scores(x, tt, k)
                        yield
                        pI = psO[x]; kI = ('psO', x)
                        for st in range(4):
                            self.mm(pI[:, st, 0:33], pT[k][0:127, st * 128:(st + 1) * 128], agg16[0:127, :], [('pT', k), 'agg16'], [kI])
                        yield
                        self.V(lambda: nc.vector.tensor_scalar(rl4[x][:], pI[:, :, 32:33], 1e-30, None, ALU.max), [kI], [('rl4', x)])
                        self.V(lambda: nc.vector.reciprocal(rl4[x][:], rl4[x][:]), [('rl4', x)], [('rl4', x)])
                        yield
                        self.V(lambda: nc.vector.tensor_tensor(ctmp[x][:, :, 0:32], pI[:, :, 0:32], rl4[x][:].to_broadcast([128, 4, 32]), ALU.mult),
                               [kI, ('rl4', x)], [('ctmp', x)])
                        yield
                        self.V(lambda: nc.vector.tensor_tensor(imp[:, tt * 4:(tt + 1) * 4, :], imp[:, tt * 4:(tt + 1) * 4, :], ctmp[x][:, :, 0:32], ALU.add),
                               ['imp', ('ctmp', x)], ['imp'])
                        yield

                for c4 in range(4):
                    cg = g * 4 + c4
                    self.load_w(wa[0][:], W[:, :, cg * 128:(cg + 1) * 128], 'wa0')
                    for tt in range(NT):
                        p0, k0 = proj_tile(wa[0], 'wa0', tt, tt % 2)
                        self.A(lambda: nc.scalar.copy(qT16[:, tt * TT:(tt + 1) * TT], p0[:]), [k0], ['qT16'])
                    self.run_chains([p1chain(0), p1chain(1)])
                f3 = lambda t_: t_[:].rearrange("p s n -> p (s n)")
                self.V(lambda: nc.vector.tensor_tensor(f3(score), f3(imp), M12[:, 0, :], ALU.mult), ['imp', 'M12'], ['score'])
                self.V(lambda: nc.vector.tensor_tensor(f3(score), f3(score), M12[:, 1, :], ALU.add), ['score', 'M12'], ['score'])
                for s_ in range(NS):
                    self.V(lambda: nc.vector.max(top8[:], score[:, s_, :]), ['score'], ['top8'])
                    self.V(lambda: nc.vector.tensor_scalar(selb[:], score[:, s_, :], top8[:, 7:8], None, ALU.is_ge), ['score', 'top8'], ['selb'])
                    self.V(lambda: nc.vector.tensor_scalar(selb[:], selb[:], -1.0, 30000.0, ALU.add, ALU.mult), ['selb'], ['selb'])
                    self.tr(ps_misc[0:32, 0:128], selb[:], ['selb'], ['ps_misc'])
                    self.A(lambda: nc.scalar.copy(selT[0:32, s_ * 128:(s_ + 1) * 128], ps_misc[0:32, 0:128]), ['ps_misc'], ['selT'])
                def combine(br, e, tt, hp):
                    x = br
                    pO = psO[br]; kO = ('psO', br)
                    self.V(lambda: nc.vector.tensor_scalar(rl4[x][:], pO[:, :, 64:65], 1e-30, None, ALU.max), [kO], [('rl4', x)])
                    self.V(lambda: nc.vector.reciprocal(rl4[x][:], rl4[x][:]), [('rl4', x)], [('rl4', x)])
                    self.V(lambda: nc.vector.tensor_tensor(rl4[x][:], rl4[x][:], gates[:, tt * 4:(tt + 1) * 4, e * 3 + br:e * 3 + br + 1], ALU.mult),
                           [('rl4', x), 'gates'], [('rl4', x)])
                    self.V(lambda: nc.vector.tensor_tensor(ctmp[x][:], pO[:, :, 0:64], rl4[x][:].to_broadcast([128, 4, 64]), ALU.mult),
                           [kO, ('rl4', x)], [('ctmp', x)])
                    self.V(lambda: nc.vector.tensor_tensor(oacc[:, :, hp], oacc[:, :, hp], ctmp[x][:], ALU.add), ['oacc', ('ctmp', x)], ['oacc'])

                def branch(br, e, tt, hh, part, nparts, kbufs):
                    ts = slice(tt * TT, (tt + 1) * TT)
                    hp = slice(hh * 64, (hh + 1) * 64)
                    if br == 1 and part == 0:
                        cmp_scores(hh, tt, kbufs[0])
                        yield
                        for st in range(4):
                            self.mm(psO[0][:, st, 0:65], pT[kbufs[0]][0:127, st * 128:(st + 1) * 128], vcx[0:127, :], [('pT', kbufs[0]), 'vcx'], [('psO', 0)])
                        yield
                        combine(0, e, tt, hp)
                        yield
                    kT_, kTk, vx, vxk = (ksT2, 'ksT2', vsx, 'vsx') if br == 1 else (kwT2, 'kwT2', vwx, 'vwx')
                    pO = psO[br]; kO = ('psO', br)
                    kt0 = 0 if br == 1 else max(0, 4 * tt - 4)
                    kts = list(range(kt0, 4 * tt + 4))[part::nparts]
                    for n_, kt in enumerate(kts):
                        r = kt - 4 * tt
                        k = kbufs[n_ % len(kbufs)]
                        pS = psS[k]; kS = ('psS', k)
                        nmask = (1 if br == 1 else 0) + (1 if (r >= 0 or br == 2) else 0)
                        self.mm(pS[:], kT_[hp, kt * 128:(kt + 1) * 128], qrT16[hp, ts], [kTk, 'qrT16'], [kS], start=True, stop=(nmask == 0), inc=(nmask == 0))
                        if br == 1:
                            nmask -= 1
                            self.mm(pS[:], E16[0:32, kt, :], selT[0:32, ts], ['E16', 'selT'], [kS], start=False, stop=(nmask == 0), inc=(nmask == 0))
                        if r >= 0:
                            self.mm(pS[:], id16[:], Cm16[:, r, :], ['id16', 'Cm16'], [kS], start=False, stop=True, inc=True)
                        elif br == 2:
                            self.mm(pS[:], id16[:], Wm16[:, r + 4, :], ['id16', 'Wm16'], [kS], start=False, stop=True, inc=True)
                        yield
                        self.A(lambda: nc.scalar.activation(pT[k][:], pS[:], AF.Exp, scale=0.125), [kS], [('pT', k)])
                        yield
                        for st in range(4):
                            tmin = tt * TT + st * 128; tmax = tmin + 127
                            if kt * 128 > tmax:
                                continue
                            if br == 2 and kt * 128 + 127 <= tmin - 512:
                                continue
                            self.mm(pO[:, st, 0:65], pT[k][:, st * 128:(st + 1) * 128], vx[:, kt, :], [('pT', k), vxk], [kO],
                                    start=False, stop=False, inc=True)
                        yield

                def head_tile(e, tt, hh):
                    hp = slice(hh * 64, (hh + 1) * 64)
                    for br in (1, 2):
                        self.mm(psO[br][:].rearrange("p a b -> p (a b)"), z16[:, 0:128], z16[:, 0:512], ['z16'], [('psO', br)], start=True, stop=False)
                    self.run_chains([branch(1, e, tt, hh, 0, 2, [0]), branch(1, e, tt, hh, 1, 2, [1]), branch(2, e, tt, hh, 0, 1, [2, 3])])
                    for br in (1, 2):
                        self.mm(psO[br][:].rearrange("p a b -> p (a b)"), z16[:, 0:128], z16[:, 0:512], ['z16'], [('psO', br)], start=False, stop=True)
                        combine(br, e, tt, hp)

                for c4 in range(4):
                    cg = g * 4 + c4
                    rope_proj(qrT16, 'qrT16', cg * 128, cg * 128, False)
                    for tt in range(NT):
                        self.P(lambda: nc.gpsimd.memset(oacc[:], 0.0), [], ['oacc'])
                        for hh in range(2):
                            e = cg * 2 + hh
                            head_tile(e, tt, hh)
                        for st in range(4):
                            s_ = tt * 4 + st
                            self.tr(ps_misc[:, 0:128], oacc[:, st, :], ['oacc'], ['ps_misc'])
                            self.A(lambda: nc.scalar.copy(oT[:, 0, s_ * 128:(s_ + 1) * 128], ps_misc[:, 0:128]), ['ps_misc'], [('oT', tt)])
                    self.out_proj_acc(self.nsa_w_out[i, cg * 128:(cg + 1) * 128, :], oT, [('oT', k_) for k_ in range(4)], 1, ps_proj, cres, wout, 'nwout')
            c.barrier()
            c.alias = {}

    def run_chains(self, gens):
        gens = list(gens)
        while gens:
            for g_ in list(gens):
                try:
                    next(g_)
                except StopIteration:
                    gens.remove(g_)

    def mix_ln(self, layer):
        nc = self.nc
        with ExitStack() as es:
            self._uid = getattr(self, '_uid', 0) + 1
            _u = "_%d" % self._uid
            sb = lambda name, shape, dtype=F32: es.enter_context(nc.sbuf_tensor(name + _u, shape, dtype))
            ps = lambda name, shape, dtype=F32: es.enter_context(nc.psum_tensor(name + _u, shape, dtype))
            scr = {
                'zsq': [sb("ln_zsq%d" % i, [128, TT], BF16) for i in range(2)],
                'zc': [sb("ln_zc%d" % i, [128, TT]) for i in range(2)],
                'mean': sb("ln_mean", [128, TT]), 'rstd': sb("ln_rstd", [128, TT]),
            }
            scr['tmp'] = scr['zc'][0]
            ps_sum = ps("ps_sum", [128, TT])
            ps_sq = ps("ps_sq", [128, TT])
            for tt in range(T // TT):
                self.layer_norm_tile(layer * 3 + 1, tt, ps_sum, ps_sq, scr)
            self.c.barrier()

    def build(self):
        self.prologue()
        for (kind, layer, which) in self.stages:
            if kind == 'ffn':
                self.ffn(layer, which, layer * 3 + (0 if which == 0 else 2))
            elif kind == 'mix':
                if layer % 2 == 0:
                    self.hybrid(layer)
                else:
                    self.nsa(layer)
                self.mix_ln(layer)
        self.epilogue()
        self.es.close()
        return self.nc


def host_inputs(inputs, b):
    m = {}
    m["xT"] = np.ascontiguousarray(inputs["x"][b].T)
    m["ffn_w_in"] = inputs["ffn_w_in"]
    m["ffn_w_out"] = inputs["ffn_w_out"]
    g = inputs["ln_g"].reshape(DEPTH * 3, NKC, 128)
    m["ln_gT"] = np.ascontiguousarray(g.transpose(2, 0, 1).reshape(128, DEPTH * 3 * NKC))
    bb = inputs["ln_b"].reshape(DEPTH * 3, NKC, 128)
    m["ln_bT"] = np.ascontiguousarray(bb.transpose(2, 0, 1).reshape(128, DEPTH * 3 * NKC))
    m["hyb_w_in"] = inputs["hyb_w_in"]
    m["hyb_w_out"] = inputs["hyb_w_out"]
    m["consts"] = make_consts()
    rep = lambda a: np.broadcast_to(a[None], (128,) + a.shape)
    m["gdn_con"] = np.ascontiguousarray(rep(np.concatenate([inputs["gdn_a_log"], inputs["gdn_dt_bias"]], axis=1))).astype(np.float32)
    cw = inputs["gdn_conv_w"].reshape(2, 4, 16, 128)
    m["gdn_cw"] = np.ascontiguousarray(cw.transpose(3, 0, 2, 1))
    m["gdn_nw"] = np.ascontiguousarray(rep(inputs["gdn_norm_w"])).astype(np.float32)
    m["ssd_con"] = np.ascontiguousarray(rep(np.concatenate([inputs["ssd_a_log"], inputs["ssd_dt_bias"], inputs["ssd_d"]], axis=1))).astype(np.float32)
    scw = np.concatenate([inputs["ssd_conv_w"], inputs["ssd_conv_b"][:, None, :]], axis=1).reshape(2, 5, 12, 128)
    m["ssd_cw"] = np.ascontiguousarray(scw.transpose(3, 0, 2, 1))
    m["ssd_nw"] = np.ascontiguousarray(inputs["ssd_norm_w"].reshape(2, 8, 128).transpose(2, 0, 1))
    m["nsa_w_in"] = inputs["nsa_w_in"]
    m["nsa_w_out"] = inputs["nsa_w_out"]
    m["nsa_cmp_w1"] = inputs["nsa_cmp_w1"]
    m["nsa_cmp_w2"] = inputs["nsa_cmp_w2"]
    perm = np.arange(64)
    perm[0:8] = np.arange(8, 16)
    perm[8:16] = np.arange(0, 8)
    wi = inputs["nsa_w_in"]
    qsw = wi[:, :, 0:1024].reshape(2, D, 16, 64)[:, :, :, perm].reshape(2, D, 1024)
    kssw = wi[:, :, 1280:1408].reshape(2, D, 2, 64)[:, :, :, perm].reshape(2, D, 128)
    kwsw = wi[:, :, 1536:1664].reshape(2, D, 2, 64)[:, :, :, perm].reshape(2, D, 128)
    m["nsa_w_sw"] = np.ascontiguousarray(np.concatenate([qsw, kssw, kwsw], axis=2))
    cp = inputs["nsa_cmp_pos"].reshape(2, 2, 16, 2, 64)
    m["nsa_pos2"] = np.ascontiguousarray(cp.transpose(3, 4, 0, 1, 2).reshape(128, 2, 2, 16))
    m["pos_rep"] = np.ascontiguousarray(np.broadcast_to(inputs["positions"][b][None, :], (128, T))).astype(np.int32)
    m.update(nsa_consts())
    return m


_NSA_CONSTS = None


def nsa_consts():
    global _NSA_CONSTS
    if _NSA_CONSTS is not None:
        return _NSA_CONSTS
    NEG = -30000.0
    p = np.arange(128)
    d = p % 64
    half = 8
    invf = np.where(d < 16, 500000.0 ** (-(d % half).astype(np.float64) / half), 0.0).astype(np.float32)
    sgn = np.where(d < 8, -1.0, np.where(d < 16, 1.0, 0.0)).astype(np.float32)
    out = {"rope_c": np.ascontiguousarray(np.stack([invf, sgn], axis=1))}
    t = np.arange(T)
    cmp_end = 16 * np.arange(128) + 31
    out["nsa_Mc"] = np.where(cmp_end[:, None] <= t[None, :], 0.0, NEG).astype(np.float32)
    tl = np.arange(TT)
    cm = np.zeros((128, 4, TT), np.float32)
    wm = np.zeros((128, 4, TT), np.float32)
    for r in range(4):
        cm[:, r, :] = np.where((p[:, None] + 128 * r) <= tl[None, :], 0.0, NEG)
        rr = r - 4
        wm[:, r, :] = np.where((p[:, None] + 128 * rr + 512) > tl[None, :], 0.0, NEG)
    out["nsa_Cm"] = cm
    out["nsa_Wm"] = wm
    E = np.zeros((32, 16, 128), np.float32)
    for kt in range(16):
        for pp in range(128):
            E[2 * kt + pp // 64, kt, pp] = 1.0
    out["nsa_E"] = E
    n_cmp = 127
    c0 = np.arange(n_cmp)[:, None] * 16
    s0 = np.arange(32)[None, :] * 64
    agg = np.clip(np.minimum(c0 + 32, s0 + 64) - np.maximum(c0, s0), 0, None) / 32.0
    aggx = np.zeros((128, 33), np.float32)
    aggx[:n_cmp, :32] = agg
    aggx[:n_cmp, 32] = 1.0
    out["nsa_agg"] = aggx
    tok = (np.arange(16)[None, :, None] * 128 + p[:, None, None])
    j = np.arange(32)[None, None, :]
    cur = tok // 64
    causal = j <= cur
    forced = (j == 0) | (causal & (j > cur - 2))
    M1 = (causal & ~forced).astype(np.float32)
    M2 = np.where(forced, 1e4, np.where(causal, 0.0, -1.0)).astype(np.float32)
    out["nsa_M12"] = np.ascontiguousarray(np.stack([M1.reshape(128, 512), M2.reshape(128, 512)], axis=1))
    _NSA_CONSTS = out
    return out


def make_consts():
    i = np.arange(128)[:, None]
    j = np.arange(128)[None, :]
    same = (i // 64) == (j // 64)
    NEG = -30000.0
    cs = np.zeros((8, 128, 128), np.float32)
    cs[0] = (i == j)
    cs[1] = np.where(same & (j < i), 0.0, NEG)
    cs[2] = np.where(same & (j >= i), 0.0, NEG)
    cs[3] = (i != j)
    cs[4] = (same & (i <= j))
    cs[5] = same
    cs[6] = (i <= j)
    cs[7] = np.where(j >= i, 0.0, NEG)
    return np.ascontiguousarray(cs.transpose(1, 0, 2))


FULL_STAGES = []
for _l in range(DEPTH):
    FULL_STAGES.append(('ffn', _l, 0))
    FULL_STAGES.append(('mix', _l, 0))
    FULL_STAGES.append(('ffn', _l, 1))


def kernel(**inputs):
    inputs = {k: np.asarray(v) for k, v in inputs.items()}
    prog = Prog(FULL_STAGES)
    nc = prog.build()
    B = inputs["x"].shape[0]
    in_maps = [host_inputs(inputs, b) for b in range(B)]
    res = run_bass_kernel_spmd(nc, in_maps, core_ids=list(range(B)))
    out = np.stack([np.ascontiguousarray(r["outT"].T) for r in res.results], axis=0)
    return out.astype(np.float32)
```

```python
import numpy as np
from contextlib import ExitStack
import concourse.bass as bass
import concourse.mybir as mybir
from concourse.bass_utils import run_bass_kernel_spmd

F32 = mybir.dt.float32
BF16 = mybir.dt.bfloat16
I32 = mybir.dt.int32
AF = mybir.ActivationFunctionType
ALU = mybir.AluOpType
AX = mybir.AxisListType

D = 1024
T = 2048
DEPTH = 4
DFF = 2816
ALPHA = (2.0 * DEPTH) ** 0.25
LN_EPS = 1e-5
NKC = D // 128
NFC = DFF // 128
TT = 512

SAME_ENGINE_SYNC = True


class Ctx:
    def __init__(self, nc, es, n_dma_sems=32):
        self.nc = nc
        self.eng = {'pe': nc.tensor, 'act': nc.scalar, 'dve': nc.vector, 'pool': nc.gpsimd, 'sp': nc.sync}
        self.sem = {}
        self.cnt = {}
        for e in ['pe', 'act', 'dve', 'pool']:
            self.sem[e] = es.enter_context(nc.semaphore("s_" + e))
            self.cnt[e] = 0
        self.dma_sems = [es.enter_context(nc.semaphore("s_dma%d" % i)) for i in range(n_dma_sems)]
        self.dma_val = [0] * n_dma_sems
        self.dma_rr = 0
        self.waited = {}
        self.last_w = {}
        self.readers = {}
        self.ninstr = 0
        self.alias = {}

    def _wait(self, e, tok):
        sem, key, val = tok
        k = (e, key)
        if self.waited.get(k, 0) >= val:
            return
        self.eng[e].wait_ge(sem, val)
        self.waited[k] = val

    def _deps(self, e, reads, writes):
        toks = {}

        def add(tok):
            if tok is None:
                return
            sem, key, val = tok
            if key == e and (e == 'pe' or not SAME_ENGINE_SYNC):
                return
            if key not in toks or toks[key][2] < val:
                toks[key] = tok
        for r in reads:
            add(self.last_w.get(r))
        for w in writes:
            add(self.last_w.get(w))
            for tok in self.readers.get(w, {}).values():
                add(tok)
        for tok in toks.values():
            self._wait(e, tok)

    def _record(self, tok, reads, writes):
        for r in reads:
            self.readers.setdefault(r, {})[tok[1]] = tok
        for w in writes:
            self.last_w[w] = tok
            self.readers[w] = {}

    def op(self, e, fn, reads=(), writes=(), inc=True):
        reads = [self.alias.get(k, k) for k in reads]
        writes = [self.alias.get(k, k) for k in writes]
        for k in reads:
            kk = k[0] if isinstance(k, tuple) else k
            if isinstance(kk, str) and kk.startswith('ps') and k not in writes:
                writes.append(k)
        reads = [k for k in reads if k not in writes]
        self._deps(e, reads, writes)
        ins = fn()
        self.ninstr += 1
        tok = (self.sem[e], e, self.cnt[e] + 1)
        self._record(tok, reads, writes)
        if inc:
            ins.then_inc(self.sem[e], 1)
            self.cnt[e] += 1
        return ins

    def dma(self, q, out, in_, reads=(), writes=(), **kw):
        reads = list(reads)
        writes = list(writes)
        i = self.dma_rr
        self.dma_rr = (self.dma_rr + 1) % len(self.dma_sems)
        sem = self.dma_sems[i]
        key = "dma%d" % i
        if self.dma_val[i] > 0:
            self._wait(q, (sem, key, self.dma_val[i]))
        self._deps(q, reads, writes)
        ins = self.eng[q].dma_start(out=out, in_=in_, **kw)
        self.dma_val[i] += 16
        ins.then_inc(sem, 16)
        tok = (sem, key, self.dma_val[i])
        self._record(tok, reads, writes)
        self.ninstr += 1
        return tok

    def barrier(self):
        for e in ['pe', 'act', 'dve', 'pool', 'sp']:
            for e2 in ['pe', 'act', 'dve', 'pool']:
                if e2 != e and self.cnt[e2] > 0:
                    self._wait(e, (self.sem[e2], e2, self.cnt[e2]))
            for i, sem in enumerate(self.dma_sems):
                if self.dma_val[i] > 0:
                    self._wait(e, (sem, "dma%d" % i, self.dma_val[i]))

    def wait_all(self, e, keys):
        for k in keys:
            tok = self.last_w.get(k)
            if tok is not None:
                self._wait(e, tok)


class Prog:
    def __init__(self, stages, dbg=()):
        self.stages = stages
        self.dbg = dbg
        nc = self.nc = bass.Bass("TRN2", target_bir_lowering=False)
        self.es = ExitStack()
        es = self.es
        self.c = Ctx(nc, es)
        dt = lambda name, shape, dtype=F32, kind="ExternalInput": nc.dram_tensor(name, shape, dtype, kind=kind).ap()
        self.xT = dt("xT", [D, T])
        self.ffn_w_in = dt("ffn_w_in", [DEPTH, 2, D, 2 * DFF])
        self.ffn_w_out = dt("ffn_w_out", [DEPTH, 2, DFF, D])
        self.ln_gT = dt("ln_gT", [128, DEPTH * 3 * NKC])
        self.ln_bT = dt("ln_bT", [128, DEPTH * 3 * NKC])
        self.outT = dt("outT", [D, T], kind="ExternalOutput")
        self.hyb_w_in = dt("hyb_w_in", [2, D, 5664])
        self.hyb_w_out = dt("hyb_w_out", [2, 2048, D])
        self.consts_d = dt("consts", [128, 8, 128])
        self.gdn_con_d = dt("gdn_con", [128, 2, 16])
        self.gdn_cw_d = dt("gdn_cw", [128, 2, 16, 4])
        self.gdn_nw_d = dt("gdn_nw", [128, 2, 128])
        self.ssd_con_d = dt("ssd_con", [128, 2, 48])
        self.ssd_cw_d = dt("ssd_cw", [128, 2, 12, 5])
        self.ssd_nw_d = dt("ssd_nw", [128, 2, 8])
        self.nsa_w_in = dt("nsa_w_in", [2, D, 1840])
        self.nsa_w_sw = dt("nsa_w_sw", [2, D, 1280])
        self.nsa_w_out = dt("nsa_w_out", [2, D, D])
        self.nsa_cmp_w1 = dt("nsa_cmp_w1", [2, 2, 2048, 256])
        self.nsa_cmp_w2 = dt("nsa_cmp_w2", [2, 2, 256, 64])
        self.nsa_pos2 = dt("nsa_pos2", [128, 2, 2, 16])
        self.pos_rep = dt("pos_rep", [128, T], I32)
        self.rope_c_d = dt("rope_c", [128, 2])
        self.nsa_Mc = dt("nsa_Mc", [128, T])
        self.nsa_Cm = dt("nsa_Cm", [128, 4, TT])
        self.nsa_Wm = dt("nsa_Wm", [128, 4, TT])
        self.nsa_E = dt("nsa_E", [32, 16, 128])
        self.nsa_agg = dt("nsa_agg", [128, 33])
        self.nsa_M12 = dt("nsa_M12", [128, 2, 512])
        sb = lambda name, shape, dtype=F32: es.enter_context(nc.sbuf_tensor(name, shape, dtype))
        self.sb = sb
        self.h32 = sb("h32", [128, NKC, T])
        self.h16 = sb("h16", [128, NKC, T], BF16)
        self.lng = sb("lng", [128, DEPTH * 3 * NKC])
        self.lnb = sb("lnb", [128, DEPTH * 3 * NKC])
        self.ones32 = sb("ones32", [128, 128])
        self.ones16 = sb("ones16", [128, 128], BF16)
        self.epsc = sb("epsc", [128, 1])
        self.onec = sb("onec", [128, 1])
        self.c1e6 = sb("c1e6", [128, 1])
        self.consts = sb("consts_sb", [128, 8, 128])
        self.ident = self.consts[:, 0, :]
        self.maskL = self.consts[:, 1, :]
        self.maskU = self.consts[:, 2, :]
        self.offdiag = self.consts[:, 3, :]
        self.tri64 = self.consts[:, 4, :]
        self.blk64 = self.consts[:, 5, :]
        self.tri128 = self.consts[:, 6, :]
        self.maskU128 = self.consts[:, 7, :]
        self.gdn_con = sb("gdn_con_sb", [128, 2, 16])
        self.gdn_cw = sb("gdn_cw_sb", [128, 2, 16, 4])
        self.gdn_nw = sb("gdn_nw_sb", [128, 2, 128])
        self.ssd_con = sb("ssd_con_sb", [128, 2, 48])
        self.ssd_cw = sb("ssd_cw_sb", [128, 2, 12, 5])
        self.ssd_nw = sb("ssd_nw_sb", [128, 2, 8])
        self.rope_c = sb("rope_c_sb", [128, 2])

    def kh32(self, c, tt):
        return ("h32", c, tt)

    def kh16(self, c, tt):
        return ("h16", c, tt)

    def prologue(self):
        nc, c = self.nc, self.c
        c.op('pool', lambda: nc.gpsimd.memset(self.ones32[:], 1.0), writes=['ones32'])
        c.op('pool', lambda: nc.gpsimd.memset(self.ones16[:], 1.0), writes=['ones16'])
        c.op('pool', lambda: nc.gpsimd.memset(self.epsc[:], LN_EPS / (ALPHA * ALPHA)), writes=['epsc'])
        c.op('pool', lambda: nc.gpsimd.memset(self.onec[:], 1.0), writes=['onec'])
        c.op('pool', lambda: nc.gpsimd.memset(self.c1e6[:], 1e-6), writes=['c1e6'])
        c.dma('sp', self.consts[:], self.consts_d[:, :, :], writes=['ident', 'maskL', 'maskU', 'offdiag', 'tri64', 'blk64', 'tri128', 'maskU128'])
        c.dma('sp', self.gdn_con[:], self.gdn_con_d[:, :, :], writes=['gdn_con'])
        c.dma('sp', self.gdn_cw[:], self.gdn_cw_d[:, :, :, :], writes=['gdn_cw'])
        c.dma('sp', self.gdn_nw[:], self.gdn_nw_d[:, :, :], writes=['gdn_nw'])
        c.dma('sp', self.ssd_con[:], self.ssd_con_d[:, :, :], writes=['ssd_con'])
        c.dma('sp', self.ssd_cw[:], self.ssd_cw_d[:, :, :, :], writes=['ssd_cw'])
        c.dma('sp', self.ssd_nw[:], self.ssd_nw_d[:, :, :], writes=['ssd_nw'])
        c.dma('sp', self.rope_c[:], self.rope_c_d[:, :], writes=['rope_c'])
        c.dma('sp', self.lng[:], self.ln_gT[:, :], writes=['lng'])
        c.dma('sp', self.lnb[:], self.ln_bT[:, :], writes=['lnb'])
        xv = self.xT.rearrange("(c p) t -> p c t", p=128)
        for kc in range(NKC):
            c.dma('sp' if kc % 2 == 0 else 'act', self.h32[:, kc, :], xv[:, kc, :],
                  writes=[self.kh32(kc, tt) for tt in range(T // TT)])
        for kc in range(NKC):
            for tt in range(T // TT):
                e = 'dve' if (kc + tt) % 2 == 0 else 'pool'
                eng = nc.vector if e == 'dve' else nc.gpsimd
                c.op(e, lambda: eng.tensor_copy(self.h16[:, kc, tt * TT:(tt + 1) * TT], self.h32[:, kc, tt * TT:(tt + 1) * TT]),
                     reads=[self.kh32(kc, tt)], writes=[self.kh16(kc, tt)])

    def epilogue(self):
        nc, c = self.nc, self.c
        ov = self.outT.rearrange("(c p) t -> p c t", p=128)
        keys = []
        for kc in range(NKC):
            k = ("out", kc)
            c.dma('sp', ov[:, kc, :], self.h32[:, kc, :], reads=[self.kh32(kc, tt) for tt in range(T // TT)], writes=[k])
            keys.append(k)
        c.wait_all('sp', keys)

    def layer_norm_tiles(self, li, tts, ps_sum, ps_sq, scr):
        nc, c = self.nc, self.c
        zsq, mean, rstd, tmp = scr['zsq'], scr['mean'], scr['rstd'], scr['tmp']

        def stats(tt):
            ts = slice(tt * TT, (tt + 1) * TT)
            for kc in range(NKC):
                b = kc % 2
                c.op('act', lambda: nc.scalar.activation(zsq[b][:], self.h32[:, kc, ts], AF.Square),
                     reads=[self.kh32(kc, tt)], writes=[('zsq', b)])
                c.op('pe', lambda: nc.tensor.matmul(ps_sum[:], self.ones32[:], self.h32[:, kc, ts], start=(kc == 0), stop=(kc == NKC - 1)),
                     reads=['ones32', self.kh32(kc, tt)], writes=['ps_sum'], inc=False)
                c.op('pe', lambda: nc.tensor.matmul(ps_sq[:], self.ones16[:], zsq[b][:], start=(kc == 0), stop=(kc == NKC - 1)),
                     reads=['ones16', ('zsq', b)], writes=['ps_sq'], inc=True)
                yield

        def small(tt):
            c.op('act', lambda: nc.scalar.activation(mean[:], ps_sum[:], AF.Copy, scale=1.0 / D), reads=['ps_sum'], writes=['mean'])
            c.op('pool', lambda: nc.gpsimd.tensor_tensor(tmp[:], mean[:], mean[:], ALU.mult), reads=['mean'], writes=[('zc', 0)])
            c.op('dve', lambda: nc.vector.scalar_tensor_tensor(rstd[:], ps_sq[:], 1.0 / D, tmp[:], ALU.mult, ALU.subtract),
                 reads=['ps_sq', ('zc', 0)], writes=['rstd'])
            c.op('act', lambda: nc.scalar.activation(rstd[:], rstd[:], AF.Sqrt, bias=self.epsc[:, 0:1], scale=1.0), reads=['rstd', 'epsc'], writes=['rstd'])
            c.op('dve', lambda: nc.vector.reciprocal(rstd[:], rstd[:]), reads=['rstd'], writes=['rstd'])

        def norm(tt):
            ts = slice(tt * TT, (tt + 1) * TT)
            for kc in range(NKC):
                gi = li * NKC + kc
                b = kc % 2
                zc = scr['zc'][b]
                c.op('dve', lambda: nc.vector.tensor_tensor(zc[:], self.h32[:, kc, ts], mean[:], ALU.subtract),
                     reads=[self.kh32(kc, tt), 'mean'], writes=[('zc', b)])
                c.op('dve', lambda: nc.vector.scalar_tensor_tensor(zc[:], zc[:], self.lng[:, gi:gi + 1], rstd[:], ALU.mult, ALU.mult),
                     reads=[('zc', b), 'rstd', 'lng'], writes=[('zc', b)])
                c.op('act', lambda: nc.scalar.activation(self.h32[:, kc, ts], zc[:], AF.Identity, bias=self.lnb[:, gi:gi + 1], scale=1.0),
                     reads=[('zc', b), 'lnb'], writes=[self.kh32(kc, tt)])
                c.op('pool', lambda: nc.gpsimd.tensor_scalar(self.h16[:, kc, ts], zc[:], 1.0, self.lnb[:, gi:gi + 1], ALU.mult, ALU.add),
                     reads=[('zc', b), 'lnb'], writes=[self.kh16(kc, tt)])
                yield

        self.run_chains([stats(tts[0])])
        for n_, tt in enumerate(tts):
            small(tt)
            gens = [norm(tt)]
            if n_ + 1 < len(tts):
                gens.append(stats(tts[n_ + 1]))
            self.run_chains(gens)

    def layer_norm_tile(self, li, tt, ps_sum, ps_sq, scr):
        nc, c = self.nc, self.c
        ts = slice(tt * TT, (tt + 1) * TT)
        zsq, mean, rstd, tmp = scr['zsq'], scr['mean'], scr['rstd'], scr['tmp']
        eps = LN_EPS / (ALPHA * ALPHA)
        for kc in range(NKC):
            b = kc % 2
            c.op('act', lambda: nc.scalar.activation(zsq[b][:], self.h32[:, kc, ts], AF.Square),
                 reads=[self.kh32(kc, tt)], writes=[('zsq', b)])
            c.op('pe', lambda: nc.tensor.matmul(ps_sum[:], self.ones32[:], self.h32[:, kc, ts], start=(kc == 0), stop=(kc == NKC - 1)),
                 reads=['ones32', self.kh32(kc, tt)], writes=['ps_sum'], inc=False)
            c.op('pe', lambda: nc.tensor.matmul(ps_sq[:], self.ones16[:], zsq[b][:], start=(kc == 0), stop=(kc == NKC - 1)),
                 reads=['ones16', ('zsq', b)], writes=['ps_sq'], inc=True)
        c.op('act', lambda: nc.scalar.activation(mean[:], ps_sum[:], AF.Copy, scale=1.0 / D), reads=['ps_sum'], writes=['mean'])
        c.op('pool', lambda: nc.gpsimd.tensor_tensor(tmp[:], mean[:], mean[:], ALU.mult), reads=['mean'], writes=[('zc', 0)])
        c.op('dve', lambda: nc.vector.scalar_tensor_tensor(rstd[:], ps_sq[:], 1.0 / D, tmp[:], ALU.mult, ALU.subtract),
             reads=['ps_sq', ('zc', 0)], writes=['rstd'])
        c.op('act', lambda: nc.scalar.activation(rstd[:], rstd[:], AF.Sqrt, bias=self.epsc[:, 0:1], scale=1.0), reads=['rstd', 'epsc'], writes=['rstd'])
        c.op('dve', lambda: nc.vector.reciprocal(rstd[:], rstd[:]), reads=['rstd'], writes=['rstd'])
        for kc in range(NKC):
            gi = li * NKC + kc
            b = kc % 2
            zc = scr['zc'][b]
            c.op('dve', lambda: nc.vector.tensor_tensor(zc[:], self.h32[:, kc, ts], mean[:], ALU.subtract),
                 reads=[self.kh32(kc, tt), 'mean'], writes=[('zc', b)])
            c.op('dve', lambda: nc.vector.scalar_tensor_tensor(zc[:], zc[:], self.lng[:, gi:gi + 1], rstd[:], ALU.mult, ALU.mult),
                 reads=[('zc', b), 'rstd', 'lng'], writes=[('zc', b)])
            c.op('act', lambda: nc.scalar.activation(self.h32[:, kc, ts], zc[:], AF.Identity, bias=self.lnb[:, gi:gi + 1], scale=1.0),
                 reads=[('zc', b), 'lnb'], writes=[self.kh32(kc, tt)])
            c.op('pool', lambda: nc.gpsimd.tensor_scalar(self.h16[:, kc, ts], zc[:], 1.0, self.lnb[:, gi:gi + 1], ALU.mult, ALU.add),
                 reads=[('zc', b), 'lnb'], writes=[self.kh16(kc, tt)])

    def ffn(self, layer, which, li):
        nc, c = self.nc, self.c
        cres = 0.5 / ALPHA
        with ExitStack() as es:
            self._uid = getattr(self, '_uid', 0) + 1
            _u = "_%d" % self._uid
            sb = lambda name, shape, dtype=F32: es.enter_context(nc.sbuf_tensor(name + _u, shape, dtype))
            ps = lambda name, shape, dtype=F32: es.enter_context(nc.psum_tensor(name + _u, shape, dtype))
            NH = 2
            HT = T // NH
            act = sb("ffn_act", [128, NFC, HT], BF16)
            NWB = 3
            wi = [sb("ffn_wi%d" % i, [128, NKC, 512], BF16) for i in range(NWB)]
            NOB = 2
            wo = [sb("ffn_wo%d" % i, [128, NFC, 256], BF16) for i in range(NOB)]
            sg = [sb("ffn_sg%d" % i, [128, TT], BF16) for i in range(2)]
            scr = {
                'zsq': [sb("ln_zsq%d" % i, [128, TT], BF16) for i in range(2)],
                'zc': [sb("ln_zc%d" % i, [128, TT]) for i in range(2)],
                'mean': sb("ln_mean", [128, TT]), 'rstd': sb("ln_rstd", [128, TT]),
            }
            scr['tmp'] = scr['zc'][0]
            pg = [ps("ps_g%d" % i, [128, TT]) for i in range(2)]
            pu = [ps("ps_u%d" % i, [128, TT]) for i in range(2)]
            po = [ps("ps_o%d" % i, [128, TT]) for i in range(2)]
            ps_sum = ps("ps_sum", [128, TT])
            ps_sq = ps("ps_sq", [128, TT])
            w_in = self.ffn_w_in[layer, which].rearrange("(kc p) n -> p kc n", p=128)
            w_out = self.ffn_w_out[layer, which].rearrange("(fc p) n -> p fc n", p=128)
            NJB = NFC // 2
            cnt = 0
            for half in range(NH):
                for jb in range(NJB):
                    s = (half * NJB + jb) % NWB
                    kw = ('ffn_wi', s)
                    c.dma('pool', wi[s][:, :, 0:256], w_in[:, :, jb * 256:(jb + 1) * 256], writes=[kw])
                    c.dma('pool', wi[s][:, :, 256:512], w_in[:, :, DFF + jb * 256:DFF + (jb + 1) * 256], writes=[kw])
                    for jj in range(2):
                        j = jb * 2 + jj
                        for t2 in range(HT // TT):
                            tt = half * (HT // TT) + t2
                            ts = slice(tt * TT, (tt + 1) * TT)
                            b = cnt % 2
                            cnt += 1
                            for kc in range(NKC):
                                c.op('pe', lambda: nc.tensor.matmul(pg[b][:], wi[s][:, kc, jj * 128:(jj + 1) * 128], self.h16[:, kc, ts],
                                                                    start=(kc == 0), stop=(kc == NKC - 1)),
                                     reads=[kw, self.kh16(kc, tt)], writes=[('pg', b)], inc=False)
                            for kc in range(NKC):
                                c.op('pe', lambda: nc.tensor.matmul(pu[b][:], wi[s][:, kc, 256 + jj * 128:256 + (jj + 1) * 128], self.h16[:, kc, ts],
                                                                    start=(kc == 0), stop=(kc == NKC - 1)),
                                     reads=[kw, self.kh16(kc, tt)], writes=[('pu', b)], inc=(kc == NKC - 1))
                            c.op('act', lambda: nc.scalar.activation(sg[b][:], pg[b][:], AF.Silu), reads=[('pg', b)], writes=[('sg', b)])
                            c.op('dve', lambda: nc.vector.tensor_tensor(act[:, j, t2 * TT:(t2 + 1) * TT], sg[b][:], pu[b][:], ALU.mult),
                                 reads=[('sg', b), ('pu', b)], writes=[('act', j, t2)])
                for db in range(4):
                    s = (half * 4 + db) % NOB
                    kw = ('ffn_wo', s)
                    c.dma('pool', wo[s][:], w_out[:, :, db * 256:(db + 1) * 256], writes=[kw])
                    for dd in range(2):
                        dc = db * 2 + dd
                        for t2 in range(HT // TT):
                            tt = half * (HT // TT) + t2
                            ts = slice(tt * TT, (tt + 1) * TT)
                            b = cnt % 2
                            cnt += 1
                            for fc in range(NFC):
                                c.op('pe', lambda: nc.tensor.matmul(po[b][:], wo[s][:, fc, dd * 128:(dd + 1) * 128], act[:, fc, t2 * TT:(t2 + 1) * TT],
                                                                    start=(fc == 0), stop=(fc == NFC - 1)),
                                     reads=[kw, ('act', fc, t2)], writes=[('po', b)], inc=(fc == NFC - 1))
                            c.op('dve', lambda: nc.vector.scalar_tensor_tensor(self.h32[:, dc, ts], po[b][:], cres, self.h32[:, dc, ts], ALU.mult, ALU.add),
                                 reads=[('po', b), self.kh32(dc, tt)], writes=[self.kh32(dc, tt)])
                self.layer_norm_tiles(li, [half * (HT // TT) + t2 for t2 in range(HT // TT)], ps_sum, ps_sq, scr)
            c.barrier()

    def mm(self, out, lhsT, rhs, r, w, start=True, stop=True, inc=True):
        nc = self.nc
        return self.c.op('pe', lambda: nc.tensor.matmul(out, lhsT, rhs, start=start, stop=stop), reads=r, writes=w, inc=inc)

    def tr(self, out, in_, r, w, inc=True):
        nc = self.nc
        return self.c.op('pe', lambda: nc.tensor.transpose(out, in_, self.ident[:]), reads=list(r) + ['ident'], writes=w, inc=inc)

    def V(self, fn, r, w):
        return self.c.op('dve', fn, reads=r, writes=w)

    def A(self, fn, r, w):
        return self.c.op('act', fn, reads=r, writes=w)

    def P(self, fn, r, w):
        return self.c.op('pool', fn, reads=r, writes=w)

    def load_w(self, dst, src, key, q='pool'):
        self.c.dma(q, dst, src, writes=[key])

    def out_proj_acc(self, wout_rows, oT, okeys, nchunks, ps_proj, scale, wbuf, wkey):
        nc, c = self.nc, self.c
        wv = wout_rows.rearrange("(cc p) n -> p cc n", p=128)
        if 'nodma' in self.dbg:
            self.P(lambda: nc.gpsimd.memset(wbuf[:, 0:nchunks, :], 0.0), [], [wkey])
        else:
            c.dma('pool', wbuf[:, 0:nchunks, :], wv, writes=[wkey])
        i = 0
        for dc in range(NKC):
            for tt in range(T // TT):
                ts = slice(tt * TT, (tt + 1) * TT)
                pp = ps_proj[i % 2]
                pk = ('ps_proj', i % 2)
                i += 1
                for cc in range(nchunks):
                    self.mm(pp[:], wbuf[:, cc, dc * 128:(dc + 1) * 128], oT[:, cc, ts], [wkey] + okeys, [pk],
                            start=(cc == 0), stop=(cc == nchunks - 1), inc=(cc == nchunks - 1))
                self.V(lambda: nc.vector.scalar_tensor_tensor(self.h32[:, dc, ts], pp[:], scale, self.h32[:, dc, ts], ALU.mult, ALU.add),
                       [pk, self.kh32(dc, tt)], [self.kh32(dc, tt)])

    def proj_fm(self, dst, wbuf, wkey, ps_proj, dkey, col0=0):
        nc = self.nc
        for tt in range(T // TT):
            ts = slice(tt * TT, (tt + 1) * TT)
            pp = ps_proj[tt % 2]
            pk = ('ps_proj', tt % 2)
            for kc in range(NKC):
                self.mm(pp[:], wbuf[:, kc, :], self.h16[:, kc, ts], [wkey, self.kh16(kc, tt)], [pk],
                        start=(kc == 0), stop=(kc == NKC - 1), inc=(kc == NKC - 1))
            self.A(lambda: nc.scalar.copy(dst[:, col0 + tt * TT:col0 + (tt + 1) * TT], pp[:]), [pk], [dkey])

    def proj_conv(self, dst, dkey, wbuf, wkey, cw4, bias, cwkey):
        nc = self.nc
        cv = self._cv
        xp16, dgw, ps_proj = cv['xp16'], cv['dgw'], cv['ps_proj']
        self.V(lambda: nc.vector.tensor_tensor(dgw[:], self.ident.unsqueeze(1).to_broadcast([128, 4, 128]),
                                               cw4.unsqueeze(2).to_broadcast([128, 4, 128]), ALU.mult), ['ident', cwkey], ['dgw'])
        for tt in range(T // TT):
            ts = slice(tt * TT, (tt + 1) * TT)
            pp = ps_proj[tt % 2]; pk = ('ps_proj', tt % 2)
            for kc in range(NKC):
                self.mm(pp[:], wbuf[:, kc, :], self.h16[:, kc, ts], [wkey, self.kh16(kc, tt)], [pk],
                        start=(kc == 0), stop=(kc == NKC - 1), inc=(kc == NKC - 1))
            self.A(lambda: nc.scalar.copy(xp16[:, 3 + tt * TT:3 + (tt + 1) * TT], pp[:]), [pk], [('xp', tt)])
            pc, pck = cv['psc'][tt % 2]
            rd = [('xp', tt)] + ([('xp', tt - 1)] if tt > 0 else [])
            for j in range(4):
                self.mm(pc[:], dgw[:, j, :], xp16[:, tt * TT + j:tt * TT + j + TT], ['dgw'] + rd, [pck], start=(j == 0), stop=(j == 3), inc=(j == 3))
            if bias is None:
                self.A(lambda: nc.scalar.activation(dst[:, ts], pc[:], AF.Silu), [pck], [dkey])
            else:
                self.A(lambda: nc.scalar.activation(dst[:, ts], pc[:], AF.Silu, bias=bias, scale=1.0), [pck, cwkey], [dkey])

    def conv_silu(self, dst, xp, cw, acc, r, w, bias=None):
        nc = self.nc
        self.V(lambda: nc.vector.tensor_scalar(acc[:], xp[:, 3:3 + T], cw[:, 3:4], None, ALU.mult), r, ['convacc'])
        for j in (2, 1, 0):
            self.V(lambda: nc.vector.scalar_tensor_tensor(acc[:], xp[:, j:j + T], cw[:, j:j + 1], acc[:], ALU.mult, ALU.add), r + ['convacc'], ['convacc'])
        if bias is None:
            self.A(lambda: nc.scalar.activation(dst, acc[:], AF.Silu), ['convacc'], w)
        else:
            self.A(lambda: nc.scalar.activation(dst, acc[:], AF.Silu, bias=bias, scale=1.0), ['convacc'] + r, w)

    def hybrid(self, layer):
        nc, c = self.nc, self.c
        i = layer // 2
        W = self.hyb_w_in[i]
        Wv = W.rearrange("(kc p) n -> p kc n", p=128)
        cres = 1.0 / ALPHA
        NS = T // 128
        with ExitStack() as es:
            self._uid = getattr(self, '_uid', 0) + 1
            _u = "_%d" % self._uid
            sb = lambda name, shape, dtype=F32: es.enter_context(nc.sbuf_tensor(name + _u, shape, dtype))
            ps = lambda name, shape, dtype=F32: es.enter_context(nc.psum_tensor(name + _u, shape, dtype))
            banks = [ps("psb%d" % k, [128, 4, 128]) for k in range(8)]
            flat = lambda t_: t_[:].rearrange("p a b -> p (a b)")
            ps_proj = [flat(banks[0]), flat(banks[1])]
            ps_misc = flat(banks[2])
            psq = banks[3:8]
            c.alias = {('ps_proj', 0): ('psb', 0), ('ps_proj', 1): ('psb', 1), 'ps_misc': ('psb', 2)}
            for k in range(5):
                c.alias[('psq', k)] = ('psb', 3 + k)
            wb = [sb("hw%d" % k, [128, NKC, 128], BF16) for k in range(4)]
            xp = sb("xp", [128, 3 + T])
            acc = sb("convacc", [128, 1024])
            self.P(lambda: nc.gpsimd.memset(xp[:, 0:3], 0.0), [], [('xp', 0)])
            dgw = sb("dgw", [128, 4, 128], BF16)
            self._cv = {'xp16': xp[:].bitcast(BF16), 'dgw': dgw, 'ps_proj': ps_proj,
                        'psc': [(flat(banks[3]), ('psq', 0)), (flat(banks[4]), ('psq', 1))]}
            wba = sb("wba", [128, NKC, 16], BF16)
            self.load_w(wba[:], Wv[:, :, 3072:3088], 'wba')
            ba = sb("ba_tok", [128, NS, 16])
            pm = ps_misc[:, 0:NS * 16].rearrange("p (s n) -> p s n", n=16)
            for s in range(NS):
                for kc in range(NKC):
                    self.mm(pm[:, s, :], self.h16[:, kc, s * 128:(s + 1) * 128], wba[:, kc, :], ['wba', self.kh16(kc, s // 4)], ['ps_misc'],
                            start=(kc == 0), stop=(kc == NKC - 1), inc=(kc == NKC - 1))
            self.A(lambda: nc.scalar.copy(ba[:], pm), ['ps_misc'], ['ba'])
            beta = sb("beta", [128, NS, 8])
            gg = sb("g_tok", [128, NS, 8])
            gc = sb("gc_tok", [128, NS, 8])
            ebg = sb("ebg", [128, NS, 8])
            ekd = sb("ekd", [128, NS, 8])
            ngc = sb("ngc", [128, NS, 8])
            tmpa = sb("tmpa", [128, NS, 8])
            gcon = self.gdn_con[:, i, :]
            alog = gcon[:, 0:8].unsqueeze(1).to_broadcast([128, NS, 8])
            dtb = gcon[:, 8:16].unsqueeze(1).to_broadcast([128, NS, 8])
            self.A(lambda: nc.scalar.activation(beta[:], ba[:, :, 0:8], AF.Exp, scale=-1.0), ['ba'], ['beta'])
            self.V(lambda: nc.vector.tensor_scalar(beta[:], beta[:], 1.0, None, ALU.add), ['beta'], ['beta'])
            self.V(lambda: nc.vector.reciprocal(beta[:], beta[:]), ['beta'], ['beta'])
            self.V(lambda: nc.vector.tensor_tensor(gg[:], ba[:, :, 8:16], dtb, ALU.add), ['ba', 'gdn_con'], ['gg'])
            self.A(lambda: nc.scalar.activation(gg[:], gg[:], AF.Exp), ['gg'], ['gg'])
            self.A(lambda: nc.scalar.activation(gg[:], gg[:], AF.Ln, bias=self.onec[:, 0:1], scale=1.0), ['gg', 'onec'], ['gg'])
            self.A(lambda: nc.scalar.activation(tmpa[:], alog, AF.Exp), ['gdn_con'], ['tmpa'])
            self.V(lambda: nc.vector.scalar_tensor_tensor(gg[:], gg[:], -1.0, tmpa[:], ALU.mult, ALU.mult), ['gg', 'tmpa'], ['gg'])
            g2 = gg[:].rearrange("p s n -> p (s n)")
            pm2 = ps_misc[:, 0:NS * 8]
            self.mm(pm2, self.tri64[:], g2, ['tri64', 'gg'], ['ps_misc'])
            self.A(lambda: nc.scalar.copy(gc[:].rearrange("p s n -> p (s n)"), pm2), ['ps_misc'], ['gc'])
            self.mm(pm2, self.blk64[:], g2, ['blk64', 'gg'], ['ps_misc'])
            self.V(lambda: nc.vector.tensor_tensor(ekd[:].rearrange("p s n -> p (s n)"), pm2, gc[:].rearrange("p s n -> p (s n)"), ALU.subtract),
                   ['ps_misc', 'gc'], ['ekd'])
            self.A(lambda: nc.scalar.activation(ekd[:], ekd[:], AF.Exp), ['ekd'], ['ekd'])
            self.A(lambda: nc.scalar.activation(ebg[:], gc[:], AF.Exp), ['gc'], ['ebg'])
            self.V(lambda: nc.vector.tensor_tensor(ebg[:], ebg[:], beta[:], ALU.mult), ['ebg', 'beta'], ['ebg'])
            self.V(lambda: nc.vector.tensor_scalar(ngc[:], gc[:], -1.0, None, ALU.mult), ['gc'], ['ngc'])
            with ExitStack() as es2:
                sb2 = lambda name, shape, dtype=F32: es2.enter_context(nc.sbuf_tensor(name + _u, shape, dtype))
                qT = sb2("qT", [128, T]); kT = sb2("kT", [128, T])
                vTs = [sb2("vT%d" % x, [128, T]) for x in range(2)]
                oTg = sb2("goT", [128, 2, T], BF16); woutg = sb2("gwout", [128, 2, D], BF16)
                sq = acc[:, 0:TT]; rinv = acc[:, TT:2 * TT]
                names = ["dg", "t1", "Dl", "Du", "Amat", "ATm", "TTa", "TTb", "Pa", "PTa", "Pb", "PTb", "u", "eRB", "og", "sz"]
                hnames = ["attnT", "kbe", "vb", "kdec", "wTe", "wTo", "qdTe", "qdTo", "vnew", "TT16"]
                Bs = [{n: sb2("g%d_" % x + n, [128, 128]) for n in names} for x in range(2)]
                for x in range(2):
                    for n in hnames:
                        Bs[x][n] = sb2("g%d_" % x + n, [128, 128], BF16)
                Ss = [[sb2("g%d_S%d" % (x, k), [128, 128]) for k in range(3)] for x in range(2)]
                Shs = [[sb2("g%d_Sh%d" % (x, k), [128, 128], BF16) for k in range(3)] for x in range(2)]
                KKs = sb2("g_KKs", [128, 128]); KQs = sb2("g_KQs", [128, 128])
                cols = [{n: sb2("g%d_c" % x + n, [128, 1]) for n in ["ss", "ri"]} for x in range(2)]
                for x in range(2):
                    for n in ["wTe", "wTo", "qdTe", "qdTo"]:
                        self.P(lambda: nc.gpsimd.memset(Bs[x][n][:], 0.0), [], [(n, x)])
                cw = self.gdn_cw[:, i]

                def chain(x, hv):
                    B_ = Bs[x]; Sst = Ss[x]; Sh = Shs[x]; col = cols[x]; vT = vTs[x]; wz = wb[x]; wzk = 'wb%d' % x
                    bb = 4 * x
                    K = lambda n: (n, x)
                    def slot(n):
                        if n >= 16:
                            return banks[bb][:, n - 16, :], ('psb', bb)
                        return banks[bb + n // 4][:, n % 4, :], ('psb', bb + n // 4)
                    self.P(lambda: nc.gpsimd.memset(Sst[0][:], 0.0), [], [('S', x, 0)])
                    self.P(lambda: nc.gpsimd.memset(Sh[0][:], 0.0), [], [('Sh', x, 0)])
                    yield
                    sidx = 0
                    for s in range(NS):
                        sp_ = slice(s * 128, (s + 1) * 128)
                        gcc = gc[:, s, hv:hv + 1]; ngcc = ngc[:, s, hv:hv + 1]; betc = beta[:, s, hv:hv + 1]
                        ebgc = ebg[:, s, hv:hv + 1]; ekdc = ekd[:, s, hv:hv + 1]
                        sc_keys = ['gc', 'ngc', 'beta', 'ebg', 'ekd']
                        pKK, kKK = slot(0); pKQ, kKQ = slot(1); pRB, kRB = slot(2); pT1, kT1 = slot(3)
                        if x == 0:
                            self.mm(pKK, kT[:, sp_], kT[:, sp_], ['kT'], [kKK])
                            self.mm(pKQ, kT[:, sp_], qT[:, sp_], ['kT', 'qT'], [kKQ])
                            self.A(lambda: nc.scalar.copy(KKs[:], pKK), [kKK], ['KKs'])
                            self.A(lambda: nc.scalar.copy(KQs[:], pKQ), [kKQ], ['KQs'])
                        self.V(lambda: nc.vector.tensor_scalar(B_['dg'][:], self.ident[:], gcc, None, ALU.mult), ['ident'] + sc_keys, [K('dg')])
                        yield
                        self.mm(pRB, self.ones32[:], B_['dg'][:], ['ones32', K('dg')], [kRB])
                        yield
                        self.V(lambda: nc.vector.scalar_tensor_tensor(B_['t1'][:], pRB, -1.0, self.maskL[:], ALU.mult, ALU.add), [kRB, 'maskL'], [K('t1')])
                        yield
                        self.A(lambda: nc.scalar.activation(B_['Dl'][:], B_['t1'][:], AF.Exp, bias=gcc, scale=1.0), [K('t1')] + sc_keys, [K('Dl')])
                        yield
                        self.V(lambda: nc.vector.scalar_tensor_tensor(B_['t1'][:], pRB, 1.0, self.maskU[:], ALU.mult, ALU.add), [kRB, 'maskU'], [K('t1')])
                        yield
                        self.A(lambda: nc.scalar.activation(B_['Du'][:], B_['t1'][:], AF.Exp, bias=ngcc, scale=1.0), [K('t1')] + sc_keys, [K('Du')])
                        self.A(lambda: nc.scalar.activation(B_['eRB'][:], pRB, AF.Exp), [kRB], [K('eRB')])
                        yield
                        self.V(lambda: nc.vector.scalar_tensor_tensor(B_['Amat'][:], KKs[:], betc, B_['Dl'][:], ALU.mult, ALU.mult), ['KKs', K('Dl')] + sc_keys, [K('Amat')])
                        yield
                        self.V(lambda: nc.vector.tensor_tensor(B_['attnT'][:], KQs[:], B_['Du'][:], ALU.mult), ['KQs', K('Du')], [K('attnT')])
                        self.tr(pT1, B_['Amat'][:], [K('Amat')], [kT1])
                        yield
                        self.A(lambda: nc.scalar.copy(B_['ATm'][:], pT1), [kT1], [K('ATm')])
                        self.V(lambda: nc.vector.tensor_tensor(B_['TTa'][:], self.ident[:], pT1, ALU.subtract), ['ident', kT1], [K('TTa')])
                        yield
                        Pc, PTc, TTc = 'Amat', 'ATm', 'TTa'
                        for lvl in range(5):
                            Pn = 'Pa' if lvl % 2 == 0 else 'Pb'
                            PTn = 'PTa' if lvl % 2 == 0 else 'PTb'
                            TTn = 'TTb' if TTc == 'TTa' else 'TTa'
                            p1, k1 = slot(4 + (lvl % 2) * 3); p2, k2 = slot(5 + (lvl % 2) * 3); p3, k3 = slot(6 + (lvl % 2) * 3)
                            self.mm(p1, B_[PTc][:], B_[Pc][:], [K(PTc), K(Pc)], [k1])
                            if lvl < 4:
                                self.mm(p2, B_[Pc][:], B_[PTc][:], [K(PTc), K(Pc)], [k2])
                            yield
                            self.A(lambda: nc.scalar.copy(B_[Pn][:], p1), [k1], [K(Pn)])
                            if lvl < 4:
                                self.V(lambda: nc.vector.tensor_copy(B_[PTn][:], p2), [k2], [K(PTn)])
                            yield
                            self.mm(p3, B_[Pn][:], B_[TTc][:], [K(Pn), K(TTc)], [k3])
                            yield
                            self.V(lambda: nc.vector.tensor_tensor(B_[TTn][:], p3, B_[TTc][:], ALU.add), [k3, K(TTc)], [K(TTn)])
                            yield
                            Pc, PTc, TTc = Pn, PTn, TTn
                        pk_, kk_ = slot(10); pv_, kv_ = slot(11)
                        self.tr(pk_, kT[:, sp_], ['kT'], [kk_])
                        self.tr(pv_, vT[:, sp_], [('vT', x)], [kv_])
                        yield
                        self.A(lambda: nc.scalar.activation(B_['kbe'][:], pk_, AF.Identity, scale=ebgc), [kk_] + sc_keys, [K('kbe')])
                        self.A(lambda: nc.scalar.activation(B_['kdec'][:], pk_, AF.Identity, scale=ekdc), [kk_] + sc_keys, [K('kdec')])
                        self.V(lambda: nc.vector.tensor_scalar(B_['vb'][:], pv_, betc, None, ALU.mult), [kv_] + sc_keys, [K('vb')])
                        yield
                        pu_, ku_ = slot(12); pw_, kw_ = slot(13)
                        self.A(lambda: nc.scalar.copy(B_['TT16'][:], B_[TTc][:]), [K(TTc)], [K('TT16')])
                        yield
                        self.mm(pu_, B_['TT16'][:], B_['vb'][:], [K('TT16'), K('vb')], [ku_])
                        self.mm(pw_, B_['kbe'][:], B_['TT16'][:], [K('TT16'), K('kbe')], [kw_])
                        yield
                        self.A(lambda: nc.scalar.copy(B_['u'][:], pu_), [ku_], [K('u')])
                        self.V(lambda: nc.vector.tensor_copy(B_['wTe'][:, 0:64], pw_[:, 0:64]), [kw_], [K('wTe')])
                        self.V(lambda: nc.vector.tensor_copy(B_['wTo'][:, 64:128], pw_[:, 64:128]), [kw_], [K('wTo')])
                        yield
                        self.V(lambda: nc.vector.tensor_tensor(B_['qdTe'][:, 0:64], qT[:, s * 128:s * 128 + 64], B_['eRB'][:, 0:64], ALU.mult), ['qT', K('eRB')], [K('qdTe')])
                        self.V(lambda: nc.vector.tensor_tensor(B_['qdTo'][:, 64:128], qT[:, s * 128 + 64:(s + 1) * 128], B_['eRB'][:, 64:128], ALU.mult), ['qT', K('eRB')], [K('qdTo')])
                        yield
                        S0 = Sst[sidx % 3]; S1 = Sst[(sidx + 1) % 3]; S2 = Sst[(sidx + 2) % 3]
                        kS0 = ('S', x, sidx % 3); kS1 = ('S', x, (sidx + 1) % 3); kS2 = ('S', x, (sidx + 2) % 3)
                        H0 = Sh[sidx % 3]; H1 = Sh[(sidx + 1) % 3]; H2 = Sh[(sidx + 2) % 3]
                        kH0 = ('Sh', x, sidx % 3); kH1 = ('Sh', x, (sidx + 1) % 3); kH2 = ('Sh', x, (sidx + 2) % 3)
                        pa, ka = slot(14); pb, kb = slot(15); pc_, kc_ = slot(16); pd, kd = slot(17); po_, ko_ = slot(18)
                        self.mm(pa, B_['wTe'][:], H0[:], [K('wTe'), kH0], [ka])
                        yield
                        self.V(lambda: nc.vector.tensor_tensor(B_['vnew'][0:64, :], B_['u'][0:64, :], pa[0:64, :], ALU.subtract), [K('u'), ka], [K('vnew')])
                        yield
                        self.mm(pb, B_['kdec'][0:64, :], B_['vnew'][0:64, :], [K('kdec'), K('vnew')], [kb])
                        yield
                        self.V(lambda: nc.vector.scalar_tensor_tensor(S1[:], S0[:], B_['eRB'][:, 63:64], pb, ALU.mult, ALU.add), [kS0, K('eRB'), kb], [kS1])
                        self.A(lambda: nc.scalar.copy(H1[:], S1[:]), [kS1], [kH1])
                        yield
                        self.mm(pc_, B_['wTo'][:], H1[:], [K('wTo'), kH1], [kc_])
                        yield
                        self.V(lambda: nc.vector.tensor_tensor(B_['vnew'][64:128, :], B_['u'][64:128, :], pc_[64:128, :], ALU.subtract), [K('u'), kc_], [K('vnew')])
                        yield
                        self.mm(pd, B_['kdec'][64:128, :], B_['vnew'][64:128, :], [K('kdec'), K('vnew')], [kd])
                        yield
                        self.V(lambda: nc.vector.scalar_tensor_tensor(S2[:], S1[:], B_['eRB'][:, 127:128], pd, ALU.mult, ALU.add), [kS1, K('eRB'), kd], [kS2])
                        self.A(lambda: nc.scalar.copy(H2[:], S2[:]), [kS2], [kH2])
                        self.mm(po_, B_['qdTe'][:], H0[:], [K('qdTe'), kH0], [ko_], start=True, stop=False, inc=False)
                        self.mm(po_, B_['qdTo'][:], H1[:], [K('qdTo'), kH1], [ko_], start=False, stop=False, inc=False)
                        self.mm(po_, B_['attnT'][:], B_['vnew'][:], [K('attnT'), K('vnew')], [ko_], start=False, stop=True, inc=True)
                        sidx += 2
                        yield
                        pz, kz = slot(19)
                        for kc in range(NKC):
                            self.mm(pz, self.h16[:, kc, sp_], wz[:, kc, :], [wzk, self.kh16(kc, s // 4)], [kz], start=(kc == 0), stop=(kc == NKC - 1), inc=(kc == NKC - 1))
                        yield
                        self.A(lambda: nc.scalar.activation(B_['sz'][:], pz, AF.Silu), [kz], [K('sz')])
                        self.A(lambda: nc.scalar.activation(B_['og'][:], po_, AF.Square, accum_out=col['ss'][:]), [ko_], [K('og'), K('ss')])
                        yield
                        self.A(lambda: nc.scalar.activation(col['ri'][:], col['ss'][:], AF.Sqrt, bias=self.c1e6[:, 0:1], scale=1.0 / 128), [K('ss'), 'c1e6'], [K('ri')])
                        yield
                        self.V(lambda: nc.vector.reciprocal(col['ri'][:], col['ri'][:]), [K('ri')], [K('ri')])
                        self.V(lambda: nc.vector.scalar_tensor_tensor(B_['og'][:], po_, col['ri'][:, 0:1], self.gdn_nw[:, i, :], ALU.mult, ALU.mult), [ko_, K('ri'), 'gdn_nw'], [K('og')])
                        self.V(lambda: nc.vector.tensor_tensor(B_['og'][:], B_['og'][:], B_['sz'][:], ALU.mult), [K('og'), K('sz')], [K('og')])
                        yield
                        self.tr(pz, B_['og'][:], [K('og')], [kz])
                        yield
                        self.A(lambda: nc.scalar.copy(oTg[:, x, sp_], pz), [kz], [('goT', x, s // 4)])
                        yield

                for hq in range(4):
                    hvs = (2 * hq, 2 * hq + 1)
                    self.load_w(wb[0][:], Wv[:, :, hq * 128:(hq + 1) * 128], 'wb0')
                    self.load_w(wb[1][:], Wv[:, :, 512 + hq * 128:512 + (hq + 1) * 128], 'wb1')
                    self.load_w(wb[2][:], Wv[:, :, 1024 + hvs[0] * 128:1024 + (hvs[0] + 1) * 128], 'wb2')
                    self.load_w(wb[3][:], Wv[:, :, 1024 + hvs[1] * 128:1024 + (hvs[1] + 1) * 128], 'wb3')
                    for (dst, dk_, wi_, cchunk, l2, sc) in ((qT, 'qT', 0, hq, True, 128.0 ** -0.5), (kT, 'kT', 1, 4 + hq, True, 1.0),
                                                            (vTs[0], ('vT', 0), 2, 8 + hvs[0], False, 1.0), (vTs[1], ('vT', 1), 3, 8 + hvs[1], False, 1.0)):
                        self.proj_conv(dst, dk_, wb[wi_], 'wb%d' % wi_, cw[:, cchunk, 0:4], None, 'gdn_cw')
                        if l2:
                            for tt in range(T // TT):
                                ts = slice(tt * TT, (tt + 1) * TT)
                                self.A(lambda: nc.scalar.activation(sq, dst[:, ts], AF.Square), [dk_], ['convacc'])
                                self.mm(ps_misc[:], self.ones32[:], sq, ['ones32', 'convacc'], ['ps_misc'])
                                self.A(lambda: nc.scalar.activation(rinv, ps_misc[:], AF.Sqrt, bias=self.c1e6[:, 0:1], scale=1.0), ['ps_misc', 'c1e6'], ['convacc'])
                                self.V(lambda: nc.vector.reciprocal(rinv, rinv), ['convacc'], ['convacc'])
                                self.V(lambda: nc.vector.scalar_tensor_tensor(dst[:, ts], dst[:, ts], sc, rinv, ALU.mult, ALU.mult), [dk_, 'convacc'], [dk_])
                    self.load_w(wb[0][:], Wv[:, :, 2048 + hvs[0] * 128:2048 + (hvs[0] + 1) * 128], 'wb0')
                    self.load_w(wb[1][:], Wv[:, :, 2048 + hvs[1] * 128:2048 + (hvs[1] + 1) * 128], 'wb1')
                    gens = [chain(0, hvs[0]), chain(1, hvs[1])]
                    while gens:
                        for g_ in list(gens):
                            try:
                                next(g_)
                            except StopIteration:
                                gens.remove(g_)
                    self.out_proj_acc(self.hyb_w_out[i, hvs[0] * 128:(hvs[0] + 2) * 128, :], oTg,
                                      [('goT', x, k) for x in range(2) for k in range(4)], 2, ps_proj, cres, woutg, 'gwout')
            c.barrier()
            wout = sb("hwout", [128, 4, D], BF16)
            oT = sb("hoT", [128, 4, T], BF16)
            self._banks1 = banks[1]
            self.ssd(layer, es, ps_proj, ps_misc, psq, wb, wout, oT, xp, acc, Wv)
            c.barrier()
            c.alias = {}

    def ssd(self, layer, es, ps_proj, ps_misc, psq, wb, wout, oT, xp, acc, Wv):
        nc, c = self.nc, self.c
        i = layer // 2
        NS = T // 128
        cres = 1.0 / ALPHA
        def slot(n):
            return psq[n // 4][:, n % 4, :], ('psq', n // 4)
        banks1 = self._banks1
        _u = "_s%d" % layer
        sb = lambda name, shape, dtype=F32: es.enter_context(nc.sbuf_tensor(name + _u, shape, dtype))
        wdt = sb("wdt", [128, NKC, 16], BF16)
        self.load_w(wdt[:], Wv[:, :, 5648:5664], 'wdt')
        dt_ = sb("dt_tok", [128, NS, 16]); acs = sb("acs", [128, NS, 16]); nacs = sb("nacs", [128, NS, 16])
        edl = sb("edl", [128, NS, 16]); adt = sb("adt", [128, NS, 16]); ea = sb("ea", [128, NS, 16])
        pm = ps_misc[:, 0:NS * 16].rearrange("p (s n) -> p s n", n=16)
        for s in range(NS):
            for kc in range(NKC):
                self.mm(pm[:, s, :], self.h16[:, kc, s * 128:(s + 1) * 128], wdt[:, kc, :], ['wdt', self.kh16(kc, s // 4)], ['ps_misc'],
                        start=(kc == 0), stop=(kc == NKC - 1), inc=(kc == NKC - 1))
        scon = self.ssd_con[:, i, :]
        alog = scon[:, 0:16].unsqueeze(1).to_broadcast([128, NS, 16])
        dtb = scon[:, 16:32].unsqueeze(1).to_broadcast([128, NS, 16])
        self.V(lambda: nc.vector.tensor_tensor(dt_[:], pm, dtb, ALU.add), ['ps_misc', 'ssd_con'], ['dt'])
        self.A(lambda: nc.scalar.activation(dt_[:], dt_[:], AF.Exp), ['dt'], ['dt'])
        self.A(lambda: nc.scalar.activation(dt_[:], dt_[:], AF.Ln, bias=self.onec[:, 0:1], scale=1.0), ['dt', 'onec'], ['dt'])
        self.A(lambda: nc.scalar.activation(ea[:], alog, AF.Exp), ['ssd_con'], ['ea'])
        self.V(lambda: nc.vector.scalar_tensor_tensor(adt[:], dt_[:], -1.0, ea[:], ALU.mult, ALU.mult), ['dt', 'ea'], ['adt'])
        f2 = lambda t_: t_[:].rearrange("p s n -> p (s n)")
        pm2 = ps_misc[:, 0:NS * 16]
        self.mm(pm2, self.tri128[:], f2(adt), ['tri128', 'adt'], ['ps_misc'])
        self.A(lambda: nc.scalar.copy(f2(acs), pm2), ['ps_misc'], ['acs'])
        self.mm(pm2, self.ones32[:], f2(adt), ['ones32', 'adt'], ['ps_misc'])
        self.V(lambda: nc.vector.tensor_tensor(f2(edl), pm2, f2(acs), ALU.subtract), ['ps_misc', 'acs'], ['edl'])
        self.A(lambda: nc.scalar.activation(edl[:], edl[:], AF.Exp), ['edl'], ['edl'])
        self.V(lambda: nc.vector.tensor_scalar(nacs[:], acs[:], -1.0, None, ALU.mult), ['acs'], ['nacs'])
        sck = ['dt', 'acs', 'nacs', 'edl']
        xT = sb("xT16", [128, 4, T], BF16)
        BT = sb("BT", [128, T]); CT = sb("CT", [128, T])
        names = ["dg", "t1", "Du", "MT", "eRB", "CdT"]
        Bs = [{n: sb("s%d_" % x + n, [128, 128]) for n in names} for x in range(2)]
        for x in (2, 3):
            o = 128 + (x - 2) * 896
            Bs.append({n: xp[:, o + k_ * 128:o + (k_ + 1) * 128] for k_, n in enumerate(names)})
        Btok = sb("s_Btok", [128, 128]); xc32 = sb("s_xc32", [128, 128]); CBs = sb("s_CBs", [128, 128])
        xtok = acc[:, 512:1024]; szb = acc[:, 0:512]
        ygrp = sb("s_ygrp", [128, 512])
        xdts = [sb("s_xdt%d" % x, [128, 64]) for x in range(2)]; xdtds = [sb("s_xdtd%d" % x, [128, 64]) for x in range(2)]
        for x in (2, 3):
            o = 128 + (x - 2) * 896 + 768
            xdts.append(xp[:, o:o + 64]); xdtds.append(xp[:, o + 64:o + 128])
        chain_banks = [(psq[2], ('psq', 2)), (psq[3], ('psq', 3)), (psq[4], ('psq', 4)), (banks1, ('ps_proj', 1))]
        ST = sb("s_ST", [128, 8, 64])
        col = {n: sb("s_c" + n, [128, 1]) for n in ["ss", "ri"]}
        wz = wout[:].rearrange("p a b -> p (a b)").rearrange("p (k n) -> p k n", n=512)
        cw = self.ssd_cw[:, i]
        for g in range(2):
            for (dst, dk_, col0, cchunk) in ((BT, 'BT', 5136 + g * 128, 8 + g), (CT, 'CT', 5392 + g * 128, 10 + g)):
                self.load_w(wb[0][:], Wv[:, :, col0:col0 + 128], 'wb0')
                self.proj_conv(dst, dk_, wb[0], 'wb0', cw[:, cchunk, 0:4], cw[:, cchunk, 4:5], 'ssd_cw')
            for cc in range(4):
                k = cc % 2
                col0 = 4112 + (g * 4 + cc) * 128
                self.load_w(wb[1 + k][:], Wv[:, :, col0:col0 + 128], 'wb%d' % (1 + k))
                self.proj_conv(xT[:, cc, :], ('xT', cc), wb[1 + k], 'wb%d' % (1 + k), cw[:, g * 4 + cc, 0:4], cw[:, g * 4 + cc, 4:5], 'ssd_cw')
            c.dma('pool', wz, Wv[:, :, 3088 + g * 512:3088 + (g + 1) * 512], writes=['hwout'])
            self.P(lambda: nc.gpsimd.memset(ST[:], 0.0), [], [('ST', e_) for e_ in range(8)])
            c.barrier()

            def hchain(x, s, heads):
                B_ = Bs[x]; xdt = xdts[x]; xdtd = xdtds[x]
                K = lambda n: (n, x)
                sp_ = slice(s * 128, (s + 1) * 128)
                bank, kb_ = chain_banks[x]
                pRB = bank[:, 0, :]; py = bank[:, 1, :]; pS = bank[:, 2, :]
                for e in heads:
                    h = g * 8 + e
                    hs = slice(e * 64, (e + 1) * 64)
                    acc_ = acs[:, s, h:h + 1]; nacc = nacs[:, s, h:h + 1]; dtc = dt_[:, s, h:h + 1]; edc = edl[:, s, h:h + 1]
                    self.V(lambda: nc.vector.tensor_scalar(B_['dg'][:], self.ident[:], acc_, None, ALU.mult), ['ident'] + sck, [K('dg')])
                    self.V(lambda: nc.vector.tensor_scalar(xdt[:], xtok[:, hs], dtc, None, ALU.mult), ['xtok'] + sck, [K('xdt')])
                    yield
                    self.mm(pRB, self.ones32[:], B_['dg'][:], ['ones32', K('dg')], [kb_])
                    self.V(lambda: nc.vector.tensor_scalar(xdtd[:], xdt[:], edc, None, ALU.mult), [K('xdt')] + sck, [K('xdtd')])
                    yield
                    self.V(lambda: nc.vector.scalar_tensor_tensor(B_['t1'][:], pRB, 1.0, self.maskU128[:], ALU.mult, ALU.add), [kb_, 'maskU128'], [K('t1')])
                    self.A(lambda: nc.scalar.activation(B_['eRB'][:], pRB, AF.Exp), [kb_], [K('eRB')])
                    yield
                    self.A(lambda: nc.scalar.activation(B_['Du'][:], B_['t1'][:], AF.Exp, bias=nacc, scale=1.0), [K('t1')] + sck, [K('Du')])
                    self.V(lambda: nc.vector.tensor_tensor(B_['CdT'][:], CT[:, sp_], B_['eRB'][:], ALU.mult), ['CT', K('eRB')], [K('CdT')])
                    yield
                    self.V(lambda: nc.vector.tensor_tensor(B_['MT'][:], CBs[:], B_['Du'][:], ALU.mult), ['CBs', K('Du')], [K('MT')])
                    yield
                    self.mm(py[:, 0:64], B_['MT'][:], xdt[:], [K('MT'), K('xdt')], [kb_], start=True, stop=False, inc=False)
                    self.mm(py[:, 0:64], B_['CdT'][:], ST[:, e, :], [K('CdT'), ('ST', e)], [kb_], start=False, stop=True, inc=True)
                    self.mm(pS[:, 0:64], Btok[:], xdtd[:], ['Btok', K('xdtd')], [kb_])
                    yield
                    self.V(lambda: nc.vector.scalar_tensor_tensor(ygrp[:, hs], xtok[:, hs], self.ssd_con[:, i, 32 + h:33 + h], py[:, 0:64], ALU.mult, ALU.add),
                           ['xtok', 'ssd_con', kb_], [('ygrp', e)])
                    self.V(lambda: nc.vector.scalar_tensor_tensor(ST[:, e, :], ST[:, e, :], B_['eRB'][:, 127:128], pS[:, 0:64], ALU.mult, ALU.add),
                           [('ST', e), K('eRB'), kb_], [('ST', e)])
                    yield

            for s in range(NS):
                sp_ = slice(s * 128, (s + 1) * 128)
                pCB, kCB = slot(0); pBt, kBt = slot(1)
                self.mm(pCB, BT[:, sp_], CT[:, sp_], ['BT', 'CT'], [kCB])
                self.tr(pBt, BT[:, sp_], ['BT'], [kBt])
                self.A(lambda: nc.scalar.copy(CBs[:], pCB), [kCB], ['CBs'])
                self.A(lambda: nc.scalar.copy(Btok[:], pBt), [kBt], ['Btok'])
                for cc in range(4):
                    px, kx = slot(4 + cc)
                    self.V(lambda: nc.vector.tensor_copy(xc32[:], xT[:, cc, sp_]), [('xT', cc)], ['xc32'])
                    self.tr(px, xc32[:], ['xc32'], [kx])
                    self.A(lambda: nc.scalar.copy(xtok[:, cc * 128:(cc + 1) * 128], px), [kx], ['xtok'])
                self.run_chains([hchain(x_, s, range(x_, 8, 4)) for x_ in range(4)])
                ygk = [('ygrp', e_) for e_ in range(8)]
                pz = ps_proj[0]; kz = ('ps_proj', 0)
                for kc in range(NKC):
                    self.mm(pz[:], self.h16[:, kc, sp_], wz[:, kc, :], ['hwout', self.kh16(kc, s // 4)], [kz], start=(kc == 0), stop=(kc == NKC - 1), inc=(kc == NKC - 1))
                self.A(lambda: nc.scalar.activation(szb, pz[:], AF.Silu), [kz], ['szb'])
                self.V(lambda: nc.vector.tensor_tensor(ygrp[:], ygrp[:], szb, ALU.mult), ygk + ['szb'], ygk)
                self.A(lambda: nc.scalar.activation(szb, ygrp[:], AF.Square, accum_out=col['ss'][:]), ygk, ['szb', 'ss'])
                self.A(lambda: nc.scalar.activation(col['ri'][:], col['ss'][:], AF.Sqrt, bias=self.c1e6[:, 0:1], scale=1.0 / 512), ['ss', 'c1e6'], ['ri'])
                self.V(lambda: nc.vector.reciprocal(col['ri'][:], col['ri'][:]), ['ri'], ['ri'])
                self.V(lambda: nc.vector.tensor_scalar(ygrp[:], ygrp[:], col['ri'][:, 0:1], None, ALU.mult), ygk + ['ri'], ygk)
                for cc in range(4):
                    pt, kt = slot(4 + cc)
                    self.tr(pt, ygrp[:, cc * 128:(cc + 1) * 128], ygk, [kt])
                    self.A(lambda: nc.scalar.activation(oT[:, cc, sp_], pt, AF.Identity, scale=self.ssd_nw[:, i, g * 4 + cc:g * 4 + cc + 1]),
                           [kt, 'ssd_nw'], [('oT', s // 4)])
            self.out_proj_acc(self.hyb_w_out[i, 1024 + g * 512:1024 + (g + 1) * 512, :], oT, [('oT', k) for k in range(4)], 4, ps_proj, cres, wout, 'hwout')
            c.barrier()

    def nsa(self, layer):
        nc, c = self.nc, self.c
        i = layer // 2
        NS = T // 128
        NT = T // TT
        cres = 1.0 / ALPHA
        W = self.nsa_w_in[i].rearrange("(kc p) n -> p kc n", p=128)
        Wsw = self.nsa_w_sw[i].rearrange("(kc p) n -> p kc n", p=128)
        with ExitStack() as es:
            self._uid = getattr(self, '_uid', 0) + 1
            _u = "_%d" % self._uid
            sb = lambda name, shape, dtype=F32: es.enter_context(nc.sbuf_tensor(name + _u, shape, dtype))
            ps = lambda name, shape, dtype=F32: es.enter_context(nc.psum_tensor(name + _u, shape, dtype))
            psS = [ps("psS%d" % k, [128, TT]) for k in range(4)]
            ps_proj = [psS[0], psS[1]]
            c.alias = {('ps_proj', 0): ('psS', 0), ('ps_proj', 1): ('psS', 1)}
            psO = {br: ps("psO%d" % br, [128, 4, 128]) for br in range(3)}
            ps_misc = ps("ps_misc", [128, TT])
            cosT = sb("cosT", [128, T]); sinS = sb("sinS", [128, T])
            Mc16 = sb("Mc16", [128, T], BF16); Cm16 = sb("Cm16", [128, 4, TT], BF16); Wm16 = sb("Wm16", [128, 4, TT], BF16)
            E16 = sb("E16", [128, 16, 128], BF16); agg16 = sb("agg16", [128, 33], BF16)
            M12 = sb("M12", [128, 2, NS * 32])
            id16 = sb("id16", [128, 128], BF16); z16 = sb("z16", [128, 520], BF16)
            self.P(lambda: nc.gpsimd.memset(z16[:], 0.0), [], ['z16'])
            self.P(lambda: nc.gpsimd.tensor_copy(id16[:], self.ident[:]), ['ident'], ['id16'])
            c.dma('pool', Mc16[:], self.nsa_Mc[:, :], writes=['Mc16'])
            c.dma('pool', Cm16[:], self.nsa_Cm[:, :, :], writes=['Cm16'])
            c.dma('pool', Wm16[:], self.nsa_Wm[:, :, :], writes=['Wm16'])
            c.dma('pool', E16[0:32, :, :], self.nsa_E[:, :, :], writes=['E16'])
            c.dma('pool', agg16[:], self.nsa_agg[:, :], writes=['agg16'])
            c.dma('sp', M12[:], self.nsa_M12[:, :, :], writes=['M12'])
            es_t = ExitStack()
            sbt = lambda name, shape, dtype=F32: es_t.enter_context(nc.sbuf_tensor(name + _u, shape, dtype))
            posi = sbt("posi", [128, T], I32)
            ang = sbt("ang", [128, T]); kf = sbt("kf", [128, T])
            c.dma('sp', posi[:], self.pos_rep[:, :], writes=['posi'])
            self.V(lambda: nc.vector.tensor_copy(ang[:], posi[:]), ['posi'], ['ang'])
            self.V(lambda: nc.vector.tensor_scalar(ang[:], ang[:], self.rope_c[:, 0:1], None, ALU.mult), ['ang', 'rope_c'], ['ang'])
            MAGIC = 12582912.0
            TWO_PI = 2.0 * np.pi
            C1 = float(np.float32(TWO_PI)); C2 = float(TWO_PI - np.float64(np.float32(TWO_PI)))
            for (dst, shift, dk_) in ((sinS, 0.0, 'sinS'), (cosT, 0.5 * np.pi, 'cosT')):
                self.V(lambda: nc.vector.tensor_scalar(kf[:], ang[:], shift, 1.0 / TWO_PI, ALU.add, ALU.mult), ['ang'], ['kf'])
                self.V(lambda: nc.vector.tensor_scalar(kf[:], kf[:], MAGIC, None, ALU.add), ['kf'], ['kf'])
                self.V(lambda: nc.vector.tensor_scalar(kf[:], kf[:], -MAGIC, None, ALU.add), ['kf'], ['kf'])
                self.V(lambda: nc.vector.tensor_scalar(dst[:], ang[:], shift, None, ALU.add), ['ang'], [dk_])
                self.V(lambda: nc.vector.scalar_tensor_tensor(dst[:], kf[:], -C1, dst[:], ALU.mult, ALU.add), ['kf', dk_], [dk_])
                self.V(lambda: nc.vector.scalar_tensor_tensor(dst[:], kf[:], -C2, dst[:], ALU.mult, ALU.add), ['kf', dk_], [dk_])
                self.V(lambda: nc.vector.tensor_scalar(dst[:], dst[:], float(np.pi), -float(np.pi), ALU.min, ALU.max), [dk_], [dk_])
                self.A(lambda: nc.scalar.activation(dst[:], dst[:], AF.Sin), [dk_], [dk_])
            self.V(lambda: nc.vector.tensor_scalar(sinS[:], sinS[:], self.rope_c[:, 1:2], None, ALU.mult), ['sinS', 'rope_c'], ['sinS'])
            c.barrier()
            es_t.close()
            gates = sb("gates", [128, NS, 48])
            wg = sb("wg", [128, NKC, 48], BF16)
            self.load_w(wg[:], W[:, :, 1792:1840], 'wg')
            pm = ps_misc[:, 0:384].rearrange("p (s n) -> p s n", n=48)
            for half in range(2):
                for s8 in range(8):
                    s_ = half * 8 + s8
                    for kc in range(NKC):
                        self.mm(pm[:, s8, :], self.h16[:, kc, s_ * 128:(s_ + 1) * 128], wg[:, kc, :], ['wg', self.kh16(kc, s_ // 4)], ['ps_misc'],
                                start=(kc == 0), stop=(kc == NKC - 1), inc=(kc == NKC - 1))
                self.A(lambda: nc.scalar.activation(gates[:, half * 8:(half + 1) * 8, :], pm, AF.Exp, scale=-1.0), ['ps_misc'], ['gates'])
            self.V(lambda: nc.vector.tensor_scalar(gates[:], gates[:], 1.0, None, ALU.add), ['gates'], ['gates'])
            self.V(lambda: nc.vector.reciprocal(gates[:], gates[:]), ['gates'], ['gates'])
            wa = [sb("nwa%d" % k, [128, NKC, 128], BF16) for k in range(2)]
            w1t = sb("w1t", [128, 16, 256], BF16); w2t = sb("w2t", [128, 2, 128], BF16)
            pos16 = sb("pos16", [128, 16], BF16); bcol = sb("bcol", [128, 1])
            hT16 = sb("hT16", [128, 2, 128], BF16)
            kcmp = sb("kcmp", [128, 128], BF16); vcx = sb("vcx", [128, 65], BF16)
            ksT2 = sb("ksT2", [128, T], BF16); kwT2 = sb("kwT2", [128, T], BF16)
            vsx = sb("vsx", [128, NS, 65], BF16); vwx = sb("vwx", [128, NS, 65], BF16)
            selT = sb("selT", [128, T], BF16)
            imp = sb("imp", [128, NS, 32]); score = sb("score", [128, NS, 32]); top8 = sb("top8", [128, 8]); selb = sb("selb", [128, 32])
            qT16 = sb("qT16", [128, T], BF16); qrT16 = sb("qrT16", [128, T], BF16)
            kc2 = qT16
            c.alias['kc2'] = 'qT16'
            t1 = sb("nt1", [128, TT]); t2 = sb("nt2", [128, TT])
            pT = [sb("pT%d" % k, [128, TT], BF16) for k in range(4)]
            rl4 = [sb("rl4_%d" % k, [128, 4, 1]) for k in range(3)]
            ctmp = [sb("ctmp%d" % k, [128, 4, 64]) for k in range(3)]
            oacc = sb("oacc", [128, 4, 128]); oT = sb("noT", [128, 1, T], BF16); wout = sb("nwout", [128, 1, D], BF16)
            rl = sb("rl", [128, 1])
            self.P(lambda: nc.gpsimd.memset(vcx[:], 1.0), [], ['vcx'])
            self.P(lambda: nc.gpsimd.memset(vsx[:], 1.0), [], ['vsx'])
            self.P(lambda: nc.gpsimd.memset(vwx[:], 1.0), [], ['vwx'])

            def load_dup(dst, key, src, col0):
                c.dma('pool', dst[:, :, 0:64], src[:, :, col0:col0 + 64], writes=[key])
                c.dma('pool', dst[:, :, 64:128], src[:, :, col0:col0 + 64], writes=[key])

            def proj_tile(wt, wkey, tt, k):
                ts = slice(tt * TT, (tt + 1) * TT)
                pp = ps_proj[k]; pk = ('ps_proj', k)
                for kc in range(NKC):
                    self.mm(pp[:], wt[:, kc, :], self.h16[:, kc, ts], [wkey, self.kh16(kc, tt)], [pk], start=(kc == 0), stop=(kc == NKC - 1), inc=(kc == NKC - 1))
                return pp, pk

            def rope_proj(dst, dkey, col0, colsw, dup):
                if dup:
                    load_dup(wa[0], 'wa0', W, col0); load_dup(wa[1], 'wa1', Wsw, colsw)
                else:
                    self.load_w(wa[0][:], W[:, :, col0:col0 + 128], 'wa0'); self.load_w(wa[1][:], Wsw[:, :, colsw:colsw + 128], 'wa1')
                for tt in range(NT):
                    ts = slice(tt * TT, (tt + 1) * TT)
                    p0, k0 = proj_tile(wa[0], 'wa0', tt, 0)
                    p1, k1 = proj_tile(wa[1], 'wa1', tt, 1)
                    self.V(lambda: nc.vector.tensor_tensor(t1[:], p0[:], cosT[:, ts], ALU.mult), [k0, 'cosT'], ['nt1'])
                    self.V(lambda: nc.vector.tensor_tensor(t2[:], p1[:], sinS[:, ts], ALU.mult), [k1, 'sinS'], ['nt2'])
                    self.P(lambda: nc.gpsimd.tensor_tensor(dst[:, ts], t1[:], t2[:], ALU.add), ['nt1', 'nt2'], [dkey])
                    if dst is qrT16:
                        self.A(lambda: nc.scalar.copy(qT16[:, ts], p0[:]), [k0], ['qT16'])

            def cmp_scores(hh, tt, k):
                ts = slice(tt * TT, (tt + 1) * TT)
                hp = slice(hh * 64, (hh + 1) * 64)
                pS = psS[k]; kS = ('psS', k)
                self.mm(pS[0:127, :], kcmp[hp, 0:127], qT16[hp, ts], ['kcmp', 'qT16'], [kS], start=True, stop=False, inc=False)
                self.mm(pS[0:127, :], id16[0:127, 0:127], Mc16[0:127, ts], ['id16', 'Mc16'], [kS], start=False, stop=True, inc=True)
                self.A(lambda: nc.scalar.activation(pT[k][0:127, :], pS[0:127, :], AF.Exp, scale=0.125), [kS], [('pT', k)])

            for g in range(2):
                rope_proj(ksT2, 'ksT2', 1280 + g * 64, 1024 + g * 64, True)
                rope_proj(kwT2, 'kwT2', 1536 + g * 64, 1152 + g * 64, True)
                c.dma('pool', wa[0][:, :, 0:64], W[:, :, 1408 + g * 64:1408 + (g + 1) * 64], writes=['wa0'])
                c.dma('pool', wa[0][:, :, 64:128], W[:, :, 1664 + g * 64:1664 + (g + 1) * 64], writes=['wa0'])
                pv4 = ps_misc[:, 0:512].rearrange("p (s n) -> p s n", n=128)
                for s4 in range(4):
                    for sj in range(4):
                        s_ = s4 * 4 + sj
                        for kc in range(NKC):
                            self.mm(pv4[:, sj, :], self.h16[:, kc, s_ * 128:(s_ + 1) * 128], wa[0][:, kc, :], ['wa0', self.kh16(kc, s_ // 4)], ['ps_misc'],
                                    start=(kc == 0), stop=(kc == NKC - 1), inc=(kc == NKC - 1))
                    self.A(lambda: nc.scalar.copy(vsx[:, s4 * 4:(s4 + 1) * 4, 0:64], pv4[:, :, 0:64]), ['ps_misc'], ['vsx'])
                    self.A(lambda: nc.scalar.copy(vwx[:, s4 * 4:(s4 + 1) * 4, 0:64], pv4[:, :, 64:128]), ['ps_misc'], ['vwx'])
                for kv in range(2):
                    load_dup(wa[0], 'wa0', W, (1024 if kv == 0 else 1152) + g * 64)
                    for tt in range(NT):
                        p0, k0 = proj_tile(wa[0], 'wa0', tt, tt % 2)
                        self.A(lambda: nc.scalar.copy(kc2[0:64, tt * TT:(tt + 1) * TT], p0[0:64, :]), [k0], ['kc2'])
                        if tt == 0:
                            self.V(lambda: nc.vector.tensor_copy(kc2[64:128, 0:TT - 1], p0[64:128, 1:TT]), [k0], ['kc2'])
                        else:
                            self.V(lambda: nc.vector.tensor_copy(kc2[64:128, tt * TT - 1:(tt + 1) * TT - 1], p0[64:128, :]), [k0], ['kc2'])
                    c.dma('pool', w1t[:], self.nsa_cmp_w1[i, kv].rearrange("(m p) j -> p m j", p=128), writes=['w1t'])
                    c.dma('pool', pos16[:], self.nsa_pos2[:, i, kv, :], writes=['pos16'])
                    w2v = self.nsa_cmp_w2[i, kv].rearrange("(jc p) d -> p jc d", p=128)
                    c.dma('pool', w2t[:, :, 0:64], w2v, writes=['w2t'])
                    c.dma('pool', w2t[:, :, 64:128], w2v, writes=['w2t'])
                    for jc in range(2):
                        js = slice(jc * 128, (jc + 1) * 128)
                        for m in range(16):
                            self.mm(ps_misc[:, 0:1], w1t[:, m, js], pos16[:, m:m + 1], ['w1t', 'pos16'], ['ps_misc'], start=(m == 0), stop=(m == 15), inc=(m == 15))
                        self.A(lambda: nc.scalar.copy(bcol[:], ps_misc[:, 0:1]), ['ps_misc'], ['bcol'])
                        for m in range(16):
                            self.mm(ps_misc[:, 0:127], w1t[:, m, js], kc2[:, 2 * m:2 * m + 16 * 126 + 1:16], ['w1t', 'kc2'], ['ps_misc'],
                                    start=(m == 0), stop=(m == 15), inc=(m == 15))
                        self.A(lambda: nc.scalar.activation(hT16[:, jc, 0:127], ps_misc[:, 0:127], AF.Silu, bias=bcol[:, 0:1], scale=1.0), ['ps_misc', 'bcol'], ['hT16'])
                    if kv == 0:
                        for jc in range(2):
                            self.mm(ps_misc[:, 0:127], w2t[:, jc, :], hT16[:, jc, 0:127], ['w2t', 'hT16'], ['ps_misc'], start=(jc == 0), stop=(jc == 1), inc=(jc == 1))
                        self.A(lambda: nc.scalar.copy(kcmp[:, 0:127], ps_misc[:, 0:127]), ['ps_misc'], ['kcmp'])
                    else:
                        for jc in range(2):
                            self.mm(ps_misc[0:127, 0:64], hT16[:, jc, 0:127], w2t[:, jc, 0:64], ['w2t', 'hT16'], ['ps_misc'], start=(jc == 0), stop=(jc == 1), inc=(jc == 1))
                        self.A(lambda: nc.scalar.copy(vcx[0:127, 0:64], ps_misc[0:127, 0:64]), ['ps_misc'], ['vcx'])
                self.P(lambda: nc.gpsimd.memset(imp[:], 0.0), [], ['imp'])

                def p1chain(x):
                    for tt in range(NT):
                        k = 2 * x + (tt % 2)
                        cmp_scores(x, tt, k)
                        yield
                        pI = psO[x]; kI = ('psO', x)
                        for st in range(4):
                            self.mm(pI[:, st, 0:33], pT[k][0:127, st * 128:(st + 1) * 128], agg16[0:127, :], [('pT', k), 'agg16'], [kI])
                        yield
                        self.V(lambda: nc.vector.tensor_scalar(rl4[x][:], pI[:, :, 32:33], 1e-30, None, ALU.max), [kI], [('rl4', x)])
                        self.V(lambda: nc.vector.reciprocal(rl4[x][:], rl4[x][:]), [('rl4', x)], [('rl4', x)])
                        yield
                        self.V(lambda: nc.vector.tensor_tensor(ctmp[x][:, :, 0:32], pI[:, :, 0:32], rl4[x][:].to_broadcast([128, 4, 32]), ALU.mult),
                               [kI, ('rl4', x)], [('ctmp', x)])
                        yield
                        self.V(lambda: nc.vector.tensor_tensor(imp[:, tt * 4:(tt + 1) * 4, :], imp[:, tt * 4:(tt + 1) * 4, :], ctmp[x][:, :, 0:32], ALU.add),
                               ['imp', ('ctmp', x)], ['imp'])
                        yield

                for c4 in range(4):
                    cg = g * 4 + c4
                    self.load_w(wa[0][:], W[:, :, cg * 128:(cg + 1) * 128], 'wa0')
                    for tt in range(NT):
                        p0, k0 = proj_tile(wa[0], 'wa0', tt, tt % 2)
                        self.A(lambda: nc.scalar.copy(qT16[:, tt * TT:(tt + 1) * TT], p0[:]), [k0], ['qT16'])
                    self.run_chains([p1chain(0), p1chain(1)])
                f3 = lambda t_: t_[:].rearrange("p s n -> p (s n)")
                self.V(lambda: nc.vector.tensor_tensor(f3(score), f3(imp), M12[:, 0, :], ALU.mult), ['imp', 'M12'], ['score'])
                self.V(lambda: nc.vector.tensor_tensor(f3(score), f3(score), M12[:, 1, :], ALU.add), ['score', 'M12'], ['score'])
                for s_ in range(NS):
                    self.V(lambda: nc.vector.max(top8[:], score[:, s_, :]), ['score'], ['top8'])
                    self.V(lambda: nc.vector.tensor_scalar(selb[:], score[:, s_, :], top8[:, 7:8], None, ALU.is_ge), ['score', 'top8'], ['selb'])
                    self.V(lambda: nc.vector.tensor_scalar(selb[:], selb[:], -1.0, 30000.0, ALU.add, ALU.mult), ['selb'], ['selb'])
                    self.tr(ps_misc[0:32, 0:128], selb[:], ['selb'], ['ps_misc'])
                    self.A(lambda: nc.scalar.copy(selT[0:32, s_ * 128:(s_ + 1) * 128], ps_misc[0:32, 0:128]), ['ps_misc'], ['selT'])
                def combine(br, e, tt, hp):
                    x = br
                    pO = psO[br]; kO = ('psO', br)
                    self.V(lambda: nc.vector.tensor_scalar(rl4[x][:], pO[:, :, 64:65], 1e-30, None, ALU.max), [kO], [('rl4', x)])
                    self.V(lambda: nc.vector.reciprocal(rl4[x][:], rl4[x][:]), [('rl4', x)], [('rl4', x)])
                    self.V(lambda: nc.vector.tensor_tensor(rl4[x][:], rl4[x][:], gates[:, tt * 4:(tt + 1) * 4, e * 3 + br:e * 3 + br + 1], ALU.mult),
                           [('rl4', x), 'gates'], [('rl4', x)])
                    self.V(lambda: nc.vector.tensor_tensor(ctmp[x][:], pO[:, :, 0:64], rl4[x][:].to_broadcast([128, 4, 64]), ALU.mult),
                           [kO, ('rl4', x)], [('ctmp', x)])
                    self.V(lambda: nc.vector.tensor_tensor(oacc[:, :, hp], oacc[:, :, hp], ctmp[x][:], ALU.add), ['oacc', ('ctmp', x)], ['oacc'])

                def branch(br, e, tt, hh, part, nparts, kbufs):
                    ts = slice(tt * TT, (tt + 1) * TT)
                    hp = slice(hh * 64, (hh + 1) * 64)
                    if br == 1 and part == 0:
                        cmp_scores(hh, tt, kbufs[0])
                        yield
                        for st in range(4):
                            self.mm(psO[0][:, st, 0:65], pT[kbufs[0]][0:127, st * 128:(st + 1) * 128], vcx[0:127, :], [('pT', kbufs[0]), 'vcx'], [('psO', 0)])
                        yield
                        combine(0, e, tt, hp)
                        yield
                    kT_, kTk, vx, vxk = (ksT2, 'ksT2', vsx, 'vsx') if br == 1 else (kwT2, 'kwT2', vwx, 'vwx')
                    pO = psO[br]; kO = ('psO', br)
                    kt0 = 0 if br == 1 else max(0, 4 * tt - 4)
                    kts = list(range(kt0, 4 * tt + 4))[part::nparts]
                    for n_, kt in enumerate(kts):
                        r = kt - 4 * tt
                        k = kbufs[n_ % len(kbufs)]
                        pS = psS[k]; kS = ('psS', k)
                        nmask = (1 if br == 1 else 0) + (1 if (r >= 0 or br == 2) else 0)
                        self.mm(pS[:], kT_[hp, kt * 128:(kt + 1) * 128], qrT16[hp, ts], [kTk, 'qrT16'], [kS], start=True, stop=(nmask == 0), inc=(nmask == 0))
                        if br == 1:
                            nmask -= 1
                            self.mm(pS[:], E16[0:32, kt, :], selT[0:32, ts], ['E16', 'selT'], [kS], start=False, stop=(nmask == 0), inc=(nmask == 0))
                        if r >= 0:
                            self.mm(pS[:], id16[:], Cm16[:, r, :], ['id16', 'Cm16'], [kS], start=False, stop=True, inc=True)
                        elif br == 2:
                            self.mm(pS[:], id16[:], Wm16[:, r + 4, :], ['id16', 'Wm16'], [kS], start=False, stop=True, inc=True)
                        yield
                        self.A(lambda: nc.scalar.activation(pT[k][:], pS[:], AF.Exp, scale=0.125), [kS], [('pT', k)])
                        yield
                        for st in range(4):
                            tmin = tt * TT + st * 128; tmax = tmin + 127
                            if kt * 128 > tmax:
                                continue
                            if br == 2 and kt * 128 + 127 <= tmin - 512:
                                continue
                            self.mm(pO[:, st, 0:65], pT[k][:, st * 128:(st + 1) * 128], vx[:, kt, :], [('pT', k), vxk], [kO],
                                    start=False, stop=False, inc=True)
                        yield

                def head_tile(e, tt, hh):
                    hp = slice(hh * 64, (hh + 1) * 64)
                    for br in (1, 2):
                        self.mm(psO[br][:].rearrange("p a b -> p (a b)"), z16[:, 0:128], z16[:, 0:512], ['z16'], [('psO', br)], start=True, stop=False)
                    self.run_chains([branch(1, e, tt, hh, 0, 2, [0]), branch(1, e, tt, hh, 1, 2, [1]), branch(2, e, tt, hh, 0, 1, [2, 3])])
                    for br in (1, 2):
                        self.mm(psO[br][:].rearrange("p a b -> p (a b)"), z16[:, 0:128], z16[:, 0:512], ['z16'], [('psO', br)], start=False, stop=True)
                        combine(br, e, tt, hp)

                for c4 in range(4):
                    cg = g * 4 + c4
                    rope_proj(qrT16, 'qrT16', cg * 128, cg * 128, False)
                    for tt in range(NT):
                        self.P(lambda: nc.gpsimd.memset(oacc[:], 0.0), [], ['oacc'])
                        for hh in range(2):
                            e = cg * 2 + hh
                            head_tile(e, tt, hh)
                        for st in range(4):
                            s_ = tt * 4 + st
                            self.tr(ps_misc[:, 0:128], oacc[:, st, :], ['oacc'], ['ps_misc'])
                            self.A(lambda: nc.scalar.copy(oT[:, 0, s_ * 128:(s_ + 1) * 128], ps_misc[:, 0:128]), ['ps_misc'], [('oT', tt)])
                    self.out_proj_acc(self.nsa_w_out[i, cg * 128:(cg + 1) * 128, :], oT, [('oT', k_) for k_ in range(4)], 1, ps_proj, cres, wout, 'nwout')
            c.barrier()
            c.alias = {}

    def run_chains(self, gens):
        gens = list(gens)
        while gens:
            for g_ in list(gens):
                try:
                    next(g_)
                except StopIteration:
                    gens.remove(g_)

    def mix_ln(self, layer):
        nc = self.nc
        with ExitStack() as es:
            self._uid = getattr(self, '_uid', 0) + 1
            _u = "_%d" % self._uid
            sb = lambda name, shape, dtype=F32: es.enter_context(nc.sbuf_tensor(name + _u, shape, dtype))
            ps = lambda name, shape, dtype=F32: es.enter_context(nc.psum_tensor(name + _u, shape, dtype))
            scr = {
                'zsq': [sb("ln_zsq%d" % i, [128, TT], BF16) for i in range(2)],
                'zc': [sb("ln_zc%d" % i, [128, TT]) for i in range(2)],
                'mean': sb("ln_mean", [128, TT]), 'rstd': sb("ln_rstd", [128, TT]),
            }
            scr['tmp'] = scr['zc'][0]
            ps_sum = ps("ps_sum", [128, TT])
            ps_sq = ps("ps_sq", [128, TT])
            self.layer_norm_tiles(layer * 3 + 1, list(range(T // TT)), ps_sum, ps_sq, scr)
            self.c.barrier()

    def build(self):
        self.prologue()
        for (kind, layer, which) in self.stages:
            if kind == 'ffn':
                self.ffn(layer, which, layer * 3 + (0 if which == 0 else 2))
            elif kind == 'mix':
                if layer % 2 == 0:
                    self.hybrid(layer)
                else:
                    self.nsa(layer)
                self.mix_ln(layer)
        self.epilogue()
        self.es.close()
        return self.nc


def host_inputs(inputs, b):
    m = {}
    m["xT"] = np.ascontiguousarray(inputs["x"][b].T)
    m["ffn_w_in"] = inputs["ffn_w_in"]
    m["ffn_w_out"] = inputs["ffn_w_out"]
    g = inputs["ln_g"].reshape(DEPTH * 3, NKC, 128)
    m["ln_gT"] = np.ascontiguousarray(g.transpose(2, 0, 1).reshape(128, DEPTH * 3 * NKC))
    bb = inputs["ln_b"].reshape(DEPTH * 3, NKC, 128)
    m["ln_bT"] = np.ascontiguousarray(bb.transpose(2, 0, 1).reshape(128, DEPTH * 3 * NKC))
    m["hyb_w_in"] = inputs["hyb_w_in"]
    m["hyb_w_out"] = inputs["hyb_w_out"]
    m["consts"] = make_consts()
    rep = lambda a: np.broadcast_to(a[None], (128,) + a.shape)
    m["gdn_con"] = np.ascontiguousarray(rep(np.concatenate([inputs["gdn_a_log"], inputs["gdn_dt_bias"]], axis=1))).astype(np.float32)
    cw = inputs["gdn_conv_w"].reshape(2, 4, 16, 128)
    m["gdn_cw"] = np.ascontiguousarray(cw.transpose(3, 0, 2, 1))
    m["gdn_nw"] = np.ascontiguousarray(rep(inputs["gdn_norm_w"])).astype(np.float32)
    m["ssd_con"] = np.ascontiguousarray(rep(np.concatenate([inputs["ssd_a_log"], inputs["ssd_dt_bias"], inputs["ssd_d"]], axis=1))).astype(np.float32)
    scw = np.concatenate([inputs["ssd_conv_w"], inputs["ssd_conv_b"][:, None, :]], axis=1).reshape(2, 5, 12, 128)
    m["ssd_cw"] = np.ascontiguousarray(scw.transpose(3, 0, 2, 1))
    m["ssd_nw"] = np.ascontiguousarray(inputs["ssd_norm_w"].reshape(2, 8, 128).transpose(2, 0, 1))
    m["nsa_w_in"] = inputs["nsa_w_in"]
    m["nsa_w_out"] = inputs["nsa_w_out"]
    m["nsa_cmp_w1"] = inputs["nsa_cmp_w1"]
    m["nsa_cmp_w2"] = inputs["nsa_cmp_w2"]
    perm = np.arange(64)
    perm[0:8] = np.arange(8, 16)
    perm[8:16] = np.arange(0, 8)
    wi = inputs["nsa_w_in"]
    qsw = wi[:, :, 0:1024].reshape(2, D, 16, 64)[:, :, :, perm].reshape(2, D, 1024)
    kssw = wi[:, :, 1280:1408].reshape(2, D, 2, 64)[:, :, :, perm].reshape(2, D, 128)
    kwsw = wi[:, :, 1536:1664].reshape(2, D, 2, 64)[:, :, :, perm].reshape(2, D, 128)
    m["nsa_w_sw"] = np.ascontiguousarray(np.concatenate([qsw, kssw, kwsw], axis=2))
    cp = inputs["nsa_cmp_pos"].reshape(2, 2, 16, 2, 64)
    m["nsa_pos2"] = np.ascontiguousarray(cp.transpose(3, 4, 0, 1, 2).reshape(128, 2, 2, 16))
    m["pos_rep"] = np.ascontiguousarray(np.broadcast_to(inputs["positions"][b][None, :], (128, T))).astype(np.int32)
    m.update(nsa_consts())
    return m


_NSA_CONSTS = None


def nsa_consts():
    global _NSA_CONSTS
    if _NSA_CONSTS is not None:
        return _NSA_CONSTS
    NEG = -30000.0
    p = np.arange(128)
    d = p % 64
    half = 8
    invf = np.where(d < 16, 500000.0 ** (-(d % half).astype(np.float64) / half), 0.0).astype(np.float32)
    sgn = np.where(d < 8, -1.0, np.where(d < 16, 1.0, 0.0)).astype(np.float32)
    out = {"rope_c": np.ascontiguousarray(np.stack([invf, sgn], axis=1))}
    t = np.arange(T)
    cmp_end = 16 * np.arange(128) + 31
    out["nsa_Mc"] = np.where(cmp_end[:, None] <= t[None, :], 0.0, NEG).astype(np.float32)
    tl = np.arange(TT)
    cm = np.zeros((128, 4, TT), np.float32)
    wm = np.zeros((128, 4, TT), np.float32)
    for r in range(4):
        cm[:, r, :] = np.where((p[:, None] + 128 * r) <= tl[None, :], 0.0, NEG)
        rr = r - 4
        wm[:, r, :] = np.where((p[:, None] + 128 * rr + 512) > tl[None, :], 0.0, NEG)
    out["nsa_Cm"] = cm
    out["nsa_Wm"] = wm
    E = np.zeros((32, 16, 128), np.float32)
    for kt in range(16):
        for pp in range(128):
            E[2 * kt + pp // 64, kt, pp] = 1.0
    out["nsa_E"] = E
    n_cmp = 127
    c0 = np.arange(n_cmp)[:, None] * 16
    s0 = np.arange(32)[None, :] * 64
    agg = np.clip(np.minimum(c0 + 32, s0 + 64) - np.maximum(c0, s0), 0, None) / 32.0
    aggx = np.zeros((128, 33), np.float32)
    aggx[:n_cmp, :32] = agg
    aggx[:n_cmp, 32] = 1.0
    out["nsa_agg"] = aggx
    tok = (np.arange(16)[None, :, None] * 128 + p[:, None, None])
    j = np.arange(32)[None, None, :]
    cur = tok // 64
    causal = j <= cur
    forced = (j == 0) | (causal & (j > cur - 2))
    M1 = (causal & ~forced).astype(np.float32)
    M2 = np.where(forced, 1e4, np.where(causal, 0.0, -1.0)).astype(np.float32)
    out["nsa_M12"] = np.ascontiguousarray(np.stack([M1.reshape(128, 512), M2.reshape(128, 512)], axis=1))
    _NSA_CONSTS = out
    return out


def make_consts():
    i = np.arange(128)[:, None]
    j = np.arange(128)[None, :]
    same = (i // 64) == (j // 64)
    NEG = -30000.0
    cs = np.zeros((8, 128, 128), np.float32)
    cs[0] = (i == j)
    cs[1] = np.where(same & (j < i), 0.0, NEG)
    cs[2] = np.where(same & (j >= i), 0.0, NEG)
    cs[3] = (i != j)
    cs[4] = (same & (i <= j))
    cs[5] = same
    cs[6] = (i <= j)
    cs[7] = np.where(j >= i, 0.0, NEG)
    return np.ascontiguousarray(cs.transpose(1, 0, 2))


FULL_STAGES = []
for _l in range(DEPTH):
    FULL_STAGES.append(('ffn', _l, 0))
    FULL_STAGES.append(('mix', _l, 0))
    FULL_STAGES.append(('ffn', _l, 1))


def kernel(**inputs):
    inputs = {k: np.asarray(v) for k, v in inputs.items()}
    prog = Prog(FULL_STAGES)
    nc = prog.build()
    B = inputs["x"].shape[0]
    in_maps = [host_inputs(inputs, b) for b in range(B)]
    res = run_bass_kernel_spmd(nc, in_maps, core_ids=list(range(B)))
    out = np.stack([np.ascontiguousarray(r["outT"].T) for r in res.results], axis=0)
    return out.astype(np.float32)
```

```python
import numpy as np
from contextlib import ExitStack
import concourse.bass as bass
import concourse.mybir as mybir
from concourse.bass_utils import run_bass_kernel_spmd

F32 = mybir.dt.float32
BF16 = mybir.dt.bfloat16
I32 = mybir.dt.int32
AF = mybir.ActivationFunctionType
ALU = mybir.AluOpType
AX = mybir.AxisListType

D = 1024
T = 2048
DEPTH = 4
DFF = 2816
ALPHA = (2.0 * DEPTH) ** 0.25
LN_EPS = 1e-5
NKC = D // 128
NFC = DFF // 128
TT = 512

SAME_ENGINE_SYNC = True


class Ctx:
    def __init__(self, nc, es, n_dma_sems=32):
        self.nc = nc
        self.eng = {'pe': nc.tensor, 'act': nc.scalar, 'dve': nc.vector, 'pool': nc.gpsimd, 'sp': nc.sync}
        self.sem = {}
        self.cnt = {}
        for e in ['pe', 'act', 'dve', 'pool']:
            self.sem[e] = es.enter_context(nc.semaphore("s_" + e))
            self.cnt[e] = 0
        self.dma_sems = [es.enter_context(nc.semaphore("s_dma%d" % i)) for i in range(n_dma_sems)]
        self.dma_val = [0] * n_dma_sems
        self.dma_rr = 0
        self.waited = {}
        self.last_w = {}
        self.readers = {}
        self.ninstr = 0
        self.alias = {}

    def _wait(self, e, tok):
        sem, key, val = tok
        k = (e, key)
        if self.waited.get(k, 0) >= val:
            return
        self.eng[e].wait_ge(sem, val)
        self.waited[k] = val

    def _deps(self, e, reads, writes):
        toks = {}

        def add(tok):
            if tok is None:
                return
            sem, key, val = tok
            if key == e and (e == 'pe' or not SAME_ENGINE_SYNC):
                return
            if key not in toks or toks[key][2] < val:
                toks[key] = tok
        for r in reads:
            add(self.last_w.get(r))
        for w in writes:
            add(self.last_w.get(w))
            for tok in self.readers.get(w, {}).values():
                add(tok)
        for tok in toks.values():
            self._wait(e, tok)

    def _record(self, tok, reads, writes):
        for r in reads:
            self.readers.setdefault(r, {})[tok[1]] = tok
        for w in writes:
            self.last_w[w] = tok
            self.readers[w] = {}

    def op(self, e, fn, reads=(), writes=(), inc=True):
        reads = [self.alias.get(k, k) for k in reads]
        writes = [self.alias.get(k, k) for k in writes]
        for k in reads:
            kk = k[0] if isinstance(k, tuple) else k
            if isinstance(kk, str) and kk.startswith('ps') and k not in writes:
                writes.append(k)
        reads = [k for k in reads if k not in writes]
        self._deps(e, reads, writes)
        ins = fn()
        self.ninstr += 1
        tok = (self.sem[e], e, self.cnt[e] + 1)
        self._record(tok, reads, writes)
        if inc:
            ins.then_inc(self.sem[e], 1)
            self.cnt[e] += 1
        return ins

    def dma(self, q, out, in_, reads=(), writes=(), **kw):
        reads = list(reads)
        writes = list(writes)
        i = self.dma_rr
        self.dma_rr = (self.dma_rr + 1) % len(self.dma_sems)
        sem = self.dma_sems[i]
        key = "dma%d" % i
        if self.dma_val[i] > 0:
            self._wait(q, (sem, key, self.dma_val[i]))
        self._deps(q, reads, writes)
        ins = self.eng[q].dma_start(out=out, in_=in_, **kw)
        self.dma_val[i] += 16
        ins.then_inc(sem, 16)
        tok = (sem, key, self.dma_val[i])
        self._record(tok, reads, writes)
        self.ninstr += 1
        return tok

    def barrier(self):
        for e in ['pe', 'act', 'dve', 'pool', 'sp']:
            for e2 in ['pe', 'act', 'dve', 'pool']:
                if e2 != e and self.cnt[e2] > 0:
                    self._wait(e, (self.sem[e2], e2, self.cnt[e2]))
            for i, sem in enumerate(self.dma_sems):
                if self.dma_val[i] > 0:
                    self._wait(e, (sem, "dma%d" % i, self.dma_val[i]))

    def wait_all(self, e, keys):
        for k in keys:
            tok = self.last_w.get(k)
            if tok is not None:
                self._wait(e, tok)


class Prog:
    def __init__(self, stages, dbg=()):
        self.stages = stages
        self.dbg = dbg
        nc = self.nc = bass.Bass("TRN2", target_bir_lowering=False)
        self.es = ExitStack()
        es = self.es
        self.c = Ctx(nc, es)
        dt = lambda name, shape, dtype=F32, kind="ExternalInput": nc.dram_tensor(name, shape, dtype, kind=kind).ap()
        self.xT = dt("xT", [D, T])
        self.ffn_w_in = dt("ffn_w_in", [DEPTH, 2, D, 2 * DFF])
        self.ffn_w_out = dt("ffn_w_out", [DEPTH, 2, DFF, D])
        self.ln_gT = dt("ln_gT", [128, DEPTH * 3 * NKC])
        self.ln_bT = dt("ln_bT", [128, DEPTH * 3 * NKC])
        self.outT = dt("outT", [D, T], kind="ExternalOutput")
        self.hyb_w_in = dt("hyb_w_in", [2, D, 5664])
        self.hyb_w_out = dt("hyb_w_out", [2, 2048, D])
        self.consts_d = dt("consts", [128, 8, 128])
        self.gdn_con_d = dt("gdn_con", [128, 2, 16])
        self.gdn_cw_d = dt("gdn_cw", [128, 2, 16, 4])
        self.gdn_nw_d = dt("gdn_nw", [128, 2, 128])
        self.ssd_con_d = dt("ssd_con", [128, 2, 48])
        self.ssd_cw_d = dt("ssd_cw", [128, 2, 12, 5])
        self.ssd_nw_d = dt("ssd_nw", [128, 2, 8])
        self.nsa_w_in = dt("nsa_w_in", [2, D, 1840])
        self.nsa_w_sw = dt("nsa_w_sw", [2, D, 1280])
        self.nsa_w_out = dt("nsa_w_out", [2, D, D])
        self.nsa_cmp_w1 = dt("nsa_cmp_w1", [2, 2, 2048, 256])
        self.nsa_cmp_w2 = dt("nsa_cmp_w2", [2, 2, 256, 64])
        self.nsa_pos2 = dt("nsa_pos2", [128, 2, 2, 16])
        self.pos_rep = dt("pos_rep", [128, T], I32)
        self.rope_c_d = dt("rope_c", [128, 2])
        self.nsa_Mc = dt("nsa_Mc", [128, T])
        self.nsa_Cm = dt("nsa_Cm", [128, 4, TT])
        self.nsa_Wm = dt("nsa_Wm", [128, 4, TT])
        self.nsa_E = dt("nsa_E", [32, 16, 128])
        self.nsa_agg = dt("nsa_agg", [128, 33])
        self.nsa_M12 = dt("nsa_M12", [128, 2, 512])
        sb = lambda name, shape, dtype=F32: es.enter_context(nc.sbuf_tensor(name, shape, dtype))
        self.sb = sb
        self.h32 = sb("h32", [128, NKC, T])
        self.h16 = sb("h16", [128, NKC, T], BF16)
        self.lng = sb("lng", [128, DEPTH * 3 * NKC])
        self.lnb = sb("lnb", [128, DEPTH * 3 * NKC])
        self.ones32 = sb("ones32", [128, 128])
        self.ones16 = sb("ones16", [128, 128], BF16)
        self.epsc = sb("epsc", [128, 1])
        self.onec = sb("onec", [128, 1])
        self.c1e6 = sb("c1e6", [128, 1])
        self.consts = sb("consts_sb", [128, 8, 128])
        self.ident = self.consts[:, 0, :]
        self.maskL = self.consts[:, 1, :]
        self.maskU = self.consts[:, 2, :]
        self.offdiag = self.consts[:, 3, :]
        self.tri64 = self.consts[:, 4, :]
        self.blk64 = self.consts[:, 5, :]
        self.tri128 = self.consts[:, 6, :]
        self.maskU128 = self.consts[:, 7, :]
        self.gdn_con = sb("gdn_con_sb", [128, 2, 16])
        self.gdn_cw = sb("gdn_cw_sb", [128, 2, 16, 4])
        self.gdn_nw = sb("gdn_nw_sb", [128, 2, 128])
        self.ssd_con = sb("ssd_con_sb", [128, 2, 48])
        self.ssd_cw = sb("ssd_cw_sb", [128, 2, 12, 5])
        self.ssd_nw = sb("ssd_nw_sb", [128, 2, 8])
        self.rope_c = sb("rope_c_sb", [128, 2])

    def kh32(self, c, tt):
        return ("h32", c, tt)

    def kh16(self, c, tt):
        return ("h16", c, tt)

    def prologue(self):
        nc, c = self.nc, self.c
        c.op('pool', lambda: nc.gpsimd.memset(self.ones32[:], 1.0), writes=['ones32'])
        c.op('pool', lambda: nc.gpsimd.memset(self.ones16[:], 1.0), writes=['ones16'])
        c.op('pool', lambda: nc.gpsimd.memset(self.epsc[:], LN_EPS / (ALPHA * ALPHA)), writes=['epsc'])
        c.op('pool', lambda: nc.gpsimd.memset(self.onec[:], 1.0), writes=['onec'])
        c.op('pool', lambda: nc.gpsimd.memset(self.c1e6[:], 1e-6), writes=['c1e6'])
        c.dma('sp', self.consts[:], self.consts_d[:, :, :], writes=['ident', 'maskL', 'maskU', 'offdiag', 'tri64', 'blk64', 'tri128', 'maskU128'])
        c.dma('sp', self.gdn_con[:], self.gdn_con_d[:, :, :], writes=['gdn_con'])
        c.dma('sp', self.gdn_cw[:], self.gdn_cw_d[:, :, :, :], writes=['gdn_cw'])
        c.dma('sp', self.gdn_nw[:], self.gdn_nw_d[:, :, :], writes=['gdn_nw'])
        c.dma('sp', self.ssd_con[:], self.ssd_con_d[:, :, :], writes=['ssd_con'])
        c.dma('sp', self.ssd_cw[:], self.ssd_cw_d[:, :, :, :], writes=['ssd_cw'])
        c.dma('sp', self.ssd_nw[:], self.ssd_nw_d[:, :, :], writes=['ssd_nw'])
        c.dma('sp', self.rope_c[:], self.rope_c_d[:, :], writes=['rope_c'])
        c.dma('sp', self.lng[:], self.ln_gT[:, :], writes=['lng'])
        c.dma('sp', self.lnb[:], self.ln_bT[:, :], writes=['lnb'])
        xv = self.xT.rearrange("(c p) t -> p c t", p=128)
        for kc in range(NKC):
            c.dma('sp' if kc % 2 == 0 else 'act', self.h32[:, kc, :], xv[:, kc, :],
                  writes=[self.kh32(kc, tt) for tt in range(T // TT)])
        for kc in range(NKC):
            for tt in range(T // TT):
                e = 'dve' if (kc + tt) % 2 == 0 else 'pool'
                eng = nc.vector if e == 'dve' else nc.gpsimd
                c.op(e, lambda: eng.tensor_copy(self.h16[:, kc, tt * TT:(tt + 1) * TT], self.h32[:, kc, tt * TT:(tt + 1) * TT]),
                     reads=[self.kh32(kc, tt)], writes=[self.kh16(kc, tt)])

    def epilogue(self):
        nc, c = self.nc, self.c
        ov = self.outT.rearrange("(c p) t -> p c t", p=128)
        keys = []
        for kc in range(NKC):
            k = ("out", kc)
            c.dma('sp', ov[:, kc, :], self.h32[:, kc, :], reads=[self.kh32(kc, tt) for tt in range(T // TT)], writes=[k])
            keys.append(k)
        c.wait_all('sp', keys)

    def layer_norm_tile(self, li, tt, ps_sum, ps_sq, scr):
        nc, c = self.nc, self.c
        ts = slice(tt * TT, (tt + 1) * TT)
        zsq, mean, rstd, tmp = scr['zsq'], scr['mean'], scr['rstd'], scr['tmp']
        eps = LN_EPS / (ALPHA * ALPHA)
        for kc in range(NKC):
            b = kc % 2
            c.op('act', lambda: nc.scalar.activation(zsq[b][:], self.h32[:, kc, ts], AF.Square),
                 reads=[self.kh32(kc, tt)], writes=[('zsq', b)])
            c.op('pe', lambda: nc.tensor.matmul(ps_sum[:], self.ones32[:], self.h32[:, kc, ts], start=(kc == 0), stop=(kc == NKC - 1)),
                 reads=['ones32', self.kh32(kc, tt)], writes=['ps_sum'], inc=False)
            c.op('pe', lambda: nc.tensor.matmul(ps_sq[:], self.ones16[:], zsq[b][:], start=(kc == 0), stop=(kc == NKC - 1)),
                 reads=['ones16', ('zsq', b)], writes=['ps_sq'], inc=True)
        c.op('act', lambda: nc.scalar.activation(mean[:], ps_sum[:], AF.Copy, scale=1.0 / D), reads=['ps_sum'], writes=['mean'])
        c.op('pool', lambda: nc.gpsimd.tensor_tensor(tmp[:], mean[:], mean[:], ALU.mult), reads=['mean'], writes=[('zc', 0)])
        c.op('dve', lambda: nc.vector.scalar_tensor_tensor(rstd[:], ps_sq[:], 1.0 / D, tmp[:], ALU.mult, ALU.subtract),
             reads=['ps_sq', ('zc', 0)], writes=['rstd'])
        c.op('act', lambda: nc.scalar.activation(rstd[:], rstd[:], AF.Sqrt, bias=self.epsc[:, 0:1], scale=1.0), reads=['rstd', 'epsc'], writes=['rstd'])
        c.op('dve', lambda: nc.vector.reciprocal(rstd[:], rstd[:]), reads=['rstd'], writes=['rstd'])
        for kc in range(NKC):
            gi = li * NKC + kc
            b = kc % 2
            zc = scr['zc'][b]
            c.op('dve', lambda: nc.vector.tensor_tensor(zc[:], self.h32[:, kc, ts], mean[:], ALU.subtract),
                 reads=[self.kh32(kc, tt), 'mean'], writes=[('zc', b)])
            c.op('dve', lambda: nc.vector.scalar_tensor_tensor(zc[:], zc[:], self.lng[:, gi:gi + 1], rstd[:], ALU.mult, ALU.mult),
                 reads=[('zc', b), 'rstd', 'lng'], writes=[('zc', b)])
            c.op('act', lambda: nc.scalar.activation(self.h32[:, kc, ts], zc[:], AF.Identity, bias=self.lnb[:, gi:gi + 1], scale=1.0),
                 reads=[('zc', b), 'lnb'], writes=[self.kh32(kc, tt)])
            c.op('pool', lambda: nc.gpsimd.tensor_scalar(self.h16[:, kc, ts], zc[:], 1.0, self.lnb[:, gi:gi + 1], ALU.mult, ALU.add),
                 reads=[('zc', b), 'lnb'], writes=[self.kh16(kc, tt)])

    def ffn(self, layer, which, li):
        nc, c = self.nc, self.c
        cres = 0.5 / ALPHA
        with ExitStack() as es:
            self._uid = getattr(self, '_uid', 0) + 1
            _u = "_%d" % self._uid
            sb = lambda name, shape, dtype=F32: es.enter_context(nc.sbuf_tensor(name + _u, shape, dtype))
            ps = lambda name, shape, dtype=F32: es.enter_context(nc.psum_tensor(name + _u, shape, dtype))
            NH = 2
            HT = T // NH
            act = sb("ffn_act", [128, NFC, HT], BF16)
            NWB = 3
            wi = [sb("ffn_wi%d" % i, [128, NKC, 512], BF16) for i in range(NWB)]
            NOB = 2
            wo = [sb("ffn_wo%d" % i, [128, NFC, 256], BF16) for i in range(NOB)]
            sg = [sb("ffn_sg%d" % i, [128, TT], BF16) for i in range(2)]
            scr = {
                'zsq': [sb("ln_zsq%d" % i, [128, TT], BF16) for i in range(2)],
                'zc': [sb("ln_zc%d" % i, [128, TT]) for i in range(2)],
                'mean': sb("ln_mean", [128, TT]), 'rstd': sb("ln_rstd", [128, TT]),
            }
            scr['tmp'] = scr['zc'][0]
            pg = [ps("ps_g%d" % i, [128, TT]) for i in range(2)]
            pu = [ps("ps_u%d" % i, [128, TT]) for i in range(2)]
            po = [ps("ps_o%d" % i, [128, TT]) for i in range(2)]
            ps_sum = ps("ps_sum", [128, TT])
            ps_sq = ps("ps_sq", [128, TT])
            w_in = self.ffn_w_in[layer, which].rearrange("(kc p) n -> p kc n", p=128)
            w_out = self.ffn_w_out[layer, which].rearrange("(fc p) n -> p fc n", p=128)
            NJB = NFC // 2
            cnt = 0
            for half in range(NH):
                for jb in range(NJB):
                    s = (half * NJB + jb) % NWB
                    kw = ('ffn_wi', s)
                    c.dma('pool', wi[s][:, :, 0:256], w_in[:, :, jb * 256:(jb + 1) * 256], writes=[kw])
                    c.dma('pool', wi[s][:, :, 256:512], w_in[:, :, DFF + jb * 256:DFF + (jb + 1) * 256], writes=[kw])
                    for jj in range(2):
                        j = jb * 2 + jj
                        for t2 in range(HT // TT):
                            tt = half * (HT // TT) + t2
                            ts = slice(tt * TT, (tt + 1) * TT)
                            b = cnt % 2
                            cnt += 1
                            for kc in range(NKC):
                                c.op('pe', lambda: nc.tensor.matmul(pg[b][:], wi[s][:, kc, jj * 128:(jj + 1) * 128], self.h16[:, kc, ts],
                                                                    start=(kc == 0), stop=(kc == NKC - 1)),
                                     reads=[kw, self.kh16(kc, tt)], writes=[('pg', b)], inc=False)
                            for kc in range(NKC):
                                c.op('pe', lambda: nc.tensor.matmul(pu[b][:], wi[s][:, kc, 256 + jj * 128:256 + (jj + 1) * 128], self.h16[:, kc, ts],
                                                                    start=(kc == 0), stop=(kc == NKC - 1)),
                                     reads=[kw, self.kh16(kc, tt)], writes=[('pu', b)], inc=(kc == NKC - 1))
                            c.op('act', lambda: nc.scalar.activation(sg[b][:], pg[b][:], AF.Silu), reads=[('pg', b)], writes=[('sg', b)])
                            c.op('dve', lambda: nc.vector.tensor_tensor(act[:, j, t2 * TT:(t2 + 1) * TT], sg[b][:], pu[b][:], ALU.mult),
                                 reads=[('sg', b), ('pu', b)], writes=[('act', j, t2)])
                for db in range(4):
                    s = (half * 4 + db) % NOB
                    kw = ('ffn_wo', s)
                    c.dma('pool', wo[s][:], w_out[:, :, db * 256:(db + 1) * 256], writes=[kw])
                    for dd in range(2):
                        dc = db * 2 + dd
                        for t2 in range(HT // TT):
                            tt = half * (HT // TT) + t2
                            ts = slice(tt * TT, (tt + 1) * TT)
                            b = cnt % 2
                            cnt += 1
                            for fc in range(NFC):
                                c.op('pe', lambda: nc.tensor.matmul(po[b][:], wo[s][:, fc, dd * 128:(dd + 1) * 128], act[:, fc, t2 * TT:(t2 + 1) * TT],
                                                                    start=(fc == 0), stop=(fc == NFC - 1)),
                                     reads=[kw, ('act', fc, t2)], writes=[('po', b)], inc=(fc == NFC - 1))
                            c.op('dve', lambda: nc.vector.scalar_tensor_tensor(self.h32[:, dc, ts], po[b][:], cres, self.h32[:, dc, ts], ALU.mult, ALU.add),
                                 reads=[('po', b), self.kh32(dc, tt)], writes=[self.kh32(dc, tt)])
                for t2 in range(HT // TT):
                    tt = half * (HT // TT) + t2
                    self.layer_norm_tile(li, tt, ps_sum, ps_sq, scr)
            c.barrier()

    def mm(self, out, lhsT, rhs, r, w, start=True, stop=True, inc=True):
        nc = self.nc
        return self.c.op('pe', lambda: nc.tensor.matmul(out, lhsT, rhs, start=start, stop=stop), reads=r, writes=w, inc=inc)

    def tr(self, out, in_, r, w, inc=True):
        nc = self.nc
        return self.c.op('pe', lambda: nc.tensor.transpose(out, in_, self.ident[:]), reads=list(r) + ['ident'], writes=w, inc=inc)

    def V(self, fn, r, w):
        return self.c.op('dve', fn, reads=r, writes=w)

    def A(self, fn, r, w):
        return self.c.op('act', fn, reads=r, writes=w)

    def P(self, fn, r, w):
        return self.c.op('pool', fn, reads=r, writes=w)

    def load_w(self, dst, src, key, q='pool'):
        self.c.dma(q, dst, src, writes=[key])

    def out_proj_acc(self, wout_rows, oT, okeys, nchunks, ps_proj, scale, wbuf, wkey):
        nc, c = self.nc, self.c
        wv = wout_rows.rearrange("(cc p) n -> p cc n", p=128)
        if 'nodma' in self.dbg:
            self.P(lambda: nc.gpsimd.memset(wbuf[:, 0:nchunks, :], 0.0), [], [wkey])
        else:
            c.dma('pool', wbuf[:, 0:nchunks, :], wv, writes=[wkey])
        i = 0
        for dc in range(NKC):
            for tt in range(T // TT):
                ts = slice(tt * TT, (tt + 1) * TT)
                pp = ps_proj[i % 2]
                pk = ('ps_proj', i % 2)
                i += 1
                for cc in range(nchunks):
                    self.mm(pp[:], wbuf[:, cc, dc * 128:(dc + 1) * 128], oT[:, cc, ts], [wkey] + okeys, [pk],
                            start=(cc == 0), stop=(cc == nchunks - 1), inc=(cc == nchunks - 1))
                self.V(lambda: nc.vector.scalar_tensor_tensor(self.h32[:, dc, ts], pp[:], scale, self.h32[:, dc, ts], ALU.mult, ALU.add),
                       [pk, self.kh32(dc, tt)], [self.kh32(dc, tt)])

    def proj_fm(self, dst, wbuf, wkey, ps_proj, dkey, col0=0):
        nc = self.nc
        for tt in range(T // TT):
            ts = slice(tt * TT, (tt + 1) * TT)
            pp = ps_proj[tt % 2]
            pk = ('ps_proj', tt % 2)
            for kc in range(NKC):
                self.mm(pp[:], wbuf[:, kc, :], self.h16[:, kc, ts], [wkey, self.kh16(kc, tt)], [pk],
                        start=(kc == 0), stop=(kc == NKC - 1), inc=(kc == NKC - 1))
            self.A(lambda: nc.scalar.copy(dst[:, col0 + tt * TT:col0 + (tt + 1) * TT], pp[:]), [pk], [dkey])

    def proj_conv(self, dst, dkey, wbuf, wkey, cw4, bias, cwkey):
        nc = self.nc
        cv = self._cv
        xp16, dgw, ps_proj = cv['xp16'], cv['dgw'], cv['ps_proj']
        self.V(lambda: nc.vector.tensor_tensor(dgw[:], self.ident.unsqueeze(1).to_broadcast([128, 4, 128]),
                                               cw4.unsqueeze(2).to_broadcast([128, 4, 128]), ALU.mult), ['ident', cwkey], ['dgw'])
        for tt in range(T // TT):
            ts = slice(tt * TT, (tt + 1) * TT)
            pp = ps_proj[tt % 2]; pk = ('ps_proj', tt % 2)
            for kc in range(NKC):
                self.mm(pp[:], wbuf[:, kc, :], self.h16[:, kc, ts], [wkey, self.kh16(kc, tt)], [pk],
                        start=(kc == 0), stop=(kc == NKC - 1), inc=(kc == NKC - 1))
            self.A(lambda: nc.scalar.copy(xp16[:, 3 + tt * TT:3 + (tt + 1) * TT], pp[:]), [pk], [('xp', tt)])
            pc, pck = cv['psc'][tt % 2]
            rd = [('xp', tt)] + ([('xp', tt - 1)] if tt > 0 else [])
            for j in range(4):
                self.mm(pc[:], dgw[:, j, :], xp16[:, tt * TT + j:tt * TT + j + TT], ['dgw'] + rd, [pck], start=(j == 0), stop=(j == 3), inc=(j == 3))
            if bias is None:
                self.A(lambda: nc.scalar.activation(dst[:, ts], pc[:], AF.Silu), [pck], [dkey])
            else:
                self.A(lambda: nc.scalar.activation(dst[:, ts], pc[:], AF.Silu, bias=bias, scale=1.0), [pck, cwkey], [dkey])

    def conv_silu(self, dst, xp, cw, acc, r, w, bias=None):
        nc = self.nc
        self.V(lambda: nc.vector.tensor_scalar(acc[:], xp[:, 3:3 + T], cw[:, 3:4], None, ALU.mult), r, ['convacc'])
        for j in (2, 1, 0):
            self.V(lambda: nc.vector.scalar_tensor_tensor(acc[:], xp[:, j:j + T], cw[:, j:j + 1], acc[:], ALU.mult, ALU.add), r + ['convacc'], ['convacc'])
        if bias is None:
            self.A(lambda: nc.scalar.activation(dst, acc[:], AF.Silu), ['convacc'], w)
        else:
            self.A(lambda: nc.scalar.activation(dst, acc[:], AF.Silu, bias=bias, scale=1.0), ['convacc'] + r, w)

    def hybrid(self, layer):
        nc, c = self.nc, self.c
        i = layer // 2
        W = self.hyb_w_in[i]
        Wv = W.rearrange("(kc p) n -> p kc n", p=128)
        cres = 1.0 / ALPHA
        NS = T // 128
        with ExitStack() as es:
            self._uid = getattr(self, '_uid', 0) + 1
            _u = "_%d" % self._uid
            sb = lambda name, shape, dtype=F32: es.enter_context(nc.sbuf_tensor(name + _u, shape, dtype))
            ps = lambda name, shape, dtype=F32: es.enter_context(nc.psum_tensor(name + _u, shape, dtype))
            banks = [ps("psb%d" % k, [128, 4, 128]) for k in range(8)]
            flat = lambda t_: t_[:].rearrange("p a b -> p (a b)")
            ps_proj = [flat(banks[0]), flat(banks[1])]
            ps_misc = flat(banks[2])
            psq = banks[3:8]
            c.alias = {('ps_proj', 0): ('psb', 0), ('ps_proj', 1): ('psb', 1), 'ps_misc': ('psb', 2)}
            for k in range(5):
                c.alias[('psq', k)] = ('psb', 3 + k)
            wb = [sb("hw%d" % k, [128, NKC, 128], BF16) for k in range(4)]
            xp = sb("xp", [128, 3 + T])
            acc = sb("convacc", [128, 1024])
            self.P(lambda: nc.gpsimd.memset(xp[:, 0:3], 0.0), [], [('xp', 0)])
            dgw = sb("dgw", [128, 4, 128], BF16)
            self._cv = {'xp16': xp[:].bitcast(BF16), 'dgw': dgw, 'ps_proj': ps_proj,
                        'psc': [(flat(banks[3]), ('psq', 0)), (flat(banks[4]), ('psq', 1))]}
            wba = sb("wba", [128, NKC, 16], BF16)
            self.load_w(wba[:], Wv[:, :, 3072:3088], 'wba')
            ba = sb("ba_tok", [128, NS, 16])
            pm = ps_misc[:, 0:NS * 16].rearrange("p (s n) -> p s n", n=16)
            for s in range(NS):
                for kc in range(NKC):
                    self.mm(pm[:, s, :], self.h16[:, kc, s * 128:(s + 1) * 128], wba[:, kc, :], ['wba', self.kh16(kc, s // 4)], ['ps_misc'],
                            start=(kc == 0), stop=(kc == NKC - 1), inc=(kc == NKC - 1))
            self.A(lambda: nc.scalar.copy(ba[:], pm), ['ps_misc'], ['ba'])
            beta = sb("beta", [128, NS, 8])
            gg = sb("g_tok", [128, NS, 8])
            gc = sb("gc_tok", [128, NS, 8])
            ebg = sb("ebg", [128, NS, 8])
            ekd = sb("ekd", [128, NS, 8])
            ngc = sb("ngc", [128, NS, 8])
            tmpa = sb("tmpa", [128, NS, 8])
            gcon = self.gdn_con[:, i, :]
            alog = gcon[:, 0:8].unsqueeze(1).to_broadcast([128, NS, 8])
            dtb = gcon[:, 8:16].unsqueeze(1).to_broadcast([128, NS, 8])
            self.A(lambda: nc.scalar.activation(beta[:], ba[:, :, 0:8], AF.Exp, scale=-1.0), ['ba'], ['beta'])
            self.V(lambda: nc.vector.tensor_scalar(beta[:], beta[:], 1.0, None, ALU.add), ['beta'], ['beta'])
            self.V(lambda: nc.vector.reciprocal(beta[:], beta[:]), ['beta'], ['beta'])
            self.V(lambda: nc.vector.tensor_tensor(gg[:], ba[:, :, 8:16], dtb, ALU.add), ['ba', 'gdn_con'], ['gg'])
            self.A(lambda: nc.scalar.activation(gg[:], gg[:], AF.Exp), ['gg'], ['gg'])
            self.A(lambda: nc.scalar.activation(gg[:], gg[:], AF.Ln, bias=self.onec[:, 0:1], scale=1.0), ['gg', 'onec'], ['gg'])
            self.A(lambda: nc.scalar.activation(tmpa[:], alog, AF.Exp), ['gdn_con'], ['tmpa'])
            self.V(lambda: nc.vector.scalar_tensor_tensor(gg[:], gg[:], -1.0, tmpa[:], ALU.mult, ALU.mult), ['gg', 'tmpa'], ['gg'])
            g2 = gg[:].rearrange("p s n -> p (s n)")
            pm2 = ps_misc[:, 0:NS * 8]
            self.mm(pm2, self.tri64[:], g2, ['tri64', 'gg'], ['ps_misc'])
            self.A(lambda: nc.scalar.copy(gc[:].rearrange("p s n -> p (s n)"), pm2), ['ps_misc'], ['gc'])
            self.mm(pm2, self.blk64[:], g2, ['blk64', 'gg'], ['ps_misc'])
            self.V(lambda: nc.vector.tensor_tensor(ekd[:].rearrange("p s n -> p (s n)"), pm2, gc[:].rearrange("p s n -> p (s n)"), ALU.subtract),
                   ['ps_misc', 'gc'], ['ekd'])
            self.A(lambda: nc.scalar.activation(ekd[:], ekd[:], AF.Exp), ['ekd'], ['ekd'])
            self.A(lambda: nc.scalar.activation(ebg[:], gc[:], AF.Exp), ['gc'], ['ebg'])
            self.V(lambda: nc.vector.tensor_tensor(ebg[:], ebg[:], beta[:], ALU.mult), ['ebg', 'beta'], ['ebg'])
            self.V(lambda: nc.vector.tensor_scalar(ngc[:], gc[:], -1.0, None, ALU.mult), ['gc'], ['ngc'])
            with ExitStack() as es2:
                sb2 = lambda name, shape, dtype=F32: es2.enter_context(nc.sbuf_tensor(name + _u, shape, dtype))
                qT = sb2("qT", [128, T]); kT = sb2("kT", [128, T])
                vTs = [sb2("vT%d" % x, [128, T]) for x in range(2)]
                oTg = sb2("goT", [128, 2, T], BF16); woutg = sb2("gwout", [128, 2, D], BF16)
                sq = acc[:, 0:TT]; rinv = acc[:, TT:2 * TT]
                names = ["dg", "t1", "Dl", "Du", "Amat", "ATm", "TTa", "TTb", "Pa", "PTa", "Pb", "PTb", "u", "eRB", "og", "sz"]
                hnames = ["attnT", "kbe", "vb", "kdec", "wTe", "wTo", "qdTe", "qdTo", "vnew", "TT16"]
                Bs = [{n: sb2("g%d_" % x + n, [128, 128]) for n in names} for x in range(2)]
                for x in range(2):
                    for n in hnames:
                        Bs[x][n] = sb2("g%d_" % x + n, [128, 128], BF16)
                Ss = [[sb2("g%d_S%d" % (x, k), [128, 128]) for k in range(3)] for x in range(2)]
                Shs = [[sb2("g%d_Sh%d" % (x, k), [128, 128], BF16) for k in range(3)] for x in range(2)]
                KKs = sb2("g_KKs", [128, 128]); KQs = sb2("g_KQs", [128, 128])
                cols = [{n: sb2("g%d_c" % x + n, [128, 1]) for n in ["ss", "ri"]} for x in range(2)]
                for x in range(2):
                    for n in ["wTe", "wTo", "qdTe", "qdTo"]:
                        self.P(lambda: nc.gpsimd.memset(Bs[x][n][:], 0.0), [], [(n, x)])
                cw = self.gdn_cw[:, i]

                def chain(x, hv):
                    B_ = Bs[x]; Sst = Ss[x]; Sh = Shs[x]; col = cols[x]; vT = vTs[x]; wz = wb[x]; wzk = 'wb%d' % x
                    bb = 4 * x
                    K = lambda n: (n, x)
                    def slot(n):
                        if n >= 16:
                            return banks[bb][:, n - 16, :], ('psb', bb)
                        return banks[bb + n // 4][:, n % 4, :], ('psb', bb + n // 4)
                    self.P(lambda: nc.gpsimd.memset(Sst[0][:], 0.0), [], [('S', x, 0)])
                    self.P(lambda: nc.gpsimd.memset(Sh[0][:], 0.0), [], [('Sh', x, 0)])
                    yield
                    sidx = 0
                    for s in range(NS):
                        sp_ = slice(s * 128, (s + 1) * 128)
                        gcc = gc[:, s, hv:hv + 1]; ngcc = ngc[:, s, hv:hv + 1]; betc = beta[:, s, hv:hv + 1]
                        ebgc = ebg[:, s, hv:hv + 1]; ekdc = ekd[:, s, hv:hv + 1]
                        sc_keys = ['gc', 'ngc', 'beta', 'ebg', 'ekd']
                        pKK, kKK = slot(0); pKQ, kKQ = slot(1); pRB, kRB = slot(2); pT1, kT1 = slot(3)
                        if x == 0:
                            self.mm(pKK, kT[:, sp_], kT[:, sp_], ['kT'], [kKK])
                            self.mm(pKQ, kT[:, sp_], qT[:, sp_], ['kT', 'qT'], [kKQ])
                            self.A(lambda: nc.scalar.copy(KKs[:], pKK), [kKK], ['KKs'])
                            self.A(lambda: nc.scalar.copy(KQs[:], pKQ), [kKQ], ['KQs'])
                        self.V(lambda: nc.vector.tensor_scalar(B_['dg'][:], self.ident[:], gcc, None, ALU.mult), ['ident'] + sc_keys, [K('dg')])
                        yield
                        self.mm(pRB, self.ones32[:], B_['dg'][:], ['ones32', K('dg')], [kRB])
                        yield
                        self.V(lambda: nc.vector.scalar_tensor_tensor(B_['t1'][:], pRB, -1.0, self.maskL[:], ALU.mult, ALU.add), [kRB, 'maskL'], [K('t1')])
                        yield
                        self.A(lambda: nc.scalar.activation(B_['Dl'][:], B_['t1'][:], AF.Exp, bias=gcc, scale=1.0), [K('t1')] + sc_keys, [K('Dl')])
                        yield
                        self.V(lambda: nc.vector.scalar_tensor_tensor(B_['t1'][:], pRB, 1.0, self.maskU[:], ALU.mult, ALU.add), [kRB, 'maskU'], [K('t1')])
                        yield
                        self.A(lambda: nc.scalar.activation(B_['Du'][:], B_['t1'][:], AF.Exp, bias=ngcc, scale=1.0), [K('t1')] + sc_keys, [K('Du')])
                        self.A(lambda: nc.scalar.activation(B_['eRB'][:], pRB, AF.Exp), [kRB], [K('eRB')])
                        yield
                        self.V(lambda: nc.vector.scalar_tensor_tensor(B_['Amat'][:], KKs[:], betc, B_['Dl'][:], ALU.mult, ALU.mult), ['KKs', K('Dl')] + sc_keys, [K('Amat')])
                        yield
                        self.V(lambda: nc.vector.tensor_tensor(B_['attnT'][:], KQs[:], B_['Du'][:], ALU.mult), ['KQs', K('Du')], [K('attnT')])
                        self.tr(pT1, B_['Amat'][:], [K('Amat')], [kT1])
                        yield
                        self.A(lambda: nc.scalar.copy(B_['ATm'][:], pT1), [kT1], [K('ATm')])
                        self.V(lambda: nc.vector.tensor_tensor(B_['TTa'][:], self.ident[:], pT1, ALU.subtract), ['ident', kT1], [K('TTa')])
                        yield
                        Pc, PTc, TTc = 'Amat', 'ATm', 'TTa'
                        for lvl in range(5):
                            Pn = 'Pa' if lvl % 2 == 0 else 'Pb'
                            PTn = 'PTa' if lvl % 2 == 0 else 'PTb'
                            TTn = 'TTb' if TTc == 'TTa' else 'TTa'
                            p1, k1 = slot(4 + (lvl % 2) * 3); p2, k2 = slot(5 + (lvl % 2) * 3); p3, k3 = slot(6 + (lvl % 2) * 3)
                            self.mm(p1, B_[PTc][:], B_[Pc][:], [K(PTc), K(Pc)], [k1])
                            if lvl < 4:
                                self.mm(p2, B_[Pc][:], B_[PTc][:], [K(PTc), K(Pc)], [k2])
                            yield
                            self.A(lambda: nc.scalar.copy(B_[Pn][:], p1), [k1], [K(Pn)])
                            if lvl < 4:
                                self.V(lambda: nc.vector.tensor_copy(B_[PTn][:], p2), [k2], [K(PTn)])
                            yield
                            self.mm(p3, B_[Pn][:], B_[TTc][:], [K(Pn), K(TTc)], [k3])
                            yield
                            self.V(lambda: nc.vector.tensor_tensor(B_[TTn][:], p3, B_[TTc][:], ALU.add), [k3, K(TTc)], [K(TTn)])
                            yield
                            Pc, PTc, TTc = Pn, PTn, TTn
                        pk_, kk_ = slot(10); pv_, kv_ = slot(11)
                        self.tr(pk_, kT[:, sp_], ['kT'], [kk_])
                        self.tr(pv_, vT[:, sp_], [('vT', x)], [kv_])
                        yield
                        self.A(lambda: nc.scalar.activation(B_['kbe'][:], pk_, AF.Identity, scale=ebgc), [kk_] + sc_keys, [K('kbe')])
                        self.A(lambda: nc.scalar.activation(B_['kdec'][:], pk_, AF.Identity, scale=ekdc), [kk_] + sc_keys, [K('kdec')])
                        self.V(lambda: nc.vector.tensor_scalar(B_['vb'][:], pv_, betc, None, ALU.mult), [kv_] + sc_keys, [K('vb')])
                        yield
                        pu_, ku_ = slot(12); pw_, kw_ = slot(13)
                        self.A(lambda: nc.scalar.copy(B_['TT16'][:], B_[TTc][:]), [K(TTc)], [K('TT16')])
                        yield
                        self.mm(pu_, B_['TT16'][:], B_['vb'][:], [K('TT16'), K('vb')], [ku_])
                        self.mm(pw_, B_['kbe'][:], B_['TT16'][:], [K('TT16'), K('kbe')], [kw_])
                        yield
                        self.A(lambda: nc.scalar.copy(B_['u'][:], pu_), [ku_], [K('u')])
                        self.V(lambda: nc.vector.tensor_copy(B_['wTe'][:, 0:64], pw_[:, 0:64]), [kw_], [K('wTe')])
                        self.V(lambda: nc.vector.tensor_copy(B_['wTo'][:, 64:128], pw_[:, 64:128]), [kw_], [K('wTo')])
                        yield
                        self.V(lambda: nc.vector.tensor_tensor(B_['qdTe'][:, 0:64], qT[:, s * 128:s * 128 + 64], B_['eRB'][:, 0:64], ALU.mult), ['qT', K('eRB')], [K('qdTe')])
                        self.V(lambda: nc.vector.tensor_tensor(B_['qdTo'][:, 64:128], qT[:, s * 128 + 64:(s + 1) * 128], B_['eRB'][:, 64:128], ALU.mult), ['qT', K('eRB')], [K('qdTo')])
                        yield
                        S0 = Sst[sidx % 3]; S1 = Sst[(sidx + 1) % 3]; S2 = Sst[(sidx + 2) % 3]
                        kS0 = ('S', x, sidx % 3); kS1 = ('S', x, (sidx + 1) % 3); kS2 = ('S', x, (sidx + 2) % 3)
                        H0 = Sh[sidx % 3]; H1 = Sh[(sidx + 1) % 3]; H2 = Sh[(sidx + 2) % 3]
                        kH0 = ('Sh', x, sidx % 3); kH1 = ('Sh', x, (sidx + 1) % 3); kH2 = ('Sh', x, (sidx + 2) % 3)
                        pa, ka = slot(14); pb, kb = slot(15); pc_, kc_ = slot(16); pd, kd = slot(17); po_, ko_ = slot(18)
                        self.mm(pa, B_['wTe'][:], H0[:], [K('wTe'), kH0], [ka])
                        yield
                        self.V(lambda: nc.vector.tensor_tensor(B_['vnew'][0:64, :], B_['u'][0:64, :], pa[0:64, :], ALU.subtract), [K('u'), ka], [K('vnew')])
                        yield
                        self.mm(pb, B_['kdec'][0:64, :], B_['vnew'][0:64, :], [K('kdec'), K('vnew')], [kb])
                        yield
                        self.V(lambda: nc.vector.scalar_tensor_tensor(S1[:], S0[:], B_['eRB'][:, 63:64], pb, ALU.mult, ALU.add), [kS0, K('eRB'), kb], [kS1])
                        self.A(lambda: nc.scalar.copy(H1[:], S1[:]), [kS1], [kH1])
                        yield
                        self.mm(pc_, B_['wTo'][:], H1[:], [K('wTo'), kH1], [kc_])
                        yield
                        self.V(lambda: nc.vector.tensor_tensor(B_['vnew'][64:128, :], B_['u'][64:128, :], pc_[64:128, :], ALU.subtract), [K('u'), kc_], [K('vnew')])
                        yield
                        self.mm(pd, B_['kdec'][64:128, :], B_['vnew'][64:128, :], [K('kdec'), K('vnew')], [kd])
                        yield
                        self.V(lambda: nc.vector.scalar_tensor_tensor(S2[:], S1[:], B_['eRB'][:, 127:128], pd, ALU.mult, ALU.add), [kS1, K('eRB'), kd], [kS2])
                        self.A(lambda: nc.scalar.copy(H2[:], S2[:]), [kS2], [kH2])
                        self.mm(po_, B_['qdTe'][:], H0[:], [K('qdTe'), kH0], [ko_], start=True, stop=False, inc=False)
                        self.mm(po_, B_['qdTo'][:], H1[:], [K('qdTo'), kH1], [ko_], start=False, stop=False, inc=False)
                        self.mm(po_, B_['attnT'][:], B_['vnew'][:], [K('attnT'), K('vnew')], [ko_], start=False, stop=True, inc=True)
                        sidx += 2
                        yield
                        pz, kz = slot(19)
                        for kc in range(NKC):
                            self.mm(pz, self.h16[:, kc, sp_], wz[:, kc, :], [wzk, self.kh16(kc, s // 4)], [kz], start=(kc == 0), stop=(kc == NKC - 1), inc=(kc == NKC - 1))
                        yield
                        self.A(lambda: nc.scalar.activation(B_['sz'][:], pz, AF.Silu), [kz], [K('sz')])
                        self.A(lambda: nc.scalar.activation(B_['og'][:], po_, AF.Square, accum_out=col['ss'][:]), [ko_], [K('og'), K('ss')])
                        yield
                        self.A(lambda: nc.scalar.activation(col['ri'][:], col['ss'][:], AF.Sqrt, bias=self.c1e6[:, 0:1], scale=1.0 / 128), [K('ss'), 'c1e6'], [K('ri')])
                        yield
                        self.V(lambda: nc.vector.reciprocal(col['ri'][:], col['ri'][:]), [K('ri')], [K('ri')])
                        self.V(lambda: nc.vector.scalar_tensor_tensor(B_['og'][:], po_, col['ri'][:, 0:1], self.gdn_nw[:, i, :], ALU.mult, ALU.mult), [ko_, K('ri'), 'gdn_nw'], [K('og')])
                        self.V(lambda: nc.vector.tensor_tensor(B_['og'][:], B_['og'][:], B_['sz'][:], ALU.mult), [K('og'), K('sz')], [K('og')])
                        yield
                        self.tr(pz, B_['og'][:], [K('og')], [kz])
                        yield
                        self.A(lambda: nc.scalar.copy(oTg[:, x, sp_], pz), [kz], [('goT', x, s // 4)])
                        yield

                for hq in range(4):
                    hvs = (2 * hq, 2 * hq + 1)
                    self.load_w(wb[0][:], Wv[:, :, hq * 128:(hq + 1) * 128], 'wb0')
                    self.load_w(wb[1][:], Wv[:, :, 512 + hq * 128:512 + (hq + 1) * 128], 'wb1')
                    self.load_w(wb[2][:], Wv[:, :, 1024 + hvs[0] * 128:1024 + (hvs[0] + 1) * 128], 'wb2')
                    self.load_w(wb[3][:], Wv[:, :, 1024 + hvs[1] * 128:1024 + (hvs[1] + 1) * 128], 'wb3')
                    for (dst, dk_, wi_, cchunk, l2, sc) in ((qT, 'qT', 0, hq, True, 128.0 ** -0.5), (kT, 'kT', 1, 4 + hq, True, 1.0),
                                                            (vTs[0], ('vT', 0), 2, 8 + hvs[0], False, 1.0), (vTs[1], ('vT', 1), 3, 8 + hvs[1], False, 1.0)):
                        self.proj_conv(dst, dk_, wb[wi_], 'wb%d' % wi_, cw[:, cchunk, 0:4], None, 'gdn_cw')
                        if l2:
                            for tt in range(T // TT):
                                ts = slice(tt * TT, (tt + 1) * TT)
                                self.A(lambda: nc.scalar.activation(sq, dst[:, ts], AF.Square), [dk_], ['convacc'])
                                self.mm(ps_misc[:], self.ones32[:], sq, ['ones32', 'convacc'], ['ps_misc'])
                                self.A(lambda: nc.scalar.activation(rinv, ps_misc[:], AF.Sqrt, bias=self.c1e6[:, 0:1], scale=1.0), ['ps_misc', 'c1e6'], ['convacc'])
                                self.V(lambda: nc.vector.reciprocal(rinv, rinv), ['convacc'], ['convacc'])
                                self.V(lambda: nc.vector.scalar_tensor_tensor(dst[:, ts], dst[:, ts], sc, rinv, ALU.mult, ALU.mult), [dk_, 'convacc'], [dk_])
                    self.load_w(wb[0][:], Wv[:, :, 2048 + hvs[0] * 128:2048 + (hvs[0] + 1) * 128], 'wb0')
                    self.load_w(wb[1][:], Wv[:, :, 2048 + hvs[1] * 128:2048 + (hvs[1] + 1) * 128], 'wb1')
                    gens = [chain(0, hvs[0]), chain(1, hvs[1])]
                    while gens:
                        for g_ in list(gens):
                            try:
                                next(g_)
                            except StopIteration:
                                gens.remove(g_)
                    self.out_proj_acc(self.hyb_w_out[i, hvs[0] * 128:(hvs[0] + 2) * 128, :], oTg,
                                      [('goT', x, k) for x in range(2) for k in range(4)], 2, ps_proj, cres, woutg, 'gwout')
            c.barrier()
            wout = sb("hwout", [128, 4, D], BF16)
            oT = sb("hoT", [128, 4, T], BF16)
            self._banks1 = banks[1]
            self.ssd(layer, es, ps_proj, ps_misc, psq, wb, wout, oT, xp, acc, Wv)
            c.barrier()
            c.alias = {}

    def ssd(self, layer, es, ps_proj, ps_misc, psq, wb, wout, oT, xp, acc, Wv):
        nc, c = self.nc, self.c
        i = layer // 2
        NS = T // 128
        cres = 1.0 / ALPHA
        def slot(n):
            return psq[n // 4][:, n % 4, :], ('psq', n // 4)
        banks1 = self._banks1
        _u = "_s%d" % layer
        sb = lambda name, shape, dtype=F32: es.enter_context(nc.sbuf_tensor(name + _u, shape, dtype))
        wdt = sb("wdt", [128, NKC, 16], BF16)
        self.load_w(wdt[:], Wv[:, :, 5648:5664], 'wdt')
        dt_ = sb("dt_tok", [128, NS, 16]); acs = sb("acs", [128, NS, 16]); nacs = sb("nacs", [128, NS, 16])
        edl = sb("edl", [128, NS, 16]); adt = sb("adt", [128, NS, 16]); ea = sb("ea", [128, NS, 16])
        pm = ps_misc[:, 0:NS * 16].rearrange("p (s n) -> p s n", n=16)
        for s in range(NS):
            for kc in range(NKC):
                self.mm(pm[:, s, :], self.h16[:, kc, s * 128:(s + 1) * 128], wdt[:, kc, :], ['wdt', self.kh16(kc, s // 4)], ['ps_misc'],
                        start=(kc == 0), stop=(kc == NKC - 1), inc=(kc == NKC - 1))
        scon = self.ssd_con[:, i, :]
        alog = scon[:, 0:16].unsqueeze(1).to_broadcast([128, NS, 16])
        dtb = scon[:, 16:32].unsqueeze(1).to_broadcast([128, NS, 16])
        self.V(lambda: nc.vector.tensor_tensor(dt_[:], pm, dtb, ALU.add), ['ps_misc', 'ssd_con'], ['dt'])
        self.A(lambda: nc.scalar.activation(dt_[:], dt_[:], AF.Exp), ['dt'], ['dt'])
        self.A(lambda: nc.scalar.activation(dt_[:], dt_[:], AF.Ln, bias=self.onec[:, 0:1], scale=1.0), ['dt', 'onec'], ['dt'])
        self.A(lambda: nc.scalar.activation(ea[:], alog, AF.Exp), ['ssd_con'], ['ea'])
        self.V(lambda: nc.vector.scalar_tensor_tensor(adt[:], dt_[:], -1.0, ea[:], ALU.mult, ALU.mult), ['dt', 'ea'], ['adt'])
        f2 = lambda t_: t_[:].rearrange("p s n -> p (s n)")
        pm2 = ps_misc[:, 0:NS * 16]
        self.mm(pm2, self.tri128[:], f2(adt), ['tri128', 'adt'], ['ps_misc'])
        self.A(lambda: nc.scalar.copy(f2(acs), pm2), ['ps_misc'], ['acs'])
        self.mm(pm2, self.ones32[:], f2(adt), ['ones32', 'adt'], ['ps_misc'])
        self.V(lambda: nc.vector.tensor_tensor(f2(edl), pm2, f2(acs), ALU.subtract), ['ps_misc', 'acs'], ['edl'])
        self.A(lambda: nc.scalar.activation(edl[:], edl[:], AF.Exp), ['edl'], ['edl'])
        self.V(lambda: nc.vector.tensor_scalar(nacs[:], acs[:], -1.0, None, ALU.mult), ['acs'], ['nacs'])
        sck = ['dt', 'acs', 'nacs', 'edl']
        xT = sb("xT16", [128, 4, T], BF16)
        BT = sb("BT", [128, T]); CT = sb("CT", [128, T])
        names = ["dg", "t1", "Du", "MT", "eRB", "CdT"]
        Bs = [{n: sb("s%d_" % x + n, [128, 128]) for n in names} for x in range(2)]
        for x in (2, 3):
            o = 128 + (x - 2) * 896
            Bs.append({n: xp[:, o + k_ * 128:o + (k_ + 1) * 128] for k_, n in enumerate(names)})
        Btok = sb("s_Btok", [128, 128]); xc32 = sb("s_xc32", [128, 128]); CBs = sb("s_CBs", [128, 128])
        xtok = acc[:, 512:1024]; szb = acc[:, 0:512]
        ygrp = sb("s_ygrp", [128, 512])
        xdts = [sb("s_xdt%d" % x, [128, 64]) for x in range(2)]; xdtds = [sb("s_xdtd%d" % x, [128, 64]) for x in range(2)]
        for x in (2, 3):
            o = 128 + (x - 2) * 896 + 768
            xdts.append(xp[:, o:o + 64]); xdtds.append(xp[:, o + 64:o + 128])
        chain_banks = [(psq[2], ('psq', 2)), (psq[3], ('psq', 3)), (psq[4], ('psq', 4)), (banks1, ('ps_proj', 1))]
        ST = sb("s_ST", [128, 8, 64])
        col = {n: sb("s_c" + n, [128, 1]) for n in ["ss", "ri"]}
        wz = wout[:].rearrange("p a b -> p (a b)").rearrange("p (k n) -> p k n", n=512)
        cw = self.ssd_cw[:, i]
        for g in range(2):
            for (dst, dk_, col0, cchunk) in ((BT, 'BT', 5136 + g * 128, 8 + g), (CT, 'CT', 5392 + g * 128, 10 + g)):
                self.load_w(wb[0][:], Wv[:, :, col0:col0 + 128], 'wb0')
                self.proj_conv(dst, dk_, wb[0], 'wb0', cw[:, cchunk, 0:4], cw[:, cchunk, 4:5], 'ssd_cw')
            for cc in range(4):
                k = cc % 2
                col0 = 4112 + (g * 4 + cc) * 128
                self.load_w(wb[1 + k][:], Wv[:, :, col0:col0 + 128], 'wb%d' % (1 + k))
                self.proj_conv(xT[:, cc, :], ('xT', cc), wb[1 + k], 'wb%d' % (1 + k), cw[:, g * 4 + cc, 0:4], cw[:, g * 4 + cc, 4:5], 'ssd_cw')
            c.dma('pool', wz, Wv[:, :, 3088 + g * 512:3088 + (g + 1) * 512], writes=['hwout'])
            self.P(lambda: nc.gpsimd.memset(ST[:], 0.0), [], [('ST', e_) for e_ in range(8)])
            c.barrier()

            def hchain(x, s, heads):
                B_ = Bs[x]; xdt = xdts[x]; xdtd = xdtds[x]
                K = lambda n: (n, x)
                sp_ = slice(s * 128, (s + 1) * 128)
                bank, kb_ = chain_banks[x]
                pRB = bank[:, 0, :]; py = bank[:, 1, :]; pS = bank[:, 2, :]
                for e in heads:
                    h = g * 8 + e
                    hs = slice(e * 64, (e + 1) * 64)
                    acc_ = acs[:, s, h:h + 1]; nacc = nacs[:, s, h:h + 1]; dtc = dt_[:, s, h:h + 1]; edc = edl[:, s, h:h + 1]
                    self.V(lambda: nc.vector.tensor_scalar(B_['dg'][:], self.ident[:], acc_, None, ALU.mult), ['ident'] + sck, [K('dg')])
                    self.V(lambda: nc.vector.tensor_scalar(xdt[:], xtok[:, hs], dtc, None, ALU.mult), ['xtok'] + sck, [K('xdt')])
                    yield
                    self.mm(pRB, self.ones32[:], B_['dg'][:], ['ones32', K('dg')], [kb_])
                    self.V(lambda: nc.vector.tensor_scalar(xdtd[:], xdt[:], edc, None, ALU.mult), [K('xdt')] + sck, [K('xdtd')])
                    yield
                    self.V(lambda: nc.vector.scalar_tensor_tensor(B_['t1'][:], pRB, 1.0, self.maskU128[:], ALU.mult, ALU.add), [kb_, 'maskU128'], [K('t1')])
                    self.A(lambda: nc.scalar.activation(B_['eRB'][:], pRB, AF.Exp), [kb_], [K('eRB')])
                    yield
                    self.A(lambda: nc.scalar.activation(B_['Du'][:], B_['t1'][:], AF.Exp, bias=nacc, scale=1.0), [K('t1')] + sck, [K('Du')])
                    self.V(lambda: nc.vector.tensor_tensor(B_['CdT'][:], CT[:, sp_], B_['eRB'][:], ALU.mult), ['CT', K('eRB')], [K('CdT')])
                    yield
                    self.V(lambda: nc.vector.tensor_tensor(B_['MT'][:], CBs[:], B_['Du'][:], ALU.mult), ['CBs', K('Du')], [K('MT')])
                    yield
                    self.mm(py[:, 0:64], B_['MT'][:], xdt[:], [K('MT'), K('xdt')], [kb_], start=True, stop=False, inc=False)
                    self.mm(py[:, 0:64], B_['CdT'][:], ST[:, e, :], [K('CdT'), ('ST', e)], [kb_], start=False, stop=True, inc=True)
                    self.mm(pS[:, 0:64], Btok[:], xdtd[:], ['Btok', K('xdtd')], [kb_])
                    yield
                    self.V(lambda: nc.vector.scalar_tensor_tensor(ygrp[:, hs], xtok[:, hs], self.ssd_con[:, i, 32 + h:33 + h], py[:, 0:64], ALU.mult, ALU.add),
                           ['xtok', 'ssd_con', kb_], [('ygrp', e)])
                    self.V(lambda: nc.vector.scalar_tensor_tensor(ST[:, e, :], ST[:, e, :], B_['eRB'][:, 127:128], pS[:, 0:64], ALU.mult, ALU.add),
                           [('ST', e), K('eRB'), kb_], [('ST', e)])
                    yield

            for s in range(NS):
                sp_ = slice(s * 128, (s + 1) * 128)
                pCB, kCB = slot(0); pBt, kBt = slot(1)
                self.mm(pCB, BT[:, sp_], CT[:, sp_], ['BT', 'CT'], [kCB])
                self.tr(pBt, BT[:, sp_], ['BT'], [kBt])
                self.A(lambda: nc.scalar.copy(CBs[:], pCB), [kCB], ['CBs'])
                self.A(lambda: nc.scalar.copy(Btok[:], pBt), [kBt], ['Btok'])
                for cc in range(4):
                    px, kx = slot(4 + cc)
                    self.V(lambda: nc.vector.tensor_copy(xc32[:], xT[:, cc, sp_]), [('xT', cc)], ['xc32'])
                    self.tr(px, xc32[:], ['xc32'], [kx])
                    self.A(lambda: nc.scalar.copy(xtok[:, cc * 128:(cc + 1) * 128], px), [kx], ['xtok'])
                self.run_chains([hchain(x_, s, range(x_, 8, 4)) for x_ in range(4)])
                ygk = [('ygrp', e_) for e_ in range(8)]
                pz = ps_proj[0]; kz = ('ps_proj', 0)
                for kc in range(NKC):
                    self.mm(pz[:], self.h16[:, kc, sp_], wz[:, kc, :], ['hwout', self.kh16(kc, s // 4)], [kz], start=(kc == 0), stop=(kc == NKC - 1), inc=(kc == NKC - 1))
                self.A(lambda: nc.scalar.activation(szb, pz[:], AF.Silu), [kz], ['szb'])
                self.V(lambda: nc.vector.tensor_tensor(ygrp[:], ygrp[:], szb, ALU.mult), ygk + ['szb'], ygk)
                self.A(lambda: nc.scalar.activation(szb, ygrp[:], AF.Square, accum_out=col['ss'][:]), ygk, ['szb', 'ss'])
                self.A(lambda: nc.scalar.activation(col['ri'][:], col['ss'][:], AF.Sqrt, bias=self.c1e6[:, 0:1], scale=1.0 / 512), ['ss', 'c1e6'], ['ri'])
                self.V(lambda: nc.vector.reciprocal(col['ri'][:], col['ri'][:]), ['ri'], ['ri'])
                self.V(lambda: nc.vector.tensor_scalar(ygrp[:], ygrp[:], col['ri'][:, 0:1], None, ALU.mult), ygk + ['ri'], ygk)
                for cc in range(4):
                    pt, kt = slot(4 + cc)
                    self.tr(pt, ygrp[:, cc * 128:(cc + 1) * 128], ygk, [kt])
                    self.A(lambda: nc.scalar.activation(oT[:, cc, sp_], pt, AF.Identity, scale=self.ssd_nw[:, i, g * 4 + cc:g * 4 + cc + 1]),
                           [kt, 'ssd_nw'], [('oT', s // 4)])
            self.out_proj_acc(self.hyb_w_out[i, 1024 + g * 512:1024 + (g + 1) * 512, :], oT, [('oT', k) for k in range(4)], 4, ps_proj, cres, wout, 'hwout')
            c.barrier()

    def nsa(self, layer):
        nc, c = self.nc, self.c
        i = layer // 2
        NS = T // 128
        NT = T // TT
        cres = 1.0 / ALPHA
        W = self.nsa_w_in[i].rearrange("(kc p) n -> p kc n", p=128)
        Wsw = self.nsa_w_sw[i].rearrange("(kc p) n -> p kc n", p=128)
        with ExitStack() as es:
            self._uid = getattr(self, '_uid', 0) + 1
            _u = "_%d" % self._uid
            sb = lambda name, shape, dtype=F32: es.enter_context(nc.sbuf_tensor(name + _u, shape, dtype))
            ps = lambda name, shape, dtype=F32: es.enter_context(nc.psum_tensor(name + _u, shape, dtype))
            psS = [ps("psS%d" % k, [128, TT]) for k in range(4)]
            ps_proj = [psS[0], psS[1]]
            c.alias = {('ps_proj', 0): ('psS', 0), ('ps_proj', 1): ('psS', 1)}
            psO = {br: ps("psO%d" % br, [128, 4, 128]) for br in range(3)}
            ps_misc = ps("ps_misc", [128, TT])
            cosT = sb("cosT", [128, T]); sinS = sb("sinS", [128, T])
            Mc16 = sb("Mc16", [128, T], BF16); Cm16 = sb("Cm16", [128, 4, TT], BF16); Wm16 = sb("Wm16", [128, 4, TT], BF16)
            E16 = sb("E16", [128, 16, 128], BF16); agg16 = sb("agg16", [128, 33], BF16)
            M12 = sb("M12", [128, 2, NS * 32])
            id16 = sb("id16", [128, 128], BF16); z16 = sb("z16", [128, 520], BF16)
            self.P(lambda: nc.gpsimd.memset(z16[:], 0.0), [], ['z16'])
            self.P(lambda: nc.gpsimd.tensor_copy(id16[:], self.ident[:]), ['ident'], ['id16'])
            c.dma('pool', Mc16[:], self.nsa_Mc[:, :], writes=['Mc16'])
            c.dma('pool', Cm16[:], self.nsa_Cm[:, :, :], writes=['Cm16'])
            c.dma('pool', Wm16[:], self.nsa_Wm[:, :, :], writes=['Wm16'])
            c.dma('pool', E16[0:32, :, :], self.nsa_E[:, :, :], writes=['E16'])
            c.dma('pool', agg16[:], self.nsa_agg[:, :], writes=['agg16'])
            c.dma('sp', M12[:], self.nsa_M12[:, :, :], writes=['M12'])
            es_t = ExitStack()
            sbt = lambda name, shape, dtype=F32: es_t.enter_context(nc.sbuf_tensor(name + _u, shape, dtype))
            posi = sbt("posi", [128, T], I32)
            ang = sbt("ang", [128, T]); kf = sbt("kf", [128, T])
            c.dma('sp', posi[:], self.pos_rep[:, :], writes=['posi'])
            self.V(lambda: nc.vector.tensor_copy(ang[:], posi[:]), ['posi'], ['ang'])
            self.V(lambda: nc.vector.tensor_scalar(ang[:], ang[:], self.rope_c[:, 0:1], None, ALU.mult), ['ang', 'rope_c'], ['ang'])
            MAGIC = 12582912.0
            TWO_PI = 2.0 * np.pi
            C1 = float(np.float32(TWO_PI)); C2 = float(TWO_PI - np.float64(np.float32(TWO_PI)))
            for (dst, shift, dk_) in ((sinS, 0.0, 'sinS'), (cosT, 0.5 * np.pi, 'cosT')):
                self.V(lambda: nc.vector.tensor_scalar(kf[:], ang[:], shift, 1.0 / TWO_PI, ALU.add, ALU.mult), ['ang'], ['kf'])
                self.V(lambda: nc.vector.tensor_scalar(kf[:], kf[:], MAGIC, None, ALU.add), ['kf'], ['kf'])
                self.V(lambda: nc.vector.tensor_scalar(kf[:], kf[:], -MAGIC, None, ALU.add), ['kf'], ['kf'])
                self.V(lambda: nc.vector.tensor_scalar(dst[:], ang[:], shift, None, ALU.add), ['ang'], [dk_])
                self.V(lambda: nc.vector.scalar_tensor_tensor(dst[:], kf[:], -C1, dst[:], ALU.mult, ALU.add), ['kf', dk_], [dk_])
                self.V(lambda: nc.vector.scalar_tensor_tensor(dst[:], kf[:], -C2, dst[:], ALU.mult, ALU.add), ['kf', dk_], [dk_])
                self.V(lambda: nc.vector.tensor_scalar(dst[:], dst[:], float(np.pi), -float(np.pi), ALU.min, ALU.max), [dk_], [dk_])
                self.A(lambda: nc.scalar.activation(dst[:], dst[:], AF.Sin), [dk_], [dk_])
            self.V(lambda: nc.vector.tensor_scalar(sinS[:], sinS[:], self.rope_c[:, 1:2], None, ALU.mult), ['sinS', 'rope_c'], ['sinS'])
            c.barrier()
            es_t.close()
            gates = sb("gates", [128, NS, 48])
            wg = sb("wg", [128, NKC, 48], BF16)
            self.load_w(wg[:], W[:, :, 1792:1840], 'wg')
            pm = ps_misc[:, 0:384].rearrange("p (s n) -> p s n", n=48)
            for half in range(2):
                for s8 in range(8):
                    s_ = half * 8 + s8
                    for kc in range(NKC):
                        self.mm(pm[:, s8, :], self.h16[:, kc, s_ * 128:(s_ + 1) * 128], wg[:, kc, :], ['wg', self.kh16(kc, s_ // 4)], ['ps_misc'],
                                start=(kc == 0), stop=(kc == NKC - 1), inc=(kc == NKC - 1))
                self.A(lambda: nc.scalar.activation(gates[:, half * 8:(half + 1) * 8, :], pm, AF.Exp, scale=-1.0), ['ps_misc'], ['gates'])
            self.V(lambda: nc.vector.tensor_scalar(gates[:], gates[:], 1.0, None, ALU.add), ['gates'], ['gates'])
            self.V(lambda: nc.vector.reciprocal(gates[:], gates[:]), ['gates'], ['gates'])
            wa = [sb("nwa%d" % k, [128, NKC, 128], BF16) for k in range(2)]
            w1t = sb("w1t", [128, 16, 256], BF16); w2t = sb("w2t", [128, 2, 128], BF16)
            pos16 = sb("pos16", [128, 16], BF16); bcol = sb("bcol", [128, 1])
            hT16 = sb("hT16", [128, 2, 128], BF16)
            kcmp = sb("kcmp", [128, 128], BF16); vcx = sb("vcx", [128, 65], BF16)
            ksT2 = sb("ksT2", [128, T], BF16); kwT2 = sb("kwT2", [128, T], BF16)
            vsx = sb("vsx", [128, NS, 65], BF16); vwx = sb("vwx", [128, NS, 65], BF16)
            selT = sb("selT", [128, T], BF16)
            imp = sb("imp", [128, NS, 32]); score = sb("score", [128, NS, 32]); top8 = sb("top8", [128, 8]); selb = sb("selb", [128, 32])
            qT16 = sb("qT16", [128, T], BF16); qrT16 = sb("qrT16", [128, T], BF16)
            kc2 = qT16
            c.alias['kc2'] = 'qT16'
            t1 = sb("nt1", [128, TT]); t2 = sb("nt2", [128, TT])
            pT = [sb("pT%d" % k, [128, TT], BF16) for k in range(4)]
            rl4 = [sb("rl4_%d" % k, [128, 4, 1]) for k in range(3)]
            ctmp = [sb("ctmp%d" % k, [128, 4, 64]) for k in range(3)]
            oacc = sb("oacc", [128, 4, 128]); oT = sb("noT", [128, 1, T], BF16); wout = sb("nwout", [128, 1, D], BF16)
            rl = sb("rl", [128, 1])
            self.P(lambda: nc.gpsimd.memset(vcx[:], 1.0), [], ['vcx'])
            self.P(lambda: nc.gpsimd.memset(vsx[:], 1.0), [], ['vsx'])
            self.P(lambda: nc.gpsimd.memset(vwx[:], 1.0), [], ['vwx'])

            def load_dup(dst, key, src, col0):
                c.dma('pool', dst[:, :, 0:64], src[:, :, col0:col0 + 64], writes=[key])
                c.dma('pool', dst[:, :, 64:128], src[:, :, col0:col0 + 64], writes=[key])

            def proj_tile(wt, wkey, tt, k):
                ts = slice(tt * TT, (tt + 1) * TT)
                pp = ps_proj[k]; pk = ('ps_proj', k)
                for kc in range(NKC):
                    self.mm(pp[:], wt[:, kc, :], self.h16[:, kc, ts], [wkey, self.kh16(kc, tt)], [pk], start=(kc == 0), stop=(kc == NKC - 1), inc=(kc == NKC - 1))
                return pp, pk

            def rope_proj(dst, dkey, col0, colsw, dup):
                if dup:
                    load_dup(wa[0], 'wa0', W, col0); load_dup(wa[1], 'wa1', Wsw, colsw)
                else:
                    self.load_w(wa[0][:], W[:, :, col0:col0 + 128], 'wa0'); self.load_w(wa[1][:], Wsw[:, :, colsw:colsw + 128], 'wa1')
                for tt in range(NT):
                    ts = slice(tt * TT, (tt + 1) * TT)
                    p0, k0 = proj_tile(wa[0], 'wa0', tt, 0)
                    p1, k1 = proj_tile(wa[1], 'wa1', tt, 1)
                    self.V(lambda: nc.vector.tensor_tensor(t1[:], p0[:], cosT[:, ts], ALU.mult), [k0, 'cosT'], ['nt1'])
                    self.V(lambda: nc.vector.tensor_tensor(t2[:], p1[:], sinS[:, ts], ALU.mult), [k1, 'sinS'], ['nt2'])
                    self.P(lambda: nc.gpsimd.tensor_tensor(dst[:, ts], t1[:], t2[:], ALU.add), ['nt1', 'nt2'], [dkey])
                    if dst is qrT16:
                        self.A(lambda: nc.scalar.copy(qT16[:, ts], p0[:]), [k0], ['qT16'])

            def cmp_scores(hh, tt, k):
                ts = slice(tt * TT, (tt + 1) * TT)
                hp = slice(hh * 64, (hh + 1) * 64)
                pS = psS[k]; kS = ('psS', k)
                self.mm(pS[0:127, :], kcmp[hp, 0:127], qT16[hp, ts], ['kcmp', 'qT16'], [kS], start=True, stop=False, inc=False)
                self.mm(pS[0:127, :], id16[0:127, 0:127], Mc16[0:127, ts], ['id16', 'Mc16'], [kS], start=False, stop=True, inc=True)
                self.A(lambda: nc.scalar.activation(pT[k][0:127, :], pS[0:127, :], AF.Exp, scale=0.125), [kS], [('pT', k)])

            for g in range(2):
                rope_proj(ksT2, 'ksT2', 1280 + g * 64, 1024 + g * 64, True)
                rope_proj(kwT2, 'kwT2', 1536 + g * 64, 1152 + g * 64, True)
                c.dma('pool', wa[0][:, :, 0:64], W[:, :, 1408 + g * 64:1408 + (g + 1) * 64], writes=['wa0'])
                c.dma('pool', wa[0][:, :, 64:128], W[:, :, 1664 + g * 64:1664 + (g + 1) * 64], writes=['wa0'])
                pv4 = ps_misc[:, 0:512].rearrange("p (s n) -> p s n", n=128)
                for s4 in range(4):
                    for sj in range(4):
                        s_ = s4 * 4 + sj
                        for kc in range(NKC):
                            self.mm(pv4[:, sj, :], self.h16[:, kc, s_ * 128:(s_ + 1) * 128], wa[0][:, kc, :], ['wa0', self.kh16(kc, s_ // 4)], ['ps_misc'],
                                    start=(kc == 0), stop=(kc == NKC - 1), inc=(kc == NKC - 1))
                    self.A(lambda: nc.scalar.copy(vsx[:, s4 * 4:(s4 + 1) * 4, 0:64], pv4[:, :, 0:64]), ['ps_misc'], ['vsx'])
                    self.A(lambda: nc.scalar.copy(vwx[:, s4 * 4:(s4 + 1) * 4, 0:64], pv4[:, :, 64:128]), ['ps_misc'], ['vwx'])
                for kv in range(2):
                    load_dup(wa[0], 'wa0', W, (1024 if kv == 0 else 1152) + g * 64)
                    for tt in range(NT):
                        p0, k0 = proj_tile(wa[0], 'wa0', tt, tt % 2)
                        self.A(lambda: nc.scalar.copy(kc2[0:64, tt * TT:(tt + 1) * TT], p0[0:64, :]), [k0], ['kc2'])
                        if tt == 0:
                            self.V(lambda: nc.vector.tensor_copy(kc2[64:128, 0:TT - 1], p0[64:128, 1:TT]), [k0], ['kc2'])
                        else:
                            self.V(lambda: nc.vector.tensor_copy(kc2[64:128, tt * TT - 1:(tt + 1) * TT - 1], p0[64:128, :]), [k0], ['kc2'])
                    c.dma('pool', w1t[:], self.nsa_cmp_w1[i, kv].rearrange("(m p) j -> p m j", p=128), writes=['w1t'])
                    c.dma('pool', pos16[:], self.nsa_pos2[:, i, kv, :], writes=['pos16'])
                    w2v = self.nsa_cmp_w2[i, kv].rearrange("(jc p) d -> p jc d", p=128)
                    c.dma('pool', w2t[:, :, 0:64], w2v, writes=['w2t'])
                    c.dma('pool', w2t[:, :, 64:128], w2v, writes=['w2t'])
                    for jc in range(2):
                        js = slice(jc * 128, (jc + 1) * 128)
                        for m in range(16):
                            self.mm(ps_misc[:, 0:1], w1t[:, m, js], pos16[:, m:m + 1], ['w1t', 'pos16'], ['ps_misc'], start=(m == 0), stop=(m == 15), inc=(m == 15))
                        self.A(lambda: nc.scalar.copy(bcol[:], ps_misc[:, 0:1]), ['ps_misc'], ['bcol'])
                        for m in range(16):
                            self.mm(ps_misc[:, 0:127], w1t[:, m, js], kc2[:, 2 * m:2 * m + 16 * 126 + 1:16], ['w1t', 'kc2'], ['ps_misc'],
                                    start=(m == 0), stop=(m == 15), inc=(m == 15))
                        self.A(lambda: nc.scalar.activation(hT16[:, jc, 0:127], ps_misc[:, 0:127], AF.Silu, bias=bcol[:, 0:1], scale=1.0), ['ps_misc', 'bcol'], ['hT16'])
                    if kv == 0:
                        for jc in range(2):
                            self.mm(ps_misc[:, 0:127], w2t[:, jc, :], hT16[:, jc, 0:127], ['w2t', 'hT16'], ['ps_misc'], start=(jc == 0), stop=(jc == 1), inc=(jc == 1))
                        self.A(lambda: nc.scalar.copy(kcmp[:, 0:127], ps_misc[:, 0:127]), ['ps_misc'], ['kcmp'])
                    else:
                        for jc in range(2):
                            self.mm(ps_misc[0:127, 0:64], hT16[:, jc, 0:127], w2t[:, jc, 0:64], ['w2t', 'hT16'], ['ps_misc'], start=(jc == 0), stop=(jc == 1), inc=(jc == 1))
                        self.A(lambda: nc.scalar.copy(vcx[0:127, 0:64], ps_misc[0:127, 0:64]), ['ps_misc'], ['vcx'])
                self.P(lambda: nc.gpsimd.memset(imp[:], 0.0), [], ['imp'])

                def p1chain(x):
                    for tt in range(NT):
                        k = 2 * x + (tt % 2)
                        cmp_scores(x, tt, k)
                        yield
                        pI = psO[x]; kI = ('psO', x)
                        for st in range(4):
                            self.mm(pI[:, st, 0:33], pT[k][0:127, st * 128:(st + 1) * 128], agg16[0:127, :], [('pT', k), 'agg16'], [kI])
                        yield
                        self.V(lambda: nc.vector.tensor_scalar(rl4[x][:], pI[:, :, 32:33], 1e-30, None, ALU.max), [kI], [('rl4', x)])
                        self.V(lambda: nc.vector.reciprocal(rl4[x][:], rl4[x][:]), [('rl4', x)], [('rl4', x)])
                        yield
                        self.V(lambda: nc.vector.tensor_tensor(ctmp[x][:, :, 0:32], pI[:, :, 0:32], rl4[x][:].to_broadcast([128, 4, 32]), ALU.mult),
                               [kI, ('rl4', x)], [('ctmp', x)])
                        yield
                        self.V(lambda: nc.vector.tensor_tensor(imp[:, tt * 4:(tt + 1) * 4, :], imp[:, tt * 4:(tt + 1) * 4, :], ctmp[x][:, :, 0:32], ALU.add),
                               ['imp', ('ctmp', x)], ['imp'])
                        yield

                for c4 in range(4):
                    cg = g * 4 + c4
                    self.load_w(wa[0][:], W[:, :, cg * 128:(cg + 1) * 128], 'wa0')
                    for tt in range(NT):
                        p0, k0 = proj_tile(wa[0], 'wa0', tt, tt % 2)
                        self.A(lambda: nc.scalar.copy(qT16[:, tt * TT:(tt + 1) * TT], p0[:]), [k0], ['qT16'])
                    self.run_chains([p1chain(0), p1chain(1)])
                f3 = lambda t_: t_[:].rearrange("p s n -> p (s n)")
                self.V(lambda: nc.vector.tensor_tensor(f3(score), f3(imp), M12[:, 0, :], ALU.mult), ['imp', 'M12'], ['score'])
                self.V(lambda: nc.vector.tensor_tensor(f3(score), f3(score), M12[:, 1, :], ALU.add), ['score', 'M12'], ['score'])
                for s_ in range(NS):
                    self.V(lambda: nc.vector.max(top8[:], score[:, s_, :]), ['score'], ['top8'])
                    self.V(lambda: nc.vector.tensor_scalar(selb[:], score[:, s_, :], top8[:, 7:8], None, ALU.is_ge), ['score', 'top8'], ['selb'])
                    self.V(lambda: nc.vector.tensor_scalar(selb[:], selb[:], -1.0, 30000.0, ALU.add, ALU.mult), ['selb'], ['selb'])
                    self.tr(ps_misc[0:32, 0:128], selb[:], ['selb'], ['ps_misc'])
                    self.A(lambda: nc.scalar.copy(selT[0:32, s_ * 128:(s_ + 1) * 128], ps_misc[0:32, 0:128]), ['ps_misc'], ['selT'])
                def combine(br, e, tt, hp):
                    x = br
                    pO = psO[br]; kO = ('psO', br)
                    self.V(lambda: nc.vector.tensor_scalar(rl4[x][:], pO[:, :, 64:65], 1e-30, None, ALU.max), [kO], [('rl4', x)])
                    self.V(lambda: nc.vector.reciprocal(rl4[x][:], rl4[x][:]), [('rl4', x)], [('rl4', x)])
                    self.V(lambda: nc.vector.tensor_tensor(rl4[x][:], rl4[x][:], gates[:, tt * 4:(tt + 1) * 4, e * 3 + br:e * 3 + br + 1], ALU.mult),
                           [('rl4', x), 'gates'], [('rl4', x)])
                    self.V(lambda: nc.vector.tensor_tensor(ctmp[x][:], pO[:, :, 0:64], rl4[x][:].to_broadcast([128, 4, 64]), ALU.mult),
                           [kO, ('rl4', x)], [('ctmp', x)])
                    self.V(lambda: nc.vector.tensor_tensor(oacc[:, :, hp], oacc[:, :, hp], ctmp[x][:], ALU.add), ['oacc', ('ctmp', x)], ['oacc'])

                def branch(br, e, tt, hh, part, nparts, kbufs):
                    ts = slice(tt * TT, (tt + 1) * TT)
                    hp = slice(hh * 64, (hh + 1) * 64)
                    if br == 1 and part == 0:
                        cmp_scores(hh, tt, kbufs[0])
                        yield
                        for st in range(4):
                            self.mm(psO[0][:, st, 0:65], pT[kbufs[0]][0:127, st * 128:(st + 1) * 128], vcx[0:127, :], [('pT', kbufs[0]), 'vcx'], [('psO', 0)])
                        yield
                        combine(0, e, tt, hp)
                        yield
                    kT_, kTk, vx, vxk = (ksT2, 'ksT2', vsx, 'vsx') if br == 1 else (kwT2, 'kwT2', vwx, 'vwx')
                    pO = psO[br]; kO = ('psO', br)
                    kt0 = 0 if br == 1 else max(0, 4 * tt - 4)
                    kts = list(range(kt0, 4 * tt + 4))[part::nparts]
                    for n_, kt in enumerate(kts):
                        r = kt - 4 * tt
                        k = kbufs[n_ % len(kbufs)]
                        pS = psS[k]; kS = ('psS', k)
                        nmask = (1 if br == 1 else 0) + (1 if (r >= 0 or br == 2) else 0)
                        self.mm(pS[:], kT_[hp, kt * 128:(kt + 1) * 128], qrT16[hp, ts], [kTk, 'qrT16'], [kS], start=True, stop=(nmask == 0), inc=(nmask == 0))
                        if br == 1:
                            nmask -= 1
                            self.mm(pS[:], E16[0:32, kt, :], selT[0:32, ts], ['E16', 'selT'], [kS], start=False, stop=(nmask == 0), inc=(nmask == 0))
                        if r >= 0:
                            self.mm(pS[:], id16[:], Cm16[:, r, :], ['id16', 'Cm16'], [kS], start=False, stop=True, inc=True)
                        elif br == 2:
                            self.mm(pS[:], id16[:], Wm16[:, r + 4, :], ['id16', 'Wm16'], [kS], start=False, stop=True, inc=True)
                        yield
                        self.A(lambda: nc.scalar.activation(pT[k][:], pS[:], AF.Exp, scale=0.125), [kS], [('pT', k)])
                        yield
                        for st in range(4):
                            tmin = tt * TT + st * 128; tmax = tmin + 127
                            if kt * 128 > tmax:
                                continue
                            if br == 2 and kt * 128 + 127 <= tmin - 512:
                                continue
                            self.mm(pO[:, st, 0:65], pT[k][:, st * 128:(st + 1) * 128], vx[:, kt, :], [('pT', k), vxk], [kO],
                                    start=False, stop=False, inc=True)
                        yield

                def head_tile(e, tt, hh):
                    hp = slice(hh * 64, (hh + 1) * 64)
                    for br in (1, 2):
                        self.mm(psO[br][:].rearrange("p a b -> p (a b)"), z16[:, 0:128], z16[:, 0:512], ['z16'], [('psO', br)], start=True, stop=False)
                    self.run_chains([branch(1, e, tt, hh, 0, 2, [0]), branch(1, e, tt, hh, 1, 2, [1]),
                                     branch(2, e, tt, hh, 0, 2, [2]), branch(2, e, tt, hh, 1, 2, [3])])
                    for br in (1, 2):
                        self.mm(psO[br][:].rearrange("p a b -> p (a b)"), z16[:, 0:128], z16[:, 0:512], ['z16'], [('psO', br)], start=False, stop=True)
                        combine(br, e, tt, hp)

                for c4 in range(4):
                    cg = g * 4 + c4
                    rope_proj(qrT16, 'qrT16', cg * 128, cg * 128, False)
                    for tt in range(NT):
                        self.P(lambda: nc.gpsimd.memset(oacc[:], 0.0), [], ['oacc'])
                        for hh in range(2):
                            e = cg * 2 + hh
                            head_tile(e, tt, hh)
                        for st in range(4):
                            s_ = tt * 4 + st
                            self.tr(ps_misc[:, 0:128], oacc[:, st, :], ['oacc'], ['ps_misc'])
                            self.A(lambda: nc.scalar.copy(oT[:, 0, s_ * 128:(s_ + 1) * 128], ps_misc[:, 0:128]), ['ps_misc'], [('oT', tt)])
                    self.out_proj_acc(self.nsa_w_out[i, cg * 128:(cg + 1) * 128, :], oT, [('oT', k_) for k_ in range(4)], 1, ps_proj, cres, wout, 'nwout')
            c.barrier()
            c.alias = {}

    def run_chains(self, gens):
        gens = list(gens)
        while gens:
            for g_ in list(gens):
                try:
                    next(g_)
                except StopIteration:
                    gens.remove(g_)

    def mix_ln(self, layer):
        nc = self.nc
        with ExitStack() as es:
            self._uid = getattr(self, '_uid', 0) + 1
            _u = "_%d" % self._uid
            sb = lambda name, shape, dtype=F32: es.enter_context(nc.sbuf_tensor(name + _u, shape, dtype))
            ps = lambda name, shape, dtype=F32: es.enter_context(nc.psum_tensor(name + _u, shape, dtype))
            scr = {
                'zsq': [sb("ln_zsq%d" % i, [128, TT], BF16) for i in range(2)],
                'zc': [sb("ln_zc%d" % i, [128, TT]) for i in range(2)],
                'mean': sb("ln_mean", [128, TT]), 'rstd': sb("ln_rstd", [128, TT]),
            }
            scr['tmp'] = scr['zc'][0]
            ps_sum = ps("ps_sum", [128, TT])
            ps_sq = ps("ps_sq", [128, TT])
            for tt in range(T // TT):
                self.layer_norm_tile(layer * 3 + 1, tt, ps_sum, ps_sq, scr)
            self.c.barrier()

    def build(self):
        self.prologue()
        for (kind, layer, which) in self.stages:
            if kind == 'ffn':
                self.ffn(layer, which, layer * 3 + (0 if which == 0 else 2))
            elif kind == 'mix':
                if layer % 2 == 0:
                    self.hybrid(layer)
                else:
                    self.nsa(layer)
                self.mix_ln(layer)
        self.epilogue()
        self.es.close()
        return self.nc


def host_inputs(inputs, b):
    m = {}
    m["xT"] = np.ascontiguousarray(inputs["x"][b].T)
    m["ffn_w_in"] = inputs["ffn_w_in"]
    m["ffn_w_out"] = inputs["ffn_w_out"]
    g = inputs["ln_g"].reshape(DEPTH * 3, NKC, 128)
    m["ln_gT"] = np.ascontiguousarray(g.transpose(2, 0, 1).reshape(128, DEPTH * 3 * NKC))
    bb = inputs["ln_b"].reshape(DEPTH * 3, NKC, 128)
    m["ln_bT"] = np.ascontiguousarray(bb.transpose(2, 0, 1).reshape(128, DEPTH * 3 * NKC))
    m["hyb_w_in"] = inputs["hyb_w_in"]
    m["hyb_w_out"] = inputs["hyb_w_out"]
    m["consts"] = make_consts()
    rep = lambda a: np.broadcast_to(a[None], (128,) + a.shape)
    m["gdn_con"] = np.ascontiguousarray(rep(np.concatenate([inputs["gdn_a_log"], inputs["gdn_dt_bias"]], axis=1))).astype(np.float32)
    cw = inputs["gdn_conv_w"].reshape(2, 4, 16, 128)
    m["gdn_cw"] = np.ascontiguousarray(cw.transpose(3, 0, 2, 1))
    m["gdn_nw"] = np.ascontiguousarray(rep(inputs["gdn_norm_w"])).astype(np.float32)
    m["ssd_con"] = np.ascontiguousarray(rep(np.concatenate([inputs["ssd_a_log"], inputs["ssd_dt_bias"], inputs["ssd_d"]], axis=1))).astype(np.float32)
    scw = np.concatenate([inputs["ssd_conv_w"], inputs["ssd_conv_b"][:, None, :]], axis=1).reshape(2, 5, 12, 128)
    m["ssd_cw"] = np.ascontiguousarray(scw.transpose(3, 0, 2, 1))
    m["ssd_nw"] = np.ascontiguousarray(inputs["ssd_norm_w"].reshape(2, 8, 128).transpose(2, 0, 1))
    m["nsa_w_in"] = inputs["nsa_w_in"]
    m["nsa_w_out"] = inputs["nsa_w_out"]
    m["nsa_cmp_w1"] = inputs["nsa_cmp_w1"]
    m["nsa_cmp_w2"] = inputs["nsa_cmp_w2"]
    perm = np.arange(64)
    perm[0:8] = np.arange(8, 16)
    perm[8:16] = np.arange(0, 8)
    wi = inputs["nsa_w_in"]
    qsw = wi[:, :, 0:1024].reshape(2, D, 16, 64)[:, :, :, perm].reshape(2, D, 1024)
    kssw = wi[:, :, 1280:1408].reshape(2, D, 2, 64)[:, :, :, perm].reshape(2, D, 128)
    kwsw = wi[:, :, 1536:1664].reshape(2, D, 2, 64)[:, :, :, perm].reshape(2, D, 128)
    m["nsa_w_sw"] = np.ascontiguousarray(np.concatenate([qsw, kssw, kwsw], axis=2))
    cp = inputs["nsa_cmp_pos"].reshape(2, 2, 16, 2, 64)
    m["nsa_pos2"] = np.ascontiguousarray(cp.transpose(3, 4, 0, 1, 2).reshape(128, 2, 2, 16))
    m["pos_rep"] = np.ascontiguousarray(np.broadcast_to(inputs["positions"][b][None, :], (128, T))).astype(np.int32)
    m.update(nsa_consts())
    return m


_NSA_CONSTS = None


def nsa_consts():
    global _NSA_CONSTS
    if _NSA_CONSTS is not None:
        return _NSA_CONSTS
    NEG = -30000.0
    p = np.arange(128)
    d = p % 64
    half = 8
    invf = np.where(d < 16, 500000.0 ** (-(d % half).astype(np.float64) / half), 0.0).astype(np.float32)
    sgn = np.where(d < 8, -1.0, np.where(d < 16, 1.0, 0.0)).astype(np.float32)
    out = {"rope_c": np.ascontiguousarray(np.stack([invf, sgn], axis=1))}
    t = np.arange(T)
    cmp_end = 16 * np.arange(128) + 31
    out["nsa_Mc"] = np.where(cmp_end[:, None] <= t[None, :], 0.0, NEG).astype(np.float32)
    tl = np.arange(TT)
    cm = np.zeros((128, 4, TT), np.float32)
    wm = np.zeros((128, 4, TT), np.float32)
    for r in range(4):
        cm[:, r, :] = np.where((p[:, None] + 128 * r) <= tl[None, :], 0.0, NEG)
        rr = r - 4
        wm[:, r, :] = np.where((p[:, None] + 128 * rr + 512) > tl[None, :], 0.0, NEG)
    out["nsa_Cm"] = cm
    out["nsa_Wm"] = wm
    E = np.zeros((32, 16, 128), np.float32)
    for kt in range(16):
        for pp in range(128):
            E[2 * kt + pp // 64, kt, pp] = 1.0
    out["nsa_E"] = E
    n_cmp = 127
    c0 = np.arange(n_cmp)[:, None] * 16
    s0 = np.arange(32)[None, :] * 64
    agg = np.clip(np.minimum(c0 + 32, s0 + 64) - np.maximum(c0, s0), 0, None) / 32.0
    aggx = np.zeros((128, 33), np.float32)
    aggx[:n_cmp, :32] = agg
    aggx[:n_cmp, 32] = 1.0
    out["nsa_agg"] = aggx
    tok = (np.arange(16)[None, :, None] * 128 + p[:, None, None])
    j = np.arange(32)[None, None, :]
    cur = tok // 64
    causal = j <= cur
    forced = (j == 0) | (causal & (j > cur - 2))
    M1 = (causal & ~forced).astype(np.float32)
    M2 = np.where(forced, 1e4, np.where(causal, 0.0, -1.0)).astype(np.float32)
    out["nsa_M12"] = np.ascontiguousarray(np.stack([M1.reshape(128, 512), M2.reshape(128, 512)], axis=1))
    _NSA_CONSTS = out
    return out


def make_consts():
    i = np.arange(128)[:, None]
    j = np.arange(128)[None, :]
    same = (i // 64) == (j // 64)
    NEG = -30000.0
    cs = np.zeros((8, 128, 128), np.float32)
    cs[0] = (i == j)
    cs[1] = np.where(same & (j < i), 0.0, NEG)
    cs[2] = np.where(same & (j >= i), 0.0, NEG)
    cs[3] = (i != j)
    cs[4] = (same & (i <= j))
    cs[5] = same
    cs[6] = (i <= j)
    cs[7] = np.where(j >= i, 0.0, NEG)
    return np.ascontiguousarray(cs.transpose(1, 0, 2))


FULL_STAGES = []
for _l in range(DEPTH):
    FULL_STAGES.append(('ffn', _l, 0))
    FULL_STAGES.append(('mix', _l, 0))
    FULL_STAGES.append(('ffn', _l, 1))


def kernel(**inputs):
    inputs = {k: np.asarray(v) for k, v in inputs.items()}
    prog = Prog(FULL_STAGES)
    nc = prog.build()
    B = inputs["x"].shape[0]
    in_maps = [host_inputs(inputs, b) for b in range(B)]
    res = run_bass_kernel_spmd(nc, in_maps, core_ids=list(range(B)))
    out = np.stack([np.ascontiguousarray(r["outT"].T) for r in res.results], axis=0)
    return out.astype(np.float32)
```
